# Optimizing a Trainium2 kernel written in Bass

```python
import jax, jax.numpy as jnp
from jax import lax
import numpy as np

D_MODEL = 2048
BATCH = 2
SEQ = 8192
DEPTH = 1
DEC_BATCH = 32
DEC_SEQ = 1
PAST_LEN = 16384
PAGE_SIZE = 128

N_MEM = 256
EPS = 1e-6
ROPE_THETA = 10000.0
NEG = -1e30

GLA_HEADS = 4
GLA_DK = D_MODEL // 16
GLA_DV = D_MODEL // 8
GLA_RANK = 16
GLA_TAU = 16.0
GLA_CHUNK = 64
GLA_KW = GLA_HEADS * GLA_DK
GLA_VW = GLA_HEADS * GLA_DV

SWA_PATTERNS = ((128, 1), (512, 4), (2048, 16))
SWA_GROUPS = 3
SWA_HEADS = 4
SWA_HD = D_MODEL // 16
SWA_W = SWA_HEADS * SWA_HD
SWA_BLK = 128

MEM_HEADS = 4
MEM_HD = D_MODEL // 16
MEM_W = MEM_HEADS * MEM_HD

N_BRANCH = 3
IN_SIZES = (GLA_KW, GLA_KW, GLA_VW, GLA_VW, GLA_RANK,
            SWA_GROUPS * SWA_W, SWA_GROUPS * SWA_W, SWA_GROUPS * SWA_W, SWA_W,
            MEM_W, MEM_W, N_BRANCH * D_MODEL)
IN_TOTAL = 2 * GLA_KW + 2 * GLA_VW + GLA_RANK + 3 * SWA_GROUPS * SWA_W + SWA_W + 2 * MEM_W + N_BRANCH * D_MODEL

kernel_name = "gated_branch_gla_dilated_swa_memory_decoder_step"


def _rmsnorm(x, g):
    xf = x.astype(jnp.float32)
    y = xf * lax.rsqrt(jnp.mean(xf * xf, axis=-1, keepdims=True) + EPS)
    return (y * g.astype(jnp.float32)).astype(x.dtype)


def _heads(t, n_heads, hd):
    return t.reshape(t.shape[0], t.shape[1], n_heads, hd)


def _rope(x, pos):
    half = x.shape[-1] // 2
    inv = ROPE_THETA ** (-jnp.arange(half, dtype=jnp.float32) / half)
    ang = pos.astype(jnp.float32)[:, None] * inv[None, :]
    cos = jnp.cos(ang)[None, :, None, :]
    sin = jnp.sin(ang)[None, :, None, :]
    xf = x.astype(jnp.float32)
    x1, x2 = xf[..., :half], xf[..., half:]
    return jnp.concatenate([x1 * cos - x2 * sin, x2 * cos + x1 * sin], axis=-1).astype(x.dtype)


def _gla(q, k, v, log_a, s0):
    B, L, H, DK = q.shape
    DV = v.shape[-1]
    C = min(GLA_CHUNK, L)
    Lp = -(-L // C) * C
    n = Lp // C
    padw = ((0, 0), (0, Lp - L), (0, 0), (0, 0))

    def chunks(t):
        t = jnp.pad(t.astype(jnp.float32), padw)
        return t.reshape(B, n, C, H, t.shape[-1]).transpose(1, 0, 3, 2, 4)

    qc = chunks(q) * (DK ** -0.5)
    kc, vc, ac = chunks(k), chunks(v), chunks(log_a)
    causal = jnp.tril(jnp.ones((C, C), dtype=bool))[:, :, None]

    def step(S, inp):
        qi, ki, vi, ai = inp
        b = jnp.cumsum(ai, axis=2)
        diff = b[:, :, :, None, :] - b[:, :, None, :, :]
        decay = jnp.exp(jnp.where(causal, diff, -jnp.inf))
        A = jnp.einsum('bhtk,bhsk,bhtsk->bhts', qi, ki, decay)
        o = jnp.einsum('bhts,bhsv->bhtv', A, vi) + jnp.einsum('bhtk,bhkv->bhtv', qi * jnp.exp(b), S)
        b_end = b[:, :, -1:, :]
        S_new = jnp.exp(b_end[:, :, 0, :])[..., None] * S + jnp.einsum('bhsk,bhsv->bhkv', ki * jnp.exp(b_end - b), vi)
        return S_new, o

    S, o = lax.scan(step, s0.astype(jnp.float32), (qc, kc, vc, ac))
    o = o.transpose(1, 0, 3, 2, 4).reshape(B, Lp, H, DV)[:, :L]
    return o, S


def _dilated_prompt(q, k, v, window, dil):
    B, S, H, E = q.shape
    n_keys = window // dil
    blk = SWA_BLK
    unit = dil * blk
    Sp = -(-S // unit) * unit
    nb = Sp // unit
    padw = ((0, 0), (0, Sp - S), (0, 0), (0, 0))

    def split(t):
        t = jnp.pad(t.astype(jnp.float32), padw)
        return t.reshape(B, nb * blk, dil, H, E).transpose(0, 2, 1, 3, 4).reshape(B, dil, nb, blk, H, E)

    def with_prev(t):
        prev = jnp.concatenate([jnp.zeros_like(t[:, :, :1]), t[:, :, :-1]], axis=2)
        return jnp.concatenate([prev, t], axis=3)

    qb = split(q) * (E ** -0.5)
    kk, vv = with_prev(split(k)), with_prev(split(v))
    s = jnp.einsum('brnqhe,brnkhe->brnhqk', qb, kk)
    qi = jnp.arange(blk)[:, None]
    ki = jnp.arange(2 * blk)[None, :]
    delta = blk + qi - ki
    band = (delta >= 0) & (delta <= n_keys)
    valid = band[None] & ((jnp.arange(nb)[:, None, None] > 0) | (ki[None] >= blk))
    s = jnp.where(valid[None, None, :, None], s, NEG)
    m = jnp.max(s, axis=-1, keepdims=True)
    p = jnp.exp(s - m)
    den = jnp.sum(p, axis=-1)
    o = jnp.einsum('brnhqk,brnkhe->brnqhe', p, vv) / den.transpose(0, 1, 2, 4, 3)[..., None]
    lse = (m[..., 0] + jnp.log(den)).transpose(0, 1, 2, 4, 3)
    o = o.reshape(B, dil, nb * blk, H, E).transpose(0, 2, 1, 3, 4).reshape(B, Sp, H, E)[:, :S]
    lse = lse.reshape(B, dil, nb * blk, H).transpose(0, 2, 1, 3).reshape(B, Sp, H)[:, :S]
    return o, lse


def _dilated_step(q, k_new, v_new, buf, window, dil):
    Lb = buf.shape[1]
    L, E = q.shape[1], q.shape[-1]
    n_keys = window // dil
    kv_all = jnp.concatenate([buf, jnp.stack([k_new, v_new], axis=2)], axis=1)
    idx = Lb + jnp.arange(L)[:, None] - dil * jnp.arange(n_keys + 1)[None, :]
    valid = idx >= 0
    g = kv_all[:, jnp.maximum(idx, 0)].astype(jnp.float32)
    s = jnp.einsum('blhe,blmhe->blhm', q.astype(jnp.float32) * (E ** -0.5), g[:, :, :, 0])
    s = jnp.where(valid[None, :, None, :], s, NEG)
    m = jnp.max(s, axis=-1, keepdims=True)
    p = jnp.exp(s - m)
    den = jnp.sum(p, axis=-1)
    o = jnp.einsum('blhm,blmhe->blhe', p, g[:, :, :, 1]) / den[..., None]
    lse = m[..., 0] + jnp.log(den)
    return o, lse, kv_all[:, L:]


def _mem_kv(mem, g_mem, w_mem_kv):
    kv = _rmsnorm(mem, g_mem) @ w_mem_kv
    return kv.reshape(mem.shape[0], mem.shape[1], 2, MEM_HEADS, MEM_HD)


def _layer(x, mem_kv, gla_s0, swa_bufs, pos0, g_norm, w_in, w_alpha2, b_alpha, g_gla_out,
           w_proj_a, w_proj_b, w_proj_c, w_out):
    nbat, L, _ = x.shape
    h = _rmsnorm(x, g_norm)
    z = h @ w_in
    cuts = [int(c) for c in np.cumsum(IN_SIZES)[:-1]]
    gq, gk, gv, gr, ga, sq, sk, sv, sr, mq, mr, gts = jnp.split(z, cuts, axis=-1)

    log_a = jax.nn.log_sigmoid((ga @ w_alpha2 + b_alpha).astype(jnp.float32)) / GLA_TAU
    o_a, gla_s = _gla(_heads(gq, GLA_HEADS, GLA_DK), _heads(gk, GLA_HEADS, GLA_DK),
                      _heads(gv, GLA_HEADS, GLA_DV), _heads(log_a, GLA_HEADS, GLA_DK), gla_s0)
    o_a = _rmsnorm(o_a, g_gla_out.reshape(GLA_HEADS, GLA_DV)).reshape(nbat, L, GLA_VW).astype(x.dtype)
    y_a = (o_a * jax.nn.silu(gr)) @ w_proj_a

    pos = pos0 + jnp.arange(L, dtype=jnp.int32)
    n_sh = SWA_GROUPS * SWA_HEADS
    q_b = _rope(_heads(sq, n_sh, SWA_HD), pos)
    k_b = _rope(_heads(sk, n_sh, SWA_HD), pos)
    v_b = _heads(sv, n_sh, SWA_HD)
    outs, lses, new_bufs = [], [], []
    for gi, (win, dil) in enumerate(SWA_PATTERNS):
        hsl = slice(gi * SWA_HEADS, (gi + 1) * SWA_HEADS)
        qg, kg, vg = q_b[:, :, hsl], k_b[:, :, hsl], v_b[:, :, hsl]
        if swa_bufs is None:
            o, lse = _dilated_prompt(qg, kg, vg, win, dil)
            buf = jnp.stack([kg, vg], axis=2)[:, L - min(win, L):]
        else:
            o, lse, buf = _dilated_step(qg, kg, vg, swa_bufs[gi], win, dil)
        outs.append(o)
        lses.append(lse)
        new_bufs.append(buf)
    w_grp = jax.nn.softmax(jnp.stack(lses, axis=0), axis=0)[..., None]
    o_b = jnp.sum(w_grp * jnp.stack(outs, axis=0), axis=0).reshape(nbat, L, SWA_W).astype(x.dtype)
    y_b = (o_b * jax.nn.silu(sr)) @ w_proj_b

    qm = _heads(mq, MEM_HEADS, MEM_HD).astype(jnp.float32) * (MEM_HD ** -0.5)
    sc = jnp.einsum('blhe,bmhe->bhlm', qm, mem_kv[:, :, 0].astype(jnp.float32))
    pm = jax.nn.softmax(sc, axis=-1)
    o_c = jnp.einsum('bhlm,bmhe->blhe', pm, mem_kv[:, :, 1].astype(jnp.float32)).reshape(nbat, L, MEM_W).astype(x.dtype)
    y_c = (o_c * jax.nn.silu(mr)) @ w_proj_c

    g_a, g_b, g_c = jnp.split(jax.nn.sigmoid(gts), N_BRANCH, axis=-1)
    x = x + (g_a * y_a + g_b * y_b + g_c * y_c) @ w_out
    return x, gla_s, new_bufs


def setup_inputs(seed: int = 0) -> dict:
    key = jax.random.key(seed)
    ks = jax.random.split(key, 24)

    def nrm(k, shape, scale):
        return jax.random.normal(k, shape, jnp.float32) * scale

    swa_len = [min(w, PAST_LEN) for w, _ in SWA_PATTERNS]
    return {
        "x_prompt": nrm(ks[0], (BATCH, SEQ, D_MODEL), 1.0),
        "x_sample": nrm(ks[1], (DEC_BATCH, DEC_SEQ, D_MODEL), 1.0),
        "mem_prompt": nrm(ks[2], (BATCH, N_MEM, D_MODEL), 1.0),
        "state_gla": nrm(ks[3], (DEPTH, DEC_BATCH, GLA_HEADS, GLA_DK, GLA_DV), 0.5),
        "cache_swa_w128": nrm(ks[4], (DEPTH, DEC_BATCH, swa_len[0], 2, SWA_HEADS, SWA_HD), 1.0),
        "cache_swa_w512": nrm(ks[5], (DEPTH, DEC_BATCH, swa_len[1], 2, SWA_HEADS, SWA_HD), 1.0),
        "cache_swa_w2048": nrm(ks[6], (DEPTH, DEC_BATCH, swa_len[2], 2, SWA_HEADS, SWA_HD), 1.0),
        "cache_mem_kv": nrm(ks[7], (DEPTH, DEC_BATCH, N_MEM, 2, MEM_HEADS, MEM_HD), 1.0),
        "g_norm": 1.0 + nrm(ks[8], (DEPTH, D_MODEL), 0.02),
        "w_in": nrm(ks[9], (DEPTH, D_MODEL, IN_TOTAL), D_MODEL ** -0.5),
        "w_alpha2": nrm(ks[10], (DEPTH, GLA_RANK, GLA_KW), GLA_RANK ** -0.5),
        "b_alpha": nrm(ks[11], (DEPTH, GLA_KW), 0.1),
        "g_gla_out": 1.0 + nrm(ks[12], (DEPTH, GLA_VW), 0.02),
        "g_mem": 1.0 + nrm(ks[13], (DEPTH, D_MODEL), 0.02),
        "w_mem_kv": nrm(ks[14], (DEPTH, D_MODEL, 2 * MEM_W), D_MODEL ** -0.5),
        "w_proj_a": nrm(ks[15], (DEPTH, GLA_VW, D_MODEL), GLA_VW ** -0.5),
        "w_proj_b": nrm(ks[16], (DEPTH, SWA_W, D_MODEL), SWA_W ** -0.5),
        "w_proj_c": nrm(ks[17], (DEPTH, MEM_W, D_MODEL), MEM_W ** -0.5),
        "w_out": nrm(ks[18], (DEPTH, D_MODEL, D_MODEL), D_MODEL ** -0.5),
        "g_final": 1.0 + nrm(ks[19], (D_MODEL,), 0.02),
    }


def reference(x_prompt, x_sample, mem_prompt, state_gla, cache_swa_w128, cache_swa_w512, cache_swa_w2048,
              cache_mem_kv, g_norm, w_in, w_alpha2, b_alpha, g_gla_out, g_mem, w_mem_kv, w_proj_a, w_proj_b,
              w_proj_c, w_out, g_final):
    hp, hs = x_prompt, x_sample
    gla_p, gla_s, mem_p = [], [], []
    swa_p = [[], [], []]
    swa_s = [[], [], []]
    for l in range(DEPTH):
        lw = (g_norm[l], w_in[l], w_alpha2[l], b_alpha[l], g_gla_out[l], w_proj_a[l], w_proj_b[l], w_proj_c[l], w_out[l])
        mkv_p = _mem_kv(mem_prompt, g_mem[l], w_mem_kv[l])
        s0 = jnp.zeros((hp.shape[0], GLA_HEADS, GLA_DK, GLA_DV), jnp.float32)
        hp, sp, bufs_p = _layer(hp, mkv_p, s0, None, 0, *lw)
        hs, ss, bufs_s = _layer(hs, cache_mem_kv[l], state_gla[l],
                                (cache_swa_w128[l], cache_swa_w512[l], cache_swa_w2048[l]), PAST_LEN, *lw)
        gla_p.append(sp)
        gla_s.append(ss)
        mem_p.append(mkv_p)
        for gi in range(SWA_GROUPS):
            swa_p[gi].append(bufs_p[gi])
            swa_s[gi].append(bufs_s[gi])
    y_prompt = _rmsnorm(hp, g_final)
    y_sample = _rmsnorm(hs, g_final)
    gla_prompt = jnp.stack(gla_p)
    swa_w128_prompt = jnp.stack(swa_p[0])
    swa_w512_prompt = jnp.stack(swa_p[1])
    swa_w2048_prompt = jnp.stack(swa_p[2])
    mem_kv_prompt = jnp.stack(mem_p)
    gla_sample = jnp.stack(gla_s)
    swa_w128_sample = jnp.stack(swa_s[0])
    swa_w512_sample = jnp.stack(swa_s[1])
    swa_w2048_sample = jnp.stack(swa_s[2])
    return (y_prompt, y_sample, gla_prompt, swa_w128_prompt, swa_w512_prompt, swa_w2048_prompt, mem_kv_prompt,
            gla_sample, swa_w128_sample, swa_w512_sample, swa_w2048_sample)
```

```python
import numpy as np
from contextlib import ExitStack
import concourse.bass as bass
import concourse.mybir as mybir
from concourse.bass_utils import run_bass_kernel_spmd

F32 = mybir.dt.float32
BF16 = mybir.dt.bfloat16
I32 = mybir.dt.int32
AF = mybir.ActivationFunctionType
ALU = mybir.AluOpType
AX = mybir.AxisListType

D = 2048
NCH = 16
T = 2048
NT = 16
INW = 15376
C_GQ, C_GK, C_GV, C_GR, C_GA = 0, 512, 1024, 2048, 3072
C_SQ, C_SK, C_SV, C_SR, C_MQ, C_MR, C_GT = 3088, 4624, 6160, 7696, 8208, 8720, 9232
EPS = 1e-6
NEGM = -30000.0
SC = 128.0 ** -0.5
TWO_PI = 6.283185307179586
SWA = ((128, 1), (512, 4), (2048, 16))


class Reg:
    __slots__ = ("name", "last_w", "readers", "dsem", "dcount", "ssem", "scount", "excl")

    def __init__(self, name):
        self.name = name
        self.last_w = None
        self.readers = {}
        self.dsem = None
        self.dcount = 0
        self.ssem = None
        self.scount = 0
        self.excl = False


class EngS:
    def __init__(self, name, eng, sem):
        self.name = name
        self.eng = eng
        self.sem = sem
        self.count = 0
        self.waited = {}


class B:
    def __init__(self, nc, es):
        self.nc = nc
        self.es = es
        self.E = {}
        for n in ["tensor", "vector", "scalar", "gpsimd", "sync"]:
            sem = es.enter_context(nc.semaphore("s_" + n))
            self.E[n] = EngS(n, getattr(nc, n), sem)
        self.regs = []
        self.nreg = 0
        self.banks = []
        self.bank_i = 0
        self.sem_pool = []
        self.sw_pool = []
        self.scope_regs = {}

    def reg(self, name):
        self.nreg += 1
        r = Reg("%s_%d" % (name, self.nreg))
        self.regs.append(r)
        return r

    def sb(self, es, name, shape, dt):
        self.nreg += 1
        t = es.enter_context(self.nc.sbuf_tensor("%s_%d" % (name, self.nreg), shape, dt))
        r = self.reg(name)
        self.scope_regs.setdefault(id(es), []).append(r)
        return t, r

    def end_scope(self, es):
        self.barrier()
        for r in self.scope_regs.pop(id(es), []):
            if r.dsem is not None:
                self.sem_pool.append((r.dsem, r.dcount))
                r.dsem = None
            if r.ssem is not None:
                self.sw_pool.append((r.ssem, r.scount))
                r.ssem = None
            self.regs.remove(r)

    def _need(self, e, toks):
        best = {}
        for t in toks:
            if t is None:
                continue
            sem, val = t
            k = id(sem)
            if k not in best or val > best[k][1]:
                best[k] = (sem, val)
        for k, (sem, val) in best.items():
            if val > e.waited.get(k, 0):
                e.eng.wait_ge(sem, val)
                e.waited[k] = val

    def _collect(self, reads, writes):
        toks = []
        for r in reads:
            toks.append(r.last_w)
            if r.excl:
                toks.extend(r.readers.values())
        for w in writes:
            toks.append(w.last_w)
            toks.extend(w.readers.values())
        return toks

    def _update(self, tok, reads, writes):
        k = id(tok[0])
        for r in reads:
            o = r.readers.get(k)
            if o is None or o[1] < tok[1]:
                r.readers[k] = tok
        for w in writes:
            w.last_w = tok
            w.readers = {}

    def op(self, en, fn, reads=(), writes=(), signal=True):
        e = self.E[en]
        toks = self._collect(reads, writes)
        if en == "tensor":
            toks = [t for t in toks if t is not None and t[0] is not e.sem]
        self._need(e, toks)
        inst = fn(e.eng)
        if signal:
            inst.then_inc(e.sem, 1)
            e.count += 1
            tok = (e.sem, e.count)
        else:
            tok = (e.sem, e.count + 1)
        self._update(tok, reads, writes)

    def dma(self, qn, out, in_, sbr, reads=(), writes=()):
        q = self.E[qn]
        sw = qn == "gpsimd"
        if sw and sbr.ssem is None:
            if self.sw_pool:
                sbr.ssem, sbr.scount = self.sw_pool.pop()
            else:
                sbr.ssem = self.es.enter_context(self.nc.semaphore("s_" + sbr.name))
        if not sw and sbr.dsem is None:
            if self.sem_pool:
                sbr.dsem, sbr.dcount = self.sem_pool.pop()
            else:
                sbr.dsem = self.es.enter_context(self.nc.semaphore("d_" + sbr.name))
        toks = self._collect(reads, writes)
        self._need(q, toks)
        if sw:
            q.eng.dma_start(out=out, in_=in_).then_inc(sbr.ssem, 16)
            sbr.scount += 16
            tok = (sbr.ssem, sbr.scount)
        else:
            q.eng.dma_start(out=out, in_=in_).then_inc(sbr.dsem, 16)
            sbr.dcount += 16
            tok = (sbr.dsem, sbr.dcount)
        self._update(tok, reads, writes)

    def barrier(self):
        for e in self.E.values():
            toks = [(f.sem, f.count) for f in self.E.values() if f is not e and f.count > 0]
            toks += [(r.dsem, r.dcount) for r in self.regs if r.dsem is not None and r.dcount > 0]
            toks += [(r.ssem, r.scount) for r in self.regs if r.ssem is not None and r.scount > 0]
            self._need(e, toks)

    def bank(self):
        t, r = self.banks[self.bank_i % len(self.banks)]
        self.bank_i += 1
        return t, r

    def mm(self, out, lhsT, rhs, start, stop, reads, writes):
        self.op("tensor", lambda e: e.matmul(out, lhsT=lhsT, rhs=rhs, start=start, stop=stop),
                reads, writes, signal=stop)

    def tr(self, out, in_, ident, reads, writes):
        self.op("tensor", lambda e: e.transpose(out, in_, ident), reads, writes)

    def act(self, out, in_, func, reads, writes, scale=1.0, bias=0.0, accum_out=None, en="scalar"):
        kw = {}
        if accum_out is not None:
            kw["accum_out"] = accum_out
        self.op("scalar", lambda e: e.activation(out=out, in_=in_, func=func, bias=bias, scale=scale, **kw),
                reads, writes)

    def tt(self, out, in0, in1, op, reads, writes, en="vector"):
        self.op(en, lambda e: e.tensor_tensor(out=out, in0=in0, in1=in1, op=op), reads, writes)

    def ts(self, out, in0, s1, op0, reads, writes, s2=None, op1=None, en="vector"):
        if op1 is None:
            self.op(en, lambda e: e.tensor_scalar(out=out, in0=in0, scalar1=s1, scalar2=None, op0=op0), reads, writes)
        else:
            self.op(en, lambda e: e.tensor_scalar(out=out, in0=in0, scalar1=s1, scalar2=s2, op0=op0, op1=op1),
                    reads, writes)

    def stt(self, out, in0, scalar, in1, op0, op1, reads, writes, en="vector"):
        self.op(en, lambda e: e.scalar_tensor_tensor(out=out, in0=in0, scalar=scalar, in1=in1, op0=op0, op1=op1),
                reads, writes)

    def cp(self, out, in_, reads, writes, en="vector"):
        self.op(en, lambda e: e.tensor_copy(out=out, in_=in_), reads, writes)

    def recip(self, out, in_, reads, writes):
        self.op("vector", lambda e: e.reciprocal(out=out, in_=in_), reads, writes)

    def memset(self, ap, val, writes, en="vector"):
        self.op(en, lambda e: e.memset(ap, val), (), writes)


def build_program():
    nc = bass.Bass("TRN2", target_bir_lowering=False)

    def din(name, shape):
        return nc.dram_tensor(name, list(shape), F32, kind="ExternalInput").ap()

    def dout(name, shape):
        return nc.dram_tensor(name, list(shape), F32, kind="ExternalOutput").ap()

    def dscr(name, shape, dt=F32):
        return nc.dram_tensor(name, list(shape), dt).ap()

    xs = din("xs", [4 * T, D])
    xsmp = din("xsmp", [4, D])
    mem = din("mem", [256, D])
    w_in = din("w_in", [D, INW])
    wa2a = din("wa2a", [17, 512])
    gnt = din("gnt", [128, NCH])
    gmt = din("gmt", [128, NCH])
    ggt = din("ggt", [128, 8])
    gfin = din("gfin", [1, D])
    wmkv = din("wmkv", [D, 1024])
    wpa_d = din("wpa", [1024, D])
    wpb_d = din("wpb", [512, D])
    wpc_d = din("wpc", [512, D])
    wo_d = din("wo", [D, D])
    sgla = din("sgla", [4, 4, 128, 256])
    cch = [din("c128", [4, 128, 1024]), din("c512", [4, 512, 1024]), din("c2048", [4, 2048, 1024])]
    cmem = din("cmem", [4, 256, 1024])
    meta = din("meta", [128, 4])

    y_own = dout("y_own", [T, D])
    y_smp = dout("y_smp", [4, D])
    gla_p = dout("gla_p", [4, 128, 256])
    swa_p = [dout("swa128_p", [128, 1024]), dout("swa512_p", [512, 1024]), dout("swa2048_p", [2048, 1024])]
    memkv_p = dout("memkv_p", [256, 1024])
    gla_s = dout("gla_s", [4, 4, 128, 256])
    swa_s = [dout("s128_s", [4, 128, 1024]), dout("s512_s", [4, 512, 1024]), dout("s2048_s", [4, 2048, 1024])]

    Zg = dscr("Zg", [4 * T, 1536])
    Zs = dscr("Zs", [2 * T, 1536])
    ZT = dscr("ZT", [INW, T])
    ZTga = dscr("ZTga", [4, 16, T])
    ZTskh = dscr("ZTskh", [1536, T])
    ZStm = dscr("ZStm", [4, INW])
    ZSfm = dscr("ZSfm", [INW, 4])
    ZSR = dscr("ZSR", [4, 3072])
    ZSP = dscr("ZSP", [4, 12])
    XN = dscr("XN", [T, D])
    XNs = dscr("XNs", [4, D])

    pp = np.arange(128)
    tri_np = (pp[:, None] <= pp[None, :]).astype(np.float32)
    consts_np = np.zeros((128, 7, 128), np.float32)
    consts_np[:, 0] = np.eye(128, dtype=np.float32)
    consts_np[:, 1] = tri_np
    consts_np[:, 2] = 1.0 - tri_np
    consts_np[:, 3] = NEGM * (1.0 - tri_np.T)
    consts_np[:, 4] = NEGM * (1.0 - tri_np)
    consts_np[:, 5] = 1.0
    consts_np[:, 6] = np.roll(np.eye(128, dtype=np.float32), 64, axis=0)
    consts_d = nc.inline_tensor(consts_np, name="consts").ap()
    half = 64
    inv_np = (np.float32(10000.0) ** (-(np.arange(half, dtype=np.float32) / np.float32(half)))).astype(np.float32)
    col_np = np.zeros((128, 2), np.float32)
    col_np[:, 0] = np.concatenate([inv_np, inv_np])
    col_np[:64, 1] = -1.0
    col_np[64:, 1] = 1.0
    col_d = nc.inline_tensor(col_np, name="colc").ap()
    invrow_d = nc.inline_tensor(inv_np.reshape(1, 64).copy(), name="invrow").ap()

    with ExitStack() as es:
        b = B(nc, es)
        for i in range(8):
            t = es.enter_context(nc.psum_tensor("bank%d" % i, [128, 512], F32))
            b.banks.append((t, b.reg("bank")))
            b.banks[-1][1].excl = True

        cF, cF_r = b.sb(es, "cF", [128, 7, 128], F32)
        cB, cB_r = b.sb(es, "cB", [128, 7, 128], BF16)
        colc, colc_r = b.sb(es, "colc", [128, 2], F32)
        metat, meta_r = b.sb(es, "meta", [128, 4], F32)
        gn, gn_r = b.sb(es, "gn", [128, NCH], F32)
        gm, gm_r = b.sb(es, "gm", [128, NCH], F32)
        gg, gg_r = b.sb(es, "gg", [128, 8], F32)
        wa2, wa2_r = b.sb(es, "wa2", [32, 512], BF16)
        b.dma("sync", cF[:], consts_d[:, :, :], cF_r, writes=[cF_r])
        b.dma("gpsimd", cB[:], consts_d[:, :, :], cB_r, writes=[cB_r])
        b.dma("sync", colc[:], col_d[:, :], colc_r, writes=[colc_r])
        b.dma("sync", metat[:], meta[:, :], meta_r, writes=[meta_r])
        b.dma("sync", gn[:], gnt[:, :], gn_r, writes=[gn_r])
        b.dma("sync", gm[:], gmt[:, :], gm_r, writes=[gm_r])
        b.dma("sync", gg[:], ggt[:, :], gg_r, writes=[gg_r])
        b.dma("gpsimd", wa2[0:17, :], wa2a[:, :], wa2_r, writes=[wa2_r])
        identF = cF[:, 0, :]
        triF = cF[:, 1, :]
        onesF = cF[:, 5, :]
        identB = cB[:, 0, :]
        triB = cB[:, 1, :]
        LmB = cB[:, 2, :]
        mskP = cB[:, 3, :]
        mskC = cB[:, 4, :]
        onesB = cB[:, 5, :]
        PswB = cB[:, 6, :]

        KMT, KMT_r = b.sb(es, "KMT", [128, 4, 256], BF16)
        VM, VM_r = b.sb(es, "VM", [128, 2, 512], BF16)
        hsT, hsT_r = b.sb(es, "hsT", [128, NCH, 4], BF16)
        Sst, Sst_r = b.sb(es, "Sst", [128, 4, 256], F32)
        dd_r = b.reg("dram2dram")


        def norm_T(es_l, x_rows, nrows, gt, dst_fn, dst_reg, xq="sync"):
            xt, xt_r = b.sb(es_l, "xt", [128, D], F32)
            sq, sq_r = b.sb(es_l, "sq", [128, D], F32)
            ss, ss_r = b.sb(es_l, "ss", [128, 2], F32)
            return xt, xt_r, sq, sq_r, ss, ss_r

        def rms_rows(xt, xt_r, sq, sq_r, ss, ss_r, n, dim):
            b.act(sq[0:n, 0:dim], xt[0:n, 0:dim], AF.Square, [xt_r], [sq_r, ss_r], accum_out=ss[0:n, 0:1])
            b.act(ss[0:n, 1:2], ss[0:n, 0:1], AF.Ln, [ss_r, epsc_r], [ss_r], scale=1.0 / dim, bias=epsc[0:n, 0:1])
            b.act(ss[0:n, 1:2], ss[0:n, 1:2], AF.Exp, [ss_r], [ss_r], scale=-0.5)

        epsc, epsc_r = b.sb(es, "epsc", [128, 2], F32)
        b.memset(epsc[:, 0:1], EPS, [epsc_r])
        b.memset(epsc[:, 1:2], 1.0, [epsc_r])
        one_col = epsc[:, 1:2]

        def build_hT(es_unused, rows_ap, ntiles, nrows_last, gt, gt_r, hT, hT_r):
          with ExitStack() as es_l:
            xt, xt_r, sq, sq_r, ss, ss_r = norm_T(es_l, None, None, None, None, None)
            xt2, xt2_r = b.sb(es_l, "xt2", [128, D], F32)
            xbuf = [(xt, xt_r), (xt2, xt2_r)]
            sq2, sq2_r = b.sb(es_l, "sq2", [128, D], F32)
            jk, jk_r = b.sb(es_l, "jk", [128, D], F32)
            ss2, ss2_r = b.sb(es_l, "ss2", [128, 2], F32)
            sqb_ = [(sq, sq_r, ss, ss_r), (sq2, sq2_r, ss2, ss2_r)]
            for i in range(ntiles):
                n = 128 if i < ntiles - 1 or nrows_last == 128 else nrows_last
                xa, xa_r = xbuf[i % 2]
                sq, sq_r, ss, ss_r = sqb_[i % 2]
                b.dma("sync", xa[0:n, :], rows_ap[i * 128:i * 128 + n, :], xa_r, writes=[xa_r])
                rms_rows(xa, xa_r, jk, jk_r, ss, ss_r, n, D)
                b.act(sq[0:n, :], xa[0:n, :], AF.Copy, [xa_r, ss_r], [sq_r], scale=ss[0:n, 1:2])
                for q in range(4):
                    pb, pb_r = b.bank()
                    for cc in range(4):
                        c = q * 4 + cc
                        b.tr(pb[:, cc * 128:cc * 128 + n], sq[0:n, c * 128:(c + 1) * 128], identF[0:n, 0:n],
                             [sq_r, cF_r], [pb_r])
                    b.tt(hT[:, q * 4:(q + 1) * 4, i * 128:i * 128 + n],
                         pb[:, :].rearrange("p (c t) -> p c t", c=4)[:, :, 0:n],
                         gt[:, q * 4:(q + 1) * 4].unsqueeze(2).to_broadcast([128, 4, n]),
                         ALU.mult, [pb_r, gt_r], [hT_r])
            b.end_scope(es_l)

        stg = []

        def wload(wb, wb_r, src, rows, w):
            nchk = rows // 128
            b.dma("gpsimd", wb[:, 0:nchk, 0:w], src.rearrange("(c p) w -> p c w", p=128), wb_r, writes=[wb_r])

        with ExitStack() as st:
            hmT, hmT_r = b.sb(st, "hmT", [128, NCH, 256], BF16)
            build_hT(st, mem, 2, 128, gm, gm_r, hmT, hmT_r)
            mk, mk_r = b.sb(st, "mk", [128, 2, 1024], F32)
            for cb in range(2):
                wb, wb_r = b.sb(st, "wbm", [128, NCH, 512], BF16)
                wload(wb, wb_r, wmkv[:, cb * 512:(cb + 1) * 512], D, 512)
                for i in range(2):
                    pb, pb_r = b.bank()
                    for c in range(NCH):
                        b.mm(pb[:, :], hmT[:, c, i * 128:(i + 1) * 128], wb[:, c, :], c == 0, c == NCH - 1,
                             [hmT_r, wb_r], [pb_r])
                    b.cp(mk[:, i, cb * 512:(cb + 1) * 512], pb[:, :], [pb_r], [mk_r])
                if cb == 0:
                    for h in range(4):
                        pb, pb_r = b.bank()
                        for c in range(NCH):
                            b.mm(pb[:, 0:256], wb[:, c, h * 128:(h + 1) * 128], hmT[:, c, :], c == 0, c == NCH - 1,
                                 [hmT_r, wb_r], [pb_r])
                        b.cp(KMT[:, h, :], pb[:, 0:256], [pb_r], [KMT_r])
            b.cp(VM[:, :, :], mk[:, :, 512:1024], [mk_r], [VM_r])
            b.dma("sync", memkv_p.rearrange("(i p) e -> p i e", p=128), mk[:, :, :], mk_r, reads=[mk_r])
            build_hT(st, xsmp, 1, 4, gn, gn_r, hsT, hsT_r)
            b.end_scope(st)

        PI_LO = 3.1415925

        def silu_to(dst, dst_r, x, x_r, tmp=None, tmp_r=None):
            rd = [] if dst_r is x_r else [x_r]
            b.act(dst, x, AF.Silu, rd, [dst_r])

        def sincos(ang, ang_r, ki, ki_r, kf, kf_r, m1, m1_r, rc, rc_r, cos_out, cos_r, sin_out, sin_r):
            b.ts(ki, ang, 1.0 / TWO_PI, ALU.mult, [ang_r], [ki_r])
            b.cp(kf, ki, [ki_r], [kf_r])
            b.stt(ang, kf, -6.28125, ang, ALU.mult, ALU.add, [kf_r, ang_r], [ang_r])
            b.stt(ang, kf, -0.0019353071795864769, ang, ALU.mult, ALU.add, [kf_r, ang_r], [ang_r])
            b.ts(m1, ang, PI_LO, ALU.is_gt, [ang_r], [m1_r])
            b.stt(ang, m1, -TWO_PI, ang, ALU.mult, ALU.add, [m1_r, ang_r], [ang_r])
            b.ts(m1, ang, -PI_LO, ALU.is_lt, [ang_r], [m1_r])
            b.stt(ang, m1, TWO_PI, ang, ALU.mult, ALU.add, [m1_r, ang_r], [ang_r])
            b.ts(rc, ang, np.pi / 2, ALU.add, [ang_r], [rc_r])
            b.ts(m1, rc, PI_LO, ALU.is_gt, [rc_r], [m1_r])
            b.stt(rc, m1, -TWO_PI, rc, ALU.mult, ALU.add, [m1_r, rc_r], [rc_r])
            b.ts(ang, ang, PI_LO, ALU.min, [ang_r], [ang_r], s2=-PI_LO, op1=ALU.max)
            b.ts(rc, rc, PI_LO, ALU.min, [rc_r], [rc_r], s2=-PI_LO, op1=ALU.max)
            b.act(sin_out, ang, AF.Sin, [ang_r], [sin_r])
            b.act(cos_out, rc, AF.Sin, [rc_r], [cos_r])

        stT = ExitStack()
        cosT, cos_r = b.sb(stT, "cosT", [128, 2 * T], F32)
        sinT, sin_r = b.sb(stT, "sinT", [128, 2 * T], F32)
        with ExitStack() as st2:
            ang, ang_r = b.sb(st2, "ang", [128, 1024], F32)
            ki, ki_r = b.sb(st2, "ki", [128, 1024], I32)
            kf, kf_r = b.sb(st2, "kf", [128, 1024], F32)
            m1, m1_r = b.sb(st2, "m1", [128, 1024], F32)
            rc, rc_r = b.sb(st2, "rc", [128, 1024], F32)
            for ch in range(4):
                b.op("gpsimd", lambda e: e.iota(ki[:], pattern=[[1, 1024]], base=ch * 1024, channel_multiplier=0),
                     (), [ki_r])
                b.cp(ang[:], ki[:], [ki_r], [ang_r])
                b.ts(ang[:], ang[:], metat[:, 0:1], ALU.add, [ang_r, meta_r, colc_r], [ang_r], s2=colc[:, 0:1],
                     op1=ALU.mult)
                csl = slice(ch * 1024, (ch + 1) * 1024)
                sincos(ang[:], ang_r, ki[:], ki_r, kf[:], kf_r, m1[:], m1_r, rc[:], rc_r,
                       cosT[:, csl], cos_r, sinT[:, csl], sin_r)
                b.ts(sinT[:, csl], sinT[:, csl], colc[:, 1:2], ALU.mult, [sin_r, colc_r], [sin_r])
            b.end_scope(st2)

        for s in range(4):
            own = s == 3
            halo = s == 2
            with ExitStack() as st:
                hT, hT_r = b.sb(st, "hT", [128, NCH, T], BF16)
                wbs = [b.sb(st, "wb", [128, NCH, 512], BF16) for _ in range(2)]
                stgs = [b.sb(st, "stg", [128, 512], F32) for _ in range(6)]
                cnt = {"w": 0, "s": 0}
                wload(wbs[0][0], wbs[0][1], w_in[:, C_GK:C_GK + 512], D, 512)
                pre_w = [C_GK]
                build_hT(st, xs[s * T:(s + 1) * T, :], NT, 128, gn, gn_r, hT, hT_r)
                def next_w():
                    r = wbs[cnt["w"] % 2]
                    cnt["w"] += 1
                    return r

                def _in(c, lo, hi):
                    return lo <= c < hi

                def need_tm(c):
                    return _in(c, C_GV, C_GR) or _in(c, C_SQ, C_SR) or _in(c, C_MQ, C_MR)

                def need_fm(c):
                    return not (_in(c, C_GV, C_GR) or _in(c, C_SQ, C_SV) or _in(c, C_MQ, C_MR))

                def evac(pb, pb_r, np_, nf, dst):
                    sg, sg_r = stgs[cnt["s"] % 6]
                    cnt["s"] += 1
                    if cnt["s"] % 2 == 0:
                        b.cp(sg[0:np_, 0:nf], pb[0:np_, 0:nf], [pb_r], [sg_r])
                    else:
                        b.act(sg[0:np_, 0:nf], pb[0:np_, 0:nf], AF.Copy, [pb_r], [sg_r])
                    b.dma("sync", dst, sg[0:np_, 0:nf], sg_r, reads=[sg_r])

                xbs = [b.sb(st, "xbr", [128, 512], BF16) for _ in range(3)]
                xbc = [0]
                t2s = [b.sb(st, "t2r", [128, 512], F32) for _ in range(2)]
                pend = []

                def rope_finish(pb, pb_r, xb, xb_r, tcol, dst):
                    pb2, pb2_r = b.bank()
                    b.mm(pb2[:, :], PswB, xb[:, :], True, True, [cB_r, xb_r], [pb2_r])
                    sg, sg_r = stgs[cnt["s"] % 6]
                    t2, t2_r = t2s[cnt["s"] % 2]
                    cnt["s"] += 1
                    b.tt(sg[:, :], pb[:, :], cosT[:, tcol:tcol + 512], ALU.mult, [pb_r, cos_r], [sg_r])
                    b.tt(t2[:, :], pb2[:, :], sinT[:, tcol:tcol + 512], ALU.mult, [pb2_r, sin_r], [t2_r])
                    b.tt(sg[:, :], sg[:, :], t2[:, :], ALU.add, [sg_r, t2_r], [sg_r])
                    b.dma("sync", dst, sg[:, :], sg_r, reads=[sg_r])

                def fm_job(col0, ncols, dst_rows_fn, tb0_fn=lambda c0: 0, rope_t0=None):
                    for c0 in range(col0, col0 + ncols, 512):
                        w = min(512, col0 + ncols - c0)
                        wb, wb_r = next_w()
                        wload(wb, wb_r, w_in[:, c0:c0 + w], D, w)
                        for m0 in range(0, w, 128):
                            mw = min(128, w - m0)
                            for tb in range(tb0_fn(c0), 4):
                                pb, pb_r = b.bank()
                                for c in range(NCH):
                                    b.mm(pb[0:mw, :], wb[:, c, m0:m0 + mw], hT[:, c, tb * 512:(tb + 1) * 512],
                                         c == 0, c == NCH - 1, [wb_r, hT_r], [pb_r])
                                dst_ = dst_rows_fn(c0 + m0, mw)[:, tb * 512:(tb + 1) * 512]
                                if rope_t0 is None:
                                    evac(pb, pb_r, mw, 512, dst_)
                                else:
                                    xb, xb_r = xbs[xbc[0] % 3]
                                    xbc[0] += 1
                                    b.act(xb[:, :], pb[:, :], AF.Copy, [pb_r], [xb_r])
                                    prev = pend[:]
                                    del pend[:]
                                    pend.append((pb, pb_r, xb, xb_r, rope_t0 + tb * 512, dst_))
                                    for pa in prev:
                                        rope_finish(*pa)
                            if rope_t0 is not None and m0 + 128 >= w and c0 + 512 >= col0 + ncols:
                                for pa in pend:
                                    rope_finish(*pa)
                                del pend[:]
                            if own and need_fm(c0):
                                pb, pb_r = b.bank()
                                for c in range(NCH):
                                    b.mm(pb[0:mw, 0:4], wb[:, c, m0:m0 + mw], hsT[:, c, :], c == 0, c == NCH - 1,
                                         [wb_r, hsT_r], [pb_r])
                                evac(pb, pb_r, mw, 4, ZSfm[c0 + m0:c0 + m0 + mw, :])
                        if own and need_tm(c0):
                            pb, pb_r = b.bank()
                            for c in range(NCH):
                                b.mm(pb[0:4, 0:w], hsT[:, c, :], wb[:, c, 0:w], c == 0, c == NCH - 1,
                                     [wb_r, hsT_r], [pb_r])
                            evac(pb, pb_r, 4, w, ZStm[:, c0:c0 + w])

                def tm_job(col0, ncols, dst_fn, tile0_fn=lambda c0: 0):
                    for c0 in range(col0, col0 + ncols, 512):
                        w = 512
                        wb, wb_r = next_w()
                        if pre_w and pre_w[0] == c0:
                            pre_w.pop()
                        else:
                            wload(wb, wb_r, w_in[:, c0:c0 + w], D, w)
                        for i in range(tile0_fn(c0), NT):
                            pb, pb_r = b.bank()
                            for c in range(NCH):
                                b.mm(pb[:, :], hT[:, c, i * 128:(i + 1) * 128], wb[:, c, :], c == 0, c == NCH - 1,
                                     [wb_r, hT_r], [pb_r])
                            evac(pb, pb_r, 128, 512, dst_fn(i, c0))
                        if own and need_fm(c0):
                            for m0 in range(0, w, 128):
                                pb, pb_r = b.bank()
                                for c in range(NCH):
                                    b.mm(pb[:, 0:4], wb[:, c, m0:m0 + 128], hsT[:, c, :], c == 0, c == NCH - 1,
                                         [wb_r, hsT_r], [pb_r])
                                evac(pb, pb_r, 128, 4, ZSfm[c0 + m0:c0 + m0 + 128, :])
                        if own and need_tm(c0):
                            pb, pb_r = b.bank()
                            for c in range(NCH):
                                b.mm(pb[0:4, 0:w], hsT[:, c, :], wb[:, c, 0:w], c == 0, c == NCH - 1,
                                     [wb_r, hsT_r], [pb_r])
                            evac(pb, pb_r, 4, w, ZStm[:, c0:c0 + w])

                tm_job(C_GK, 1536, lambda i, c0: Zg[s * T + i * 128:s * T + (i + 1) * 128, c0 - C_GK:c0 - C_GK + 512])
                fm_job(C_GA, 16, lambda r0, n: ZTga[s, r0 - C_GA:r0 - C_GA + n, :] if not own else ZTga[s, r0 - C_GA:r0 - C_GA + n, :])
                if halo or own:
                    tm_job(C_SV, 1536,
                           lambda i, c0: Zs[(s - 2) * T + i * 128:(s - 2) * T + (i + 1) * 128, c0 - C_SV:c0 - C_SV + 512],
                           tile0_fn=(lambda c0: NT - min(NT, SWA[(c0 - C_SV) // 512][0] // 128)) if halo else (lambda c0: 0))
                if halo:
                    fm_job(C_SK, 1536, lambda r0, n: ZTskh[r0 - C_SK:r0 - C_SK + n, :],
                           tb0_fn=lambda c0: 4 - max(1, SWA[(c0 - C_SK) // 512][0] // 512), rope_t0=0)
                if own:
                    fm_job(C_GQ, 1024, lambda r0, n: ZT[r0:r0 + n, :])
                    fm_job(C_GR, 1024, lambda r0, n: ZT[r0:r0 + n, :])
                    fm_job(C_SQ, 3072, lambda r0, n: ZT[r0:r0 + n, :], rope_t0=T)
                    fm_job(C_SR, INW - C_SR, lambda r0, n: ZT[r0:r0 + n, :])
                b.end_scope(st)

        b.end_scope(stT)
        stT.close()
        actA, actA_r = b.sb(es, "actA", [128, 8, T + 4], BF16)
        actB, actB_r = b.sb(es, "actB", [128, 4, T + 4], BF16)
        actC, actC_r = b.sb(es, "actC", [128, 4, T + 4], BF16)
        with ExitStack() as st:
            gaT, gaT_r = b.sb(st, "gaT", [32, T], BF16)
            kts = [b.sb(st, "kt", [128, NT, 128], F32) for _ in range(2)]
            vbs2 = [b.sb(st, "vb", [128, NT, 256], BF16) for _ in range(2)]
            la, la_r = b.sb(st, "la", [128, NT, 128], BF16)
            kh, kh_r = b.sb(st, "kh", [128, NT, 128], BF16)
            Sb, Sb_r = b.sb(st, "Sb", [128, NT, 256], BF16)
            tE, tE_r = b.sb(st, "tE", [128, 512], F32)
            tE2, tE2_r = b.sb(st, "tE2", [128, 512], F32)
            Td, Td_r = b.sb(st, "Td", [128, NT], F32)
            qT, qT_r = b.sb(st, "qT", [128, T], F32)
            kT, kT_r = b.sb(st, "kT", [128, T], F32)
            qh, qh_r = b.sb(st, "qh", [128, T], BF16)
            kth, kth_r = b.sb(st, "kth", [128, T], BF16)
            AT, AT_r = b.sb(st, "AT", [128, NT, 128], BF16)
            oF, oF_r = b.sb(st, "oF", [128, 2, T], F32)
            grT, grT_r = b.sb(st, "grT", [128, 2, T], F32)
            sqb, sqb_r = b.sb(st, "sqb", [128, 2, 512], BF16)
            rstd, rstd_r = b.sb(st, "rstd", [128, 512], F32)
            tmpn, tmpn_r = b.sb(st, "tmpn", [128, 512], F32)
            b.memset(Sst[:], 0.0, [Sst_r])
            b.memset(gaT[:], 1.0, [gaT_r])
            for g, (win, dil) in enumerate(SWA):
                for bb in range(4):
                    b.dma("sync", swa_s[g][bb, 0:win - 1, :], cch[g][bb, 1:win, :], dd_r)
            for s in range(4):
                own = s == 3
                b.dma("gpsimd", gaT[0:16, :], ZTga[s, :, :], gaT_r, writes=[gaT_r])
                for hh in range(4):
                    kt, kt_r = kts[(s * 4 + hh) % 2]
                    vb, vb_r = vbs2[(s * 4 + hh) % 2]
                    b.dma("scalar", kt[:],
                          Zg[s * T:(s + 1) * T, hh * 128:(hh + 1) * 128].rearrange("(i p) e -> p i e", p=128),
                          kt_r, writes=[kt_r])
                    b.dma("gpsimd", vb[:],
                          Zg[s * T:(s + 1) * T, 512 + hh * 256:512 + (hh + 1) * 256].rearrange("(i p) e -> p i e", p=128),
                          vb_r, writes=[vb_r])
                    for q in range(4):
                        pb, pb_r = b.bank()
                        for cc in range(4):
                            i = q * 4 + cc
                            b.mm(pb[:, cc * 128:(cc + 1) * 128], gaT[0:17, i * 128:(i + 1) * 128],
                                 wa2[0:17, hh * 128:(hh + 1) * 128], True, True, [gaT_r, wa2_r], [pb_r])
                        b.act(tE[:], pb[:], AF.Exp, [pb_r], [tE_r], scale=-1.0)
                        b.act(la[:, q * 4:(q + 1) * 4, :], tE[:].rearrange("p (c t) -> p c t", c=4), AF.Ln,
                              [tE_r, epsc_r], [la_r], bias=one_col)
                    if not own:
                        for q in range(4):
                            pb, pb_r = b.bank()
                            for cc in range(4):
                                i = q * 4 + cc
                                b.mm(pb[:, cc * 128:(cc + 1) * 128], LmB, la[:, i, :], True, i == NT - 1,
                                     [cB_r, la_r], [pb_r])
                                for j2 in range(i + 1, NT):
                                    b.mm(pb[:, cc * 128:(cc + 1) * 128], onesB, la[:, j2, :], False, j2 == NT - 1,
                                         [cB_r, la_r], [pb_r])
                            b.act(tE[:], pb[:], AF.Exp, [pb_r], [tE_r], scale=-1.0 / 16)
                            b.tt(kh[:, q * 4:(q + 1) * 4, :], kt[:, q * 4:(q + 1) * 4, :],
                                 tE[:].rearrange("p (c t) -> p c t", c=4), ALU.mult, [kt_r, tE_r], [kh_r])
                        pb, pb_r = b.bank()
                        for i in range(NT):
                            b.mm(pb[:, 0:1], la[:, i, :], onesB[:, 0:1], i == 0, i == NT - 1, [la_r, cB_r], [pb_r])
                        b.act(Td[:, 0:1], pb[:, 0:1], AF.Exp, [pb_r], [Td_r], scale=-1.0 / 16)
                        pbU, pbU_r = b.bank()
                        for i in range(NT):
                            b.mm(pbU[:, 0:256], kh[:, i, :], vb[:, i, :], i == 0, i == NT - 1, [kh_r, vb_r], [pbU_r])
                        b.stt(Sst[:, hh, :], Sst[:, hh, :], Td[:, 0:1], pbU[:, 0:256], ALU.mult, ALU.add,
                              [Sst_r, Td_r, pbU_r], [Sst_r])
                        continue
                    for q in range(4):
                        pb, pb_r = b.bank()
                        for cc in range(4):
                            i = q * 4 + cc
                            b.mm(pb[:, cc * 128:(cc + 1) * 128], LmB, la[:, i, :], True, True, [cB_r, la_r], [pb_r])
                        b.act(tE[:], pb[:], AF.Exp, [pb_r], [tE_r], scale=-1.0 / 16)
                        b.tt(kh[:, q * 4:(q + 1) * 4, :], kt[:, q * 4:(q + 1) * 4, :],
                             tE[:].rearrange("p (c t) -> p c t", c=4), ALU.mult, [kt_r, tE_r], [kh_r])
                    pb, pb_r = b.bank()
                    for i in range(NT):
                        b.mm(pb[:, i:i + 1], la[:, i, :], onesB[:, 0:1], True, True, [la_r, cB_r], [pb_r])
                    b.act(Td[:], pb[:, 0:NT], AF.Exp, [pb_r], [Td_r], scale=-1.0 / 16)
                    for i in range(NT):
                        if i % 2 == 0:
                            pbU, pbU_r = b.bank()
                        usl = slice((i % 2) * 256, (i % 2 + 1) * 256)
                        b.mm(pbU[:, usl], kh[:, i, :], vb[:, i, :], True, True, [kh_r, vb_r], [pbU_r])
                        if own:
                            b.act(Sb[:, i, :], Sst[:, hh, :], AF.Copy, [Sst_r], [Sb_r])
                        b.stt(Sst[:, hh, :], Sst[:, hh, :], Td[:, i:i + 1], pbU[:, usl], ALU.mult, ALU.add,
                              [Sst_r, Td_r, pbU_r], [Sst_r])
                    if not own:
                        continue
                    b.dma("scalar", qT[:], ZT[C_GQ + hh * 128:C_GQ + (hh + 1) * 128, :], qT_r, writes=[qT_r])
                    b.dma("scalar", kT[:], ZT[C_GK + hh * 128:C_GK + (hh + 1) * 128, :], kT_r, writes=[kT_r])
                    b.dma("scalar", grT[:], ZT[C_GR + hh * 256:C_GR + (hh + 1) * 256, :].rearrange("(e p) t -> p e t", p=128),
                          grT_r, writes=[grT_r])
                    for q in range(4):
                        pb, pb_r = b.bank()
                        for cc in range(4):
                            i = q * 4 + cc
                            b.mm(pb[:, cc * 128:(cc + 1) * 128], la[:, i, :], triB, True, True, [la_r, cB_r], [pb_r])
                        sl = slice(q * 512, (q + 1) * 512)
                        b.act(tE[:], pb[:], AF.Exp, [pb_r], [tE_r], scale=-1.0 / 16)
                        b.stt(qh[:, sl], qT[:, sl], SC, tE[:], ALU.mult, ALU.mult, [qT_r, tE_r], [qh_r])
                        b.act(tE2[:], pb[:], AF.Exp, [pb_r], [tE2_r], scale=1.0 / 16)
                        b.tt(kth[:, sl], kT[:, sl], tE2[:], ALU.mult, [kT_r, tE2_r], [kth_r])
                    for q in range(4):
                        pb, pb_r = b.bank()
                        for cc in range(4):
                            i = q * 4 + cc
                            b.mm(pb[:, cc * 128:(cc + 1) * 128], kth[:, i * 128:(i + 1) * 128],
                                 qh[:, i * 128:(i + 1) * 128], True, True, [kth_r, qh_r], [pb_r])
                        b.tt(AT[:, q * 4:(q + 1) * 4, :], pb[:].rearrange("p (c t) -> p c t", c=4),
                             triF.unsqueeze(1).to_broadcast([128, 4, 128]), ALU.mult, [pb_r, cF_r], [AT_r])
                    for e2 in range(2):
                        for q in range(4):
                            pb, pb_r = b.bank()
                            for cc in range(4):
                                i = q * 4 + cc
                                b.mm(pb[:, cc * 128:(cc + 1) * 128], vb[:, i, e2 * 128:(e2 + 1) * 128], AT[:, i, :],
                                     True, False, [vb_r, AT_r], [pb_r])
                                b.mm(pb[:, cc * 128:(cc + 1) * 128], Sb[:, i, e2 * 128:(e2 + 1) * 128],
                                     qh[:, i * 128:(i + 1) * 128], False, True, [Sb_r, qh_r], [pb_r])
                            b.act(oF[:, e2, q * 512:(q + 1) * 512], pb[:], AF.Copy, [pb_r], [oF_r])
                    silu_to(grT[:], grT_r, grT[:], grT_r)
                    for tb in range(4):
                        sl = slice(tb * 512, (tb + 1) * 512)
                        b.tt(sqb[:], oF[:, :, sl], oF[:, :, sl], ALU.mult, [oF_r], [sqb_r])
                        pb, pb_r = b.bank()
                        b.mm(pb[:], onesB, sqb[:, 0, :], True, False, [cB_r, sqb_r], [pb_r])
                        b.mm(pb[:], onesB, sqb[:, 1, :], False, True, [cB_r, sqb_r], [pb_r])
                        b.act(rstd[:], pb[:], AF.Ln, [pb_r, epsc_r], [rstd_r], scale=1.0 / 256, bias=epsc[:, 0:1])
                        b.act(rstd[:], rstd[:], AF.Exp, [rstd_r], [rstd_r], scale=-0.5)
                        for e2 in range(2):
                            b.stt(tmpn[:], oF[:, e2, sl], gg[:, hh * 2 + e2:hh * 2 + e2 + 1], rstd[:], ALU.mult, ALU.mult,
                                  [oF_r, gg_r, rstd_r], [tmpn_r])
                            b.tt(actA[:, hh * 2 + e2, sl], tmpn[:], grT[:, e2, sl], ALU.mult, [tmpn_r, grT_r], [actA_r])
            b.dma("sync", gla_p.rearrange("h p v -> p h v"), Sst[:], Sst_r, reads=[Sst_r])
            b.end_scope(st)

        with ExitStack() as st:
            krFs = [b.sb(st, "krF", [128, 2 * T], F32) for _ in range(2)]
            krbs = [b.sb(st, "krb", [128, 2 * T], BF16) for _ in range(2)]
            qrbs = [b.sb(st, "qrb", [128, T], BF16) for _ in range(2)]
            vbss = [b.sb(st, "vbs", [128, 32, 128], BF16) for _ in range(2)]
            nd, nd_r = b.sb(st, "nd", [128, 2, T], F32)
            num, num_r, den, den_r = nd[:, 0, :], nd_r, nd[:, 1, :], nd_r
            PTs = [b.sb(st, "PT", [128, 256], BF16) for _ in range(3)]
            kst, kst_r = b.sb(st, "kst", [128, 4, 128], F32)
            srT, srT_r = b.sb(st, "srT", [128, 1024], F32)
            ssil, ssil_r = b.sb(st, "ssil", [128, 1024], F32)
            ptc = [0]
            unit = 0
            for j in range(4):
                b.memset(nd[:], 0.0, [nd_r])
                for g, (win, d) in enumerate(SWA):
                    krF, krF_r = krFs[unit % 2]
                    krb, krb_r = krbs[unit % 2]
                    qrb, qrb_r = qrbs[unit % 2]
                    vbs, vbs_r = vbss[unit % 2]
                    unit += 1
                    H = 128 * d
                    L = H + T
                    NB = 16 // d + 1
                    row = g * 512 + j * 128
                    W = min(win, T)
                    b.dma("sync", krF[:, 0:H], ZTskh[row:row + 128, T - H:T], krF_r, writes=[krF_r])
                    b.dma("scalar", krF[:, H:L], ZT[C_SK + row:C_SK + row + 128, :], krF_r, writes=[krF_r])
                    b.dma("gpsimd", qrb[:, :], ZT[C_SQ + row:C_SQ + row + 128, :], qrb_r, writes=[qrb_r])
                    if d <= NB:
                        vsrc = Zs[T - H:2 * T, row:row + 128].rearrange("(nb p dd) e -> dd p nb e", p=128, dd=d)
                        for r in range(d):
                            b.dma("gpsimd", vbs[:, r * NB:(r + 1) * NB, :], vsrc[r], vbs_r, writes=[vbs_r])
                        vblk = lambda r, nbi, NB=NB, d=d: r * NB + nbi
                    else:
                        for nbi in range(NB):
                            r0_ = T - H + nbi * 128 * d
                            b.dma("gpsimd", vbs[:, nbi * d:(nbi + 1) * d, :],
                                  Zs[r0_:r0_ + 128 * d, row:row + 128].rearrange("(p dd) e -> p dd e", dd=d),
                                  vbs_r, writes=[vbs_r])
                        vblk = lambda r, nbi, NB=NB, d=d: nbi * d + r
                    b.act(krb[:, 0:L], krF[:, 0:L], AF.Copy, [krF_r], [krb_r])
                    nt_out = W // 128
                    for q0 in range(0, nt_out, 4):
                        nq = min(4, nt_out - q0)
                        pb, pb_r = b.bank()
                        for cc in range(nq):
                            col0 = L - W + (q0 + cc) * 128
                            b.tr(pb[:, cc * 128:(cc + 1) * 128], krF[:, col0:col0 + 128], identF, [krF_r, cF_r], [pb_r])
                        b.cp(kst[:, 0:nq, :], pb[:, 0:nq * 128].rearrange("p (c t) -> p c t", c=nq), [pb_r], [kst_r])
                        b.dma("sync",
                              swa_p[g][q0 * 128:(q0 + nq) * 128, j * 128:(j + 1) * 128].rearrange("(c p) e -> p c e", p=128),
                              kst[:, 0:nq, :], kst_r, reads=[kst_r])
                    if j == 0:
                        b.dma("sync", swa_p[g][:, 512:1024], Zs[2 * T - W:2 * T, g * 512:(g + 1) * 512], dd_r)
                    qv = qrb[:, :].rearrange("p (u dd) -> p dd u", dd=d)
                    kv = krb[:, 0:L].rearrange("p (u dd) -> p dd u", dd=d)
                    ndv = nd[:, :, :].rearrange("p c (u dd) -> p c dd u", dd=d)

                    def emit_S(r, n):
                        qs = qv[:, r, n * 128:(n + 1) * 128]
                        PT, PT_r = PTs[ptc[0] % 3]
                        ptc[0] += 1
                        pb, pb_r = b.bank()
                        b.mm(pb[:, 0:128], kv[:, r, n * 128:(n + 1) * 128], qs, True, False, [krb_r, qrb_r], [pb_r])
                        b.mm(pb[:, 0:128], identB, mskP, False, True, [cB_r], [pb_r])
                        b.mm(pb[:, 128:256], kv[:, r, (n + 1) * 128:(n + 2) * 128], qs, True, False,
                             [krb_r, qrb_r], [pb_r])
                        b.mm(pb[:, 128:256], identB, mskC, False, True, [cB_r], [pb_r])
                        if n == 0:
                            b.act(PT[:, 0:128], pb[:, 0:128], AF.Exp, [pb_r, meta_r], [PT_r], scale=SC,
                                  bias=metat[:, 1:2])
                            b.act(PT[:, 128:256], pb[:, 128:256], AF.Exp, [pb_r], [PT_r], scale=SC)
                        else:
                            b.act(PT[:, 0:256], pb[:, 0:256], AF.Exp, [pb_r], [PT_r], scale=SC)
                        return PT, PT_r

                    def emit_PV(r, n, PT, PT_r):
                        pb2, pb2_r = b.bank()
                        b.mm(pb2[:, 0:128], vbs[:, vblk(r, n), :], PT[:, 0:128], True, False, [vbs_r, PT_r], [pb2_r])
                        b.mm(pb2[:, 0:128], vbs[:, vblk(r, n + 1), :], PT[:, 128:256], False, True,
                             [vbs_r, PT_r], [pb2_r])
                        b.mm(pb2[:, 128:256], onesB, PT[:, 0:128], True, False, [cB_r, PT_r], [pb2_r])
                        b.mm(pb2[:, 128:256], onesB, PT[:, 128:256], False, True, [cB_r, PT_r], [pb2_r])
                        ndsl = ndv[:, :, r, n * 128:(n + 1) * 128]
                        b.tt(ndsl, ndsl, pb2[:, 0:256].rearrange("p (c q) -> p c q", c=2), ALU.add,
                             [nd_r, pb2_r], [nd_r])

                    prevS = None
                    for r in range(d):
                        for n in range(16 // d):
                            cur = emit_S(r, n)
                            if prevS is not None:
                                emit_PV(*prevS)
                            prevS = (r, n) + cur
                    emit_PV(*prevS)
                b.act(den, den, AF.Ln, [], [den_r])
                b.act(den, den, AF.Exp, [], [den_r], scale=-1.0)
                b.tt(num, num, den, ALU.mult, [nd_r], [nd_r])
                for c0 in range(0, T, 1024):
                    b.dma("sync", srT[:], ZT[C_SR + j * 128:C_SR + (j + 1) * 128, c0:c0 + 1024], srT_r, writes=[srT_r])
                    silu_to(ssil[:], ssil_r, srT[:], srT_r)
                    b.tt(actB[:, j, c0:c0 + 1024], nd[:, 0, c0:c0 + 1024], ssil[:], ALU.mult, [nd_r, ssil_r], [actB_r])
            b.end_scope(st)

        with ExitStack() as st:
            mqb, mqb_r = b.sb(st, "mqb", [128, T], BF16)
            mrT, mrT_r = b.sb(st, "mrT", [128, T], F32)
            mo, mo_r = b.sb(st, "mo", [128, 512], F32)
            mden, mden_r = b.sb(st, "mden", [128, 512], F32)
            PM = [b.sb(st, "PM", [128, 512], BF16) for _ in range(2)]
            for h in range(4):
                b.dma("gpsimd", mqb[:], ZT[C_MQ + h * 128:C_MQ + (h + 1) * 128, :], mqb_r, writes=[mqb_r])
                b.dma("sync", mrT[:], ZT[C_MR + h * 128:C_MR + (h + 1) * 128, :], mrT_r, writes=[mrT_r])
                silu_to(mrT[:], mrT_r, mrT[:], mrT_r)
                for tb in range(4):
                    sl = slice(tb * 512, (tb + 1) * 512)
                    for t2 in range(2):
                        pb, pb_r = b.bank()
                        b.mm(pb[:], KMT[:, h, t2 * 128:(t2 + 1) * 128], mqb[:, sl], True, True, [KMT_r, mqb_r], [pb_r])
                        b.act(PM[t2][0][:], pb[:], AF.Exp, [pb_r], [PM[t2][1]], scale=SC)
                    pb, pb_r = b.bank()
                    b.mm(pb[:], VM[:, 0, h * 128:(h + 1) * 128], PM[0][0][:], True, False, [VM_r, PM[0][1]], [pb_r])
                    b.mm(pb[:], VM[:, 1, h * 128:(h + 1) * 128], PM[1][0][:], False, True, [VM_r, PM[1][1]], [pb_r])
                    pb2, pb2_r = b.bank()
                    b.mm(pb2[:], onesB, PM[0][0][:], True, False, [cB_r, PM[0][1]], [pb2_r])
                    b.mm(pb2[:], onesB, PM[1][0][:], False, True, [cB_r, PM[1][1]], [pb2_r])
                    b.act(mden[:], pb2[:], AF.Ln, [pb2_r], [mden_r])
                    b.act(mden[:], mden[:], AF.Exp, [], [mden_r], scale=-1.0)
                    b.tt(mo[:], pb[:], mden[:], ALU.mult, [pb_r, mden_r], [mo_r])
                    b.tt(actC[:, h, sl], mo[:], mrT[:, sl], ALU.mult, [mo_r, mrT_r], [actC_r])
            b.end_scope(st)

        with ExitStack() as st:
            stA = ExitStack()
            zqk, zqk_r = b.sb(stA, "zqk", [4, 3072], F32)
            zrot, zrot_r = b.sb(stA, "zrot", [4, 3072], F32)
            angs, angs_r = b.sb(stA, "angs", [4, 64], F32)
            kis, kis_r = b.sb(stA, "kis", [4, 64], I32)
            kfs, kfs_r = b.sb(stA, "kfs", [4, 64], F32)
            m1s, m1s_r = b.sb(stA, "m1s", [4, 64], F32)
            rcs, rcs_r = b.sb(stA, "rcs", [4, 64], F32)
            coss, coss_r = b.sb(stA, "coss", [4, 64], F32)
            sins, sins_r = b.sb(stA, "sins", [4, 64], F32)
            cos12, cos12_r = b.sb(stA, "cos12", [4, 24, 64], F32)
            sin12, sin12_r = b.sb(stA, "sin12", [4, 24, 64], F32)
            t1, t1_r = b.sb(stA, "t1", [4, 24, 64], F32)
            t2, t2_r = b.sb(stA, "t2", [4, 24, 64], F32)
            sself, sself_r = b.sb(stA, "sself", [4, 12], F32)
            b.dma("sync", zqk[:], ZStm[:, C_SQ:C_SQ + 3072], zqk_r, writes=[zqk_r])
            b.dma("sync", angs[:], invrow_d[0:1, :].to_broadcast([4, 64]), angs_r, writes=[angs_r])
            b.ts(angs[:], angs[:], 16384.0, ALU.mult, [angs_r], [angs_r])
            sincos(angs[:], angs_r, kis[:], kis_r, kfs[:], kfs_r, m1s[:], m1s_r, rcs[:], rcs_r,
                   coss[:], coss_r, sins[:], sins_r)
            b.cp(cos12[:], coss[:].unsqueeze(1).to_broadcast([4, 24, 64]), [coss_r], [cos12_r])
            b.cp(sin12[:], sins[:].unsqueeze(1).to_broadcast([4, 24, 64]), [sins_r], [sin12_r])
            z3 = zqk[:, :].rearrange("p (h e) -> p h e", e=128)
            r3 = zrot[:, :].rearrange("p (h e) -> p h e", e=128)
            x1, x2 = z3[:, :, 0:64], z3[:, :, 64:128]
            b.tt(t1[:], x1, cos12[:], ALU.mult, [zqk_r, cos12_r], [t1_r])
            b.tt(t2[:], x2, sin12[:], ALU.mult, [zqk_r, sin12_r], [t2_r])
            b.tt(r3[:, :, 0:64], t1[:], t2[:], ALU.subtract, [t1_r, t2_r], [zrot_r])
            b.tt(t1[:], x2, cos12[:], ALU.mult, [zqk_r, cos12_r], [t1_r])
            b.tt(t2[:], x1, sin12[:], ALU.mult, [zqk_r, sin12_r], [t2_r])
            b.tt(r3[:, :, 64:128], t1[:], t2[:], ALU.add, [t1_r, t2_r], [zrot_r])
            b.dma("sync", ZSR[:, :], zrot[:], zrot_r, reads=[zrot_r])
            prod, prod_r = b.sb(stA, "prod", [4, 1536], F32)
            b.tt(prod[:], zrot[:, 0:1536], zrot[:, 1536:3072], ALU.mult, [zrot_r], [prod_r])
            b.op("vector", lambda e: e.tensor_reduce(out=sself[:], in_=prod[:, :].rearrange("p (h e) -> p h e", e=128),
                                                     axis=AX.X, op=ALU.add), [prod_r], [sself_r])
            b.act(sself[:], sself[:], AF.Exp, [sself_r], [sself_r], scale=SC)
            b.dma("sync", ZSP[:, :], sself[:], sself_r, reads=[sself_r])
            b.end_scope(stA)
            stA.close()

            def fmload(name, c0, nch, dt=F32, q="sync"):
                t, r = b.sb(st, name, [128, nch, 4], dt)
                b.dma(q, t[:], ZSfm[c0:c0 + nch * 128, :].rearrange("(c p) bb -> p c bb", p=128), r, writes=[r])
                return t, r

            gqTs, gqTs_r = fmload("gqTs", C_GQ, 4)
            gkTs, gkTs_r = fmload("gkTs", C_GK, 4)
            grTs, grTs_r = fmload("grTs", C_GR, 8)
            svTs, svTs_r = fmload("svTs", C_SV, 12)
            srTs, srTs_r = fmload("srTs", C_SR, 4)
            mrTs, mrTs_r = fmload("mrTs", C_MR, 4)
            gaTs, gaTs_r = b.sb(st, "gaTs", [32, 4], BF16)
            b.memset(gaTs[:], 1.0, [gaTs_r])
            b.dma("gpsimd", gaTs[0:16, :], ZSfm[C_GA:C_GA + 16, :], gaTs_r, writes=[gaTs_r])
            dec, dec_r = b.sb(st, "dec", [128, 16], F32)
            pb, pb_r = b.bank()
            for h in range(4):
                b.mm(pb[:, h * 4:(h + 1) * 4], wa2[0:17, h * 128:(h + 1) * 128], gaTs[0:17, :], True, True,
                     [wa2_r, gaTs_r], [pb_r])
            b.act(dec[:], pb[:, 0:16], AF.Exp, [pb_r], [dec_r], scale=-1.0)
            b.act(dec[:], dec[:], AF.Ln, [dec_r, epsc_r], [dec_r], bias=one_col)
            b.act(dec[:], dec[:], AF.Exp, [dec_r], [dec_r], scale=-1.0 / 16)

            S0_l = [b.sb(st, "S0", [128, 4, 256], F32) for _ in range(2)]
            Sn_l = [b.sb(st, "Sn", [128, 4, 256], F32) for _ in range(2)]
            vbc_l = [b.sb(st, "vbc", [128, 1024], F32) for _ in range(2)]
            tmpv_l = [b.sb(st, "tmpv", [128, 256], F32) for _ in range(2)]
            os8_l = [b.sb(st, "os8", [128, 8], F32) for _ in range(2)]
            sq8_l = [b.sb(st, "sq8", [128, 8], F32) for _ in range(2)]
            ss4_l = [b.sb(st, "ss4", [128, 4], F32) for _ in range(2)]
            gs8_l = [b.sb(st, "gs8", [128, 8], F32) for _ in range(2)]
            gt8_l = [b.sb(st, "gt8", [128, 8], F32) for _ in range(2)]
            gx8_l = [b.sb(st, "gx8", [128, 8], F32) for _ in range(2)]
            qbc_l = [b.sb(st, "qbc", [128, 1536], F32) for _ in range(2)]
            mqbc_l = [b.sb(st, "mqbc", [128, 512], F32) for _ in range(2)]
            psbc_l = [b.sb(st, "psbc", [128, 12], F32) for _ in range(2)]
            Kc_l = [b.sb(st, "Kc", [128, 3, 512], F32) for _ in range(2)]
            Vc_l = [b.sb(st, "Vc", [128, 3, 512], F32) for _ in range(2)]
            Kcm_l = [b.sb(st, "Kcm", [128, 2, 512], F32) for _ in range(2)]
            Vcm_l = [b.sb(st, "Vcm", [128, 2, 512], F32) for _ in range(2)]
            prd_l = [b.sb(st, "prd", [128, 512], F32) for _ in range(2)]
            sc12_l = [b.sb(st, "sc12", [128, 12], F32) for _ in range(2)]
            sm8_l = [b.sb(st, "sm8", [128, 8], F32) for _ in range(2)]
            n4_l = [b.sb(st, "n4", [128, 4], F32) for _ in range(2)]
            d4_l = [b.sb(st, "d4", [128, 4], F32) for _ in range(2)]
            u4_l = [b.sb(st, "u4", [128, 4], F32) for _ in range(2)]
            for bb in range(4):
                S0, S0_r = S0_l[bb % 2]
                Sn, Sn_r = Sn_l[bb % 2]
                vbc, vbc_r = vbc_l[bb % 2]
                tmpv, tmpv_r = tmpv_l[bb % 2]
                os8, os8_r = os8_l[bb % 2]
                sq8, sq8_r = sq8_l[bb % 2]
                ss4, ss4_r = ss4_l[bb % 2]
                gs8, gs8_r = gs8_l[bb % 2]
                gt8, gt8_r = gt8_l[bb % 2]
                gx8, gx8_r = gx8_l[bb % 2]
                qbc, qbc_r = qbc_l[bb % 2]
                mqbc, mqbc_r = mqbc_l[bb % 2]
                psbc, psbc_r = psbc_l[bb % 2]
                Kc, Kc_r = Kc_l[bb % 2]
                Vc, Vc_r = Vc_l[bb % 2]
                Kcm, Kcm_r = Kcm_l[bb % 2]
                Vcm, Vcm_r = Vcm_l[bb % 2]
                prd, prd_r = prd_l[bb % 2]
                sc12, sc12_r = sc12_l[bb % 2]
                sm8, sm8_r = sm8_l[bb % 2]
                n4, n4_r = n4_l[bb % 2]
                d4, d4_r = d4_l[bb % 2]
                u4, u4_r = u4_l[bb % 2]
                b.dma("sync", S0[:], sgla[bb].rearrange("h p v -> p h v"), S0_r, writes=[S0_r])
                b.dma("sync", vbc[:], ZStm[bb:bb + 1, C_GV:C_GV + 1024].to_broadcast([128, 1024]), vbc_r, writes=[vbc_r])
                for h in range(4):
                    b.ts(tmpv[:], vbc[:, h * 256:(h + 1) * 256], gkTs[:, h, bb:bb + 1], ALU.mult, [vbc_r, gkTs_r], [tmpv_r])
                    b.stt(Sn[:, h, :], S0[:, h, :], dec[:, h * 4 + bb:h * 4 + bb + 1], tmpv[:], ALU.mult, ALU.add,
                          [S0_r, dec_r, tmpv_r], [Sn_r])
                b.dma("sync", gla_s[bb].rearrange("h p v -> p h v"), Sn[:], Sn_r, reads=[Sn_r])
                pb, pb_r = b.bank()
                for h in range(4):
                    for e2 in range(2):
                        b.mm(pb[:, h * 2 + e2:h * 2 + e2 + 1], Sn[:, h, e2 * 128:(e2 + 1) * 128], gqTs[:, h, bb:bb + 1],
                             True, True, [Sn_r, gqTs_r], [pb_r])
                b.ts(os8[:], pb[:, 0:8], SC, ALU.mult, [pb_r], [os8_r])
                b.tt(sq8[:], os8[:], os8[:], ALU.mult, [os8_r], [sq8_r])
                pb2, pb2_r = b.bank()
                b.mm(pb2[:, 0:8], onesF, sq8[:], True, True, [cF_r, sq8_r], [pb2_r])
                b.op("vector", lambda e: e.tensor_reduce(out=ss4[:], in_=pb2[:, 0:8].rearrange("p (h e) -> p h e", e=2),
                                                         axis=AX.X, op=ALU.add), [pb2_r], [ss4_r])
                b.act(ss4[:], ss4[:], AF.Ln, [ss4_r, epsc_r], [ss4_r], scale=1.0 / 256, bias=epsc[:, 0:1])
                b.act(ss4[:], ss4[:], AF.Exp, [ss4_r], [ss4_r], scale=-0.5)
                b.cp(gx8[:], grTs[:, :, bb], [grTs_r], [gx8_r])
                silu_to(gs8[:], gs8_r, gx8[:], gx8_r, gt8[:], gt8_r)
                b.tt(os8[:], os8[:], gg[:, 0:8], ALU.mult, [os8_r, gg_r], [os8_r])
                b.tt(os8[:, :].rearrange("p (h e) -> p h e", e=2), os8[:, :].rearrange("p (h e) -> p h e", e=2),
                     ss4[:].unsqueeze(2).to_broadcast([128, 4, 2]), ALU.mult, [os8_r, ss4_r], [os8_r])
                b.tt(actA[:, :, T + bb], os8[:], gs8[:], ALU.mult, [os8_r, gs8_r], [actA_r])
                b.dma("sync", qbc[:], ZSR[bb:bb + 1, 0:1536].to_broadcast([128, 1536]), qbc_r, writes=[qbc_r])
                b.dma("sync", mqbc[:], ZStm[bb:bb + 1, C_MQ:C_MQ + 512].to_broadcast([128, 512]), mqbc_r, writes=[mqbc_r])
                b.dma("sync", psbc[:], ZSP[bb:bb + 1, :].to_broadcast([128, 12]), psbc_r, writes=[psbc_r])
                for g, (win, d) in enumerate(SWA):
                    csrc = cch[g][bb].rearrange("(u dd) e -> dd u e", dd=d)[0]
                    b.dma("sync" if g != 1 else "scalar", Kc[:, g, :], csrc[:, 0:512], Kc_r, writes=[Kc_r])
                    b.dma("sync", Vc[:, g, :], csrc[:, 512:1024], Vc_r, writes=[Vc_r])
                    b.tt(prd[:], Kc[:, g, :], qbc[:, g * 512:(g + 1) * 512], ALU.mult, [Kc_r, qbc_r], [prd_r])
                    b.op("vector", lambda e: e.tensor_reduce(out=sc12[:, g * 4:(g + 1) * 4],
                                                             in_=prd[:, :].rearrange("p (h e) -> p h e", e=128),
                                                             axis=AX.X, op=ALU.add), [prd_r], [sc12_r])
                b.act(sc12[:], sc12[:], AF.Exp, [sc12_r], [sc12_r], scale=SC)
                pb, pb_r = b.bank()
                for jj in range(4):
                    for g in range(3):
                        b.mm(pb[:, jj:jj + 1], Vc[:, g, jj * 128:(jj + 1) * 128], sc12[:, g * 4 + jj:g * 4 + jj + 1],
                             g == 0, g == 2, [Vc_r, sc12_r], [pb_r])
                pb2, pb2_r = b.bank()
                b.mm(pb2[:, 0:12], onesF, sc12[:], True, True, [cF_r, sc12_r], [pb2_r])
                b.cp(n4[:], pb[:, 0:4], [pb_r], [n4_r])
                b.cp(d4[:], pb2[:, 0:4], [pb2_r], [d4_r])
                for g in range(3):
                    b.tt(u4[:], psbc[:, g * 4:(g + 1) * 4], svTs[:, g * 4:(g + 1) * 4, bb], ALU.mult,
                         [psbc_r, svTs_r], [u4_r])
                    b.tt(n4[:], n4[:], u4[:], ALU.add, [n4_r, u4_r], [n4_r])
                    b.tt(d4[:], d4[:], psbc[:, g * 4:(g + 1) * 4], ALU.add, [d4_r, psbc_r], [d4_r])
                    if g > 0:
                        b.tt(d4[:], d4[:], pb2[:, g * 4:(g + 1) * 4], ALU.add, [d4_r, pb2_r], [d4_r])
                b.recip(d4[:], d4[:], [d4_r], [d4_r])
                b.tt(n4[:], n4[:], d4[:], ALU.mult, [n4_r, d4_r], [n4_r])
                b.cp(gx8[:, 0:4], srTs[:, :, bb], [srTs_r], [gx8_r])
                silu_to(gs8[:, 0:4], gs8_r, gx8[:, 0:4], gx8_r, gt8[:, 0:4], gt8_r)
                b.tt(actB[:, :, T + bb], n4[:], gs8[:, 0:4], ALU.mult, [n4_r, gs8_r], [actB_r])
                msrc = cmem[bb].rearrange("(t p) e -> p t e", p=128)
                b.dma("sync", Kcm[:], msrc[:, :, 0:512], Kcm_r, writes=[Kcm_r])
                b.dma("sync", Vcm[:], msrc[:, :, 512:1024], Vcm_r, writes=[Vcm_r])
                for t2i in range(2):
                    b.tt(prd[:], Kcm[:, t2i, :], mqbc[:], ALU.mult, [Kcm_r, mqbc_r], [prd_r])
                    b.op("vector", lambda e: e.tensor_reduce(out=sm8[:, t2i * 4:(t2i + 1) * 4],
                                                             in_=prd[:, :].rearrange("p (h e) -> p h e", e=128),
                                                             axis=AX.X, op=ALU.add), [prd_r], [sm8_r])
                b.act(sm8[:], sm8[:], AF.Exp, [sm8_r], [sm8_r], scale=SC)
                pb, pb_r = b.bank()
                for jj in range(4):
                    for t2i in range(2):
                        b.mm(pb[:, jj:jj + 1], Vcm[:, t2i, jj * 128:(jj + 1) * 128], sm8[:, t2i * 4 + jj:t2i * 4 + jj + 1],
                             t2i == 0, t2i == 1, [Vcm_r, sm8_r], [pb_r])
                pb2, pb2_r = b.bank()
                b.mm(pb2[:, 0:8], onesF, sm8[:], True, True, [cF_r, sm8_r], [pb2_r])
                b.cp(d4[:], pb2[:, 0:4], [pb2_r], [d4_r])
                b.tt(d4[:], d4[:], pb2[:, 4:8], ALU.add, [d4_r, pb2_r], [d4_r])
                b.recip(d4[:], d4[:], [d4_r], [d4_r])
                b.tt(n4[:], pb[:, 0:4], d4[:], ALU.mult, [pb_r, d4_r], [n4_r])
                b.cp(gx8[:, 0:4], mrTs[:, :, bb], [mrTs_r], [gx8_r])
                silu_to(gs8[:, 0:4], gs8_r, gx8[:, 0:4], gx8_r, gt8[:, 0:4], gt8_r)
                b.tt(actC[:, :, T + bb], n4[:], gs8[:, 0:4], ALU.mult, [n4_r, gs8_r], [actC_r])
                for g, (win, d) in enumerate(SWA):
                    b.dma("sync", swa_s[g][bb, win - 1:win, 0:512], ZSR[bb:bb + 1, 1536 + g * 512:1536 + (g + 1) * 512], dd_r)
                    b.dma("sync", swa_s[g][bb, win - 1:win, 512:1024],
                          ZStm[bb:bb + 1, C_SV + g * 512:C_SV + (g + 1) * 512], dd_r)
            b.end_scope(st)

        with ExitStack() as st:
            mTf, mTf_r = b.sb(st, "mTf", [128, NCH, T + 4], BF16)
            acts = [(actA, actA_r, 8, wpa_d, 0), (actB, actB_r, 4, wpb_d, 8), (actC, actC_r, 4, wpc_d, 12)]
            with ExitStack() as st1:
                wpfs = [b.sb(st1, "wpf", [128, 16, 128], BF16) for _ in range(2)]
                gtts = [b.sb(st1, "gtt", [128, 3, 1024], F32) for _ in range(3)]
                mgs = [b.sb(st1, "mg", [128, 512], F32) for _ in range(2)]
                mg2s = [b.sb(st1, "mg2", [128, 512], F32) for _ in range(2)]
                fc = 0
                gsrc = ZT[C_GT:C_GT + 6144, :].rearrange("(br f p) t -> f p br t", br=3, f=16, p=128)
                gsrc_s = ZSfm[C_GT:C_GT + 6144, :].rearrange("(br f p) t -> f p br t", br=3, f=16, p=128)
                for f in range(16):
                    wpf, wpf_r = wpfs[f % 2]
                    for (a_t, a_r, nk, wd, ko) in acts:
                        b.dma("gpsimd", wpf[:, ko:ko + nk, :],
                              wd[:, f * 128:(f + 1) * 128].rearrange("(c p) w -> p c w", p=128), wpf_r, writes=[wpf_r])
                    for tb2 in range(3):
                        NN = 1024 if tb2 < 2 else 4
                        t0 = tb2 * 1024 if tb2 < 2 else T
                        gtt, gtt_r = gtts[fc % 3]
                        fc += 1
                        src_ = gsrc[f][:, :, t0:t0 + NN] if tb2 < 2 else gsrc_s[f]
                        b.dma("sync" if fc % 2 == 0 else "scalar", gtt[:, :, 0:NN], src_, gtt_r, writes=[gtt_r])
                        b.act(gtt[:, :, 0:NN], gtt[:, :, 0:NN], AF.Sigmoid, [], [gtt_r])
                        for hb in range(2 if tb2 < 2 else 1):
                            N = 512 if tb2 < 2 else 4
                            tok0 = t0 + hb * 512
                            gs = slice(hb * 512, hb * 512 + N)
                            mg, mg_r = mgs[hb]
                            mg2, mg2_r = mg2s[hb]
                            for br, (a_t, a_r, nk, wd, ko) in enumerate(acts):
                                pb, pb_r = b.bank()
                                for c in range(nk):
                                    b.mm(pb[:, 0:N], wpf[:, ko + c, :], a_t[:, c, tok0:tok0 + N], c == 0, c == nk - 1,
                                         [wpf_r, a_r], [pb_r])
                                if br == 0:
                                    b.tt(mg[:, 0:N], pb[:, 0:N], gtt[:, 0, gs], ALU.mult, [pb_r, gtt_r], [mg_r])
                                else:
                                    b.tt(mg2[:, 0:N], pb[:, 0:N], gtt[:, br, gs], ALU.mult, [pb_r, gtt_r], [mg2_r])
                                    if br == 1:
                                        b.tt(mg[:, 0:N], mg[:, 0:N], mg2[:, 0:N], ALU.add, [mg_r, mg2_r], [mg_r])
                                    else:
                                        b.tt(mTf[:, f, tok0:tok0 + N], mg[:, 0:N], mg2[:, 0:N], ALU.add,
                                             [mg_r, mg2_r], [mTf_r])
                b.end_scope(st1)
            ssqa, ssqa_r = b.sb(st, "ssqa", [128, 17, 4], F32)
            with ExitStack() as st2:
                wobs = [b.sb(st2, "wob", [128, NCH, 512], BF16) for _ in range(2)]
                xcs = [b.sb(st2, "xc", [128, 4, 512], F32) for _ in range(3)]
                junk, junk_r = b.sb(st2, "junk", [128, 512], F32)
                b.memset(ssqa[:], 0.0, [ssqa_r])
                xcnt = 0
                for cb in range(4):
                    wob, wob_r = wobs[cb % 2]
                    wload(wob, wob_r, wo_d[:, cb * 512:(cb + 1) * 512], D, 512)
                    csl = slice(cb * 512, (cb + 1) * 512)
                    for tg in range(5):
                        ntl = 4 if tg < 4 else 1
                        M = 128 if tg < 4 else 4
                        xc, xc_r = xcs[xcnt % 3]
                        xcnt += 1
                        if tg < 4:
                            src_ = xs[3 * T + tg * 512:3 * T + (tg + 1) * 512, csl].rearrange("(i p) c -> p i c", p=128)
                            b.dma("sync", xc[:, :, :], src_, xc_r, writes=[xc_r])
                        else:
                            b.dma("sync", xc[0:4, 0, :], xsmp[:, csl], xc_r, writes=[xc_r])
                        for il in range(ntl):
                            ti = tg * 4 + il
                            tok0 = ti * 128 if tg < 4 else T
                            pb, pb_r = b.bank()
                            for c in range(NCH):
                                b.mm(pb[0:M, :], mTf[:, c, tok0:tok0 + M], wob[:, c, :], c == 0, c == NCH - 1,
                                     [mTf_r, wob_r], [pb_r])
                            b.tt(xc[0:M, il, :], xc[0:M, il, :], pb[0:M, :], ALU.add, [xc_r, pb_r], [xc_r])
                            b.act(junk[0:M, :], xc[0:M, il, :], AF.Square, [xc_r], [junk_r, ssqa_r],
                                  accum_out=ssqa[0:M, ti, cb:cb + 1])
                        if tg < 4:
                            dst = XN[tg * 512:(tg + 1) * 512, csl].rearrange("(i p) c -> p i c", p=128)
                            b.dma("scalar", dst, xc[:, :, :], xc_r, reads=[xc_r])
                        else:
                            b.dma("scalar", XNs[:, csl], xc[0:4, 0, :], xc_r, reads=[xc_r])
                b.end_scope(st2)
            with ExitStack() as st3:
                gfb, gfb_r = b.sb(st3, "gfb", [128, D], F32)
                b.dma("sync", gfb[:], gfin[0:1, :].to_broadcast([128, D]), gfb_r, writes=[gfb_r])
                xts = [b.sb(st3, "xts", [128, D], F32) for _ in range(4)]
                rst, rst_r = b.sb(st3, "rst", [128, 17], F32)
                b.op("vector", lambda e: e.tensor_reduce(out=rst[:], in_=ssqa[:], axis=AX.X, op=ALU.add), [ssqa_r], [rst_r])
                b.act(rst[:], rst[:], AF.Ln, [rst_r, epsc_r], [rst_r], scale=1.0 / D, bias=epsc[:, 0:1])
                b.act(rst[:], rst[:], AF.Exp, [rst_r], [rst_r], scale=-0.5)
                for ti in range(17):
                    M = 128 if ti < 16 else 4
                    tok0 = ti * 128 if ti < 16 else T
                    xt_, xt_r_ = xts[ti % 4]
                    src_ = XN[tok0:tok0 + M, :] if ti < 16 else XNs[:, :]
                    b.dma("sync" if ti % 2 == 0 else "gpsimd", xt_[0:M, :], src_, xt_r_, writes=[xt_r_])
                    b.stt(xt_[0:M, :], xt_[0:M, :], rst[0:M, ti:ti + 1], gfb[0:M, :], ALU.mult, ALU.mult,
                          [xt_r_, rst_r, gfb_r], [xt_r_])
                    dst = y_own[tok0:tok0 + M, :] if ti < 16 else y_smp[:, :]
                    b.dma("scalar" if ti % 2 == 0 else "sync", dst, xt_[0:M, :], xt_r_, reads=[xt_r_])
                b.end_scope(st3)
            b.end_scope(st)

        b.barrier()
        b.barrier()
    return nc


_NC = None
DEBUG = {}


def _prep_inputs(x_prompt, x_sample, mem_prompt, state_gla, cache_swa_w128, cache_swa_w512, cache_swa_w2048,
                 cache_mem_kv, g_norm, w_in, w_alpha2, b_alpha, g_gla_out, g_mem, w_mem_kv, w_proj_a, w_proj_b,
                 w_proj_c, w_out, g_final):
    f = lambda a: np.ascontiguousarray(np.asarray(a, dtype=np.float32))
    x_prompt = f(x_prompt)
    shared = {
        "w_in": f(w_in[0]),
        "wa2a": f(np.concatenate([np.asarray(w_alpha2[0]), np.asarray(b_alpha[0])[None, :]], axis=0)),
        "gnt": f(np.asarray(g_norm[0]).reshape(NCH, 128).T),
        "gmt": f(np.asarray(g_mem[0]).reshape(NCH, 128).T),
        "ggt": f(np.asarray(g_gla_out[0]).reshape(8, 128).T),
        "gfin": f(np.asarray(g_final).reshape(1, D)),
        "wmkv": f(w_mem_kv[0]),
        "wpa": f(w_proj_a[0]),
        "wpb": f(w_proj_b[0]),
        "wpc": f(w_proj_c[0]),
        "wo": f(w_out[0]),
    }
    maps = []
    for c in range(8):
        bq, j = c // 4, c % 4
        xs = np.zeros((4, T, D), np.float32)
        for s in range(4):
            sh = j - 3 + s
            if sh >= 0:
                xs[s] = x_prompt[bq, sh * T:(sh + 1) * T]
        meta = np.zeros((128, 4), np.float32)
        meta[:, 0] = float(j * T - T)
        meta[:, 1] = 0.0 if j > 0 else NEGM
        m = dict(shared)
        m.update({
            "xs": xs.reshape(4 * T, D),
            "xsmp": f(np.asarray(x_sample)[4 * c:4 * c + 4, 0]),
            "mem": f(np.asarray(mem_prompt)[bq]),
            "sgla": f(np.asarray(state_gla)[0, 4 * c:4 * c + 4]),
            "c128": f(np.asarray(cache_swa_w128)[0, 4 * c:4 * c + 4]).reshape(4, 128, 1024),
            "c512": f(np.asarray(cache_swa_w512)[0, 4 * c:4 * c + 4]).reshape(4, 512, 1024),
            "c2048": f(np.asarray(cache_swa_w2048)[0, 4 * c:4 * c + 4]).reshape(4, 2048, 1024),
            "cmem": f(np.asarray(cache_mem_kv)[0, 4 * c:4 * c + 4]).reshape(4, 256, 1024),
            "meta": meta,
        })
        maps.append(m)
    return maps


def _assemble(res):
    R = res.results
    y_prompt = np.zeros((2, 4 * T, D), np.float32)
    for c in range(8):
        y_prompt[c // 4, (c % 4) * T:(c % 4 + 1) * T] = R[c]["y_own"]
    y_sample = np.concatenate([R[c]["y_smp"] for c in range(8)], axis=0).reshape(32, 1, D)
    last = [3, 7]
    gla_prompt = np.stack([R[c]["gla_p"] for c in last])[None]
    swa_p = []
    for nm, w in (("swa128_p", 128), ("swa512_p", 512), ("swa2048_p", 2048)):
        swa_p.append(np.stack([R[c][nm].reshape(w, 2, 4, 128) for c in last])[None])
    mem_kv = np.stack([R[c]["memkv_p"].reshape(256, 2, 4, 128) for c in (0, 4)])[None]
    gla_sample = np.concatenate([R[c]["gla_s"] for c in range(8)], axis=0)[None]
    swa_s = []
    for nm, w in (("s128_s", 128), ("s512_s", 512), ("s2048_s", 2048)):
        swa_s.append(np.concatenate([R[c][nm].reshape(4, w, 2, 4, 128) for c in range(8)], axis=0)[None])
    return (y_prompt, y_sample, gla_prompt, swa_p[0], swa_p[1], swa_p[2], mem_kv, gla_sample,
            swa_s[0], swa_s[1], swa_s[2])


def kernel(**inputs):
    global _NC
    maps = _prep_inputs(**inputs)
    nc = build_program()
    res = run_bass_kernel_spmd(nc, maps, core_ids=list(range(8)))
    DEBUG["res"] = res
    outs = _assemble(res)
    return tuple(np.ascontiguousarray(o.astype(np.float32)) for o in outs)
```

```python
import numpy as np
from contextlib import ExitStack
import concourse.bass as bass
import concourse.mybir as mybir
from concourse.bass_utils import run_bass_kernel_spmd

F32 = mybir.dt.float32
BF16 = mybir.dt.bfloat16
I32 = mybir.dt.int32
AF = mybir.ActivationFunctionType
ALU = mybir.AluOpType
AX = mybir.AxisListType

D = 2048
NCH = 16
T = 2048
NT = 16
INW = 15376
C_GQ, C_GK, C_GV, C_GR, C_GA = 0, 512, 1024, 2048, 3072
C_SQ, C_SK, C_SV, C_SR, C_MQ, C_MR, C_GT = 3088, 4624, 6160, 7696, 8208, 8720, 9232
EPS = 1e-6
NEGM = -30000.0
SC = 128.0 ** -0.5
TWO_PI = 6.283185307179586
SWA = ((128, 1), (512, 4), (2048, 16))


class Reg:
    __slots__ = ("name", "last_w", "readers", "dsem", "dcount", "ssem", "scount", "excl")

    def __init__(self, name):
        self.name = name
        self.last_w = None
        self.readers = {}
        self.dsem = None
        self.dcount = 0
        self.ssem = None
        self.scount = 0
        self.excl = False


class EngS:
    def __init__(self, name, eng, sem):
        self.name = name
        self.eng = eng
        self.sem = sem
        self.count = 0
        self.waited = {}


class B:
    def __init__(self, nc, es):
        self.nc = nc
        self.es = es
        self.E = {}
        for n in ["tensor", "vector", "scalar", "gpsimd", "sync"]:
            sem = es.enter_context(nc.semaphore("s_" + n))
            self.E[n] = EngS(n, getattr(nc, n), sem)
        self.regs = []
        self.nreg = 0
        self.banks = []
        self.bank_i = 0
        self.sem_pool = []
        self.sw_pool = []
        self.scope_regs = {}

    def reg(self, name):
        self.nreg += 1
        r = Reg("%s_%d" % (name, self.nreg))
        self.regs.append(r)
        return r

    def sb(self, es, name, shape, dt):
        self.nreg += 1
        t = es.enter_context(self.nc.sbuf_tensor("%s_%d" % (name, self.nreg), shape, dt))
        r = self.reg(name)
        self.scope_regs.setdefault(id(es), []).append(r)
        return t, r

    def end_scope(self, es):
        self.barrier()
        for r in self.scope_regs.pop(id(es), []):
            if r.dsem is not None:
                self.sem_pool.append((r.dsem, r.dcount))
                r.dsem = None
            if r.ssem is not None:
                self.sw_pool.append((r.ssem, r.scount))
                r.ssem = None
            self.regs.remove(r)

    def _need(self, e, toks):
        best = {}
        for t in toks:
            if t is None:
                continue
            sem, val = t
            k = id(sem)
            if k not in best or val > best[k][1]:
                best[k] = (sem, val)
        for k, (sem, val) in best.items():
            if val > e.waited.get(k, 0):
                e.eng.wait_ge(sem, val)
                e.waited[k] = val

    def _collect(self, reads, writes):
        toks = []
        for r in reads:
            toks.append(r.last_w)
            if r.excl:
                toks.extend(r.readers.values())
        for w in writes:
            toks.append(w.last_w)
            toks.extend(w.readers.values())
        return toks

    def _update(self, tok, reads, writes):
        k = id(tok[0])
        for r in reads:
            o = r.readers.get(k)
            if o is None or o[1] < tok[1]:
                r.readers[k] = tok
        for w in writes:
            w.last_w = tok
            w.readers = {}

    def op(self, en, fn, reads=(), writes=(), signal=True):
        e = self.E[en]
        toks = self._collect(reads, writes)
        if en == "tensor":
            toks = [t for t in toks if t is not None and t[0] is not e.sem]
        self._need(e, toks)
        inst = fn(e.eng)
        if signal:
            inst.then_inc(e.sem, 1)
            e.count += 1
            tok = (e.sem, e.count)
        else:
            tok = (e.sem, e.count + 1)
        self._update(tok, reads, writes)

    def dma(self, qn, out, in_, sbr, reads=(), writes=()):
        q = self.E[qn]
        sw = qn == "gpsimd"
        if sw and sbr.ssem is None:
            if self.sw_pool:
                sbr.ssem, sbr.scount = self.sw_pool.pop()
            else:
                sbr.ssem = self.es.enter_context(self.nc.semaphore("s_" + sbr.name))
        if not sw and sbr.dsem is None:
            if self.sem_pool:
                sbr.dsem, sbr.dcount = self.sem_pool.pop()
            else:
                sbr.dsem = self.es.enter_context(self.nc.semaphore("d_" + sbr.name))
        toks = self._collect(reads, writes)
        self._need(q, toks)
        if sw:
            q.eng.dma_start(out=out, in_=in_).then_inc(sbr.ssem, 16)
            sbr.scount += 16
            tok = (sbr.ssem, sbr.scount)
        else:
            q.eng.dma_start(out=out, in_=in_).then_inc(sbr.dsem, 16)
            sbr.dcount += 16
            tok = (sbr.dsem, sbr.dcount)
        self._update(tok, reads, writes)

    def barrier(self):
        for e in self.E.values():
            toks = [(f.sem, f.count) for f in self.E.values() if f is not e and f.count > 0]
            toks += [(r.dsem, r.dcount) for r in self.regs if r.dsem is not None and r.dcount > 0]
            toks += [(r.ssem, r.scount) for r in self.regs if r.ssem is not None and r.scount > 0]
            self._need(e, toks)

    def bank(self):
        t, r = self.banks[self.bank_i % len(self.banks)]
        self.bank_i += 1
        return t, r

    def mm(self, out, lhsT, rhs, start, stop, reads, writes):
        self.op("tensor", lambda e: e.matmul(out, lhsT=lhsT, rhs=rhs, start=start, stop=stop),
                reads, writes, signal=stop)

    def tr(self, out, in_, ident, reads, writes):
        self.op("tensor", lambda e: e.transpose(out, in_, ident), reads, writes)

    def act(self, out, in_, func, reads, writes, scale=1.0, bias=0.0, accum_out=None, en="scalar"):
        kw = {}
        if accum_out is not None:
            kw["accum_out"] = accum_out
        self.op("scalar", lambda e: e.activation(out=out, in_=in_, func=func, bias=bias, scale=scale, **kw),
                reads, writes)

    def tt(self, out, in0, in1, op, reads, writes, en="vector"):
        self.op(en, lambda e: e.tensor_tensor(out=out, in0=in0, in1=in1, op=op), reads, writes)

    def ts(self, out, in0, s1, op0, reads, writes, s2=None, op1=None, en="vector"):
        if op1 is None:
            self.op(en, lambda e: e.tensor_scalar(out=out, in0=in0, scalar1=s1, scalar2=None, op0=op0), reads, writes)
        else:
            self.op(en, lambda e: e.tensor_scalar(out=out, in0=in0, scalar1=s1, scalar2=s2, op0=op0, op1=op1),
                    reads, writes)

    def stt(self, out, in0, scalar, in1, op0, op1, reads, writes, en="vector"):
        self.op(en, lambda e: e.scalar_tensor_tensor(out=out, in0=in0, scalar=scalar, in1=in1, op0=op0, op1=op1),
                reads, writes)

    def cp(self, out, in_, reads, writes, en="vector"):
        self.op(en, lambda e: e.tensor_copy(out=out, in_=in_), reads, writes)

    def recip(self, out, in_, reads, writes):
        self.op("vector", lambda e: e.reciprocal(out=out, in_=in_), reads, writes)

    def memset(self, ap, val, writes, en="vector"):
        self.op(en, lambda e: e.memset(ap, val), (), writes)


def build_program():
    nc = bass.Bass("TRN2", target_bir_lowering=False)

    def din(name, shape):
        return nc.dram_tensor(name, list(shape), F32, kind="ExternalInput").ap()

    def dout(name, shape):
        return nc.dram_tensor(name, list(shape), F32, kind="ExternalOutput").ap()

    def dscr(name, shape, dt=F32):
        return nc.dram_tensor(name, list(shape), dt).ap()

    xs = din("xs", [4 * T, D])
    xsmp = din("xsmp", [4, D])
    mem = din("mem", [256, D])
    w_in = din("w_in", [D, INW])
    wa2a = din("wa2a", [17, 512])
    gnt = din("gnt", [128, NCH])
    gmt = din("gmt", [128, NCH])
    ggt = din("ggt", [128, 8])
    gfin = din("gfin", [1, D])
    wmkv = din("wmkv", [D, 1024])
    wpa_d = din("wpa", [1024, D])
    wpb_d = din("wpb", [512, D])
    wpc_d = din("wpc", [512, D])
    wo_d = din("wo", [D, D])
    sgla = din("sgla", [4, 4, 128, 256])
    cch = [din("c128", [4, 128, 1024]), din("c512", [4, 512, 1024]), din("c2048", [4, 2048, 1024])]
    cmem = din("cmem", [4, 256, 1024])
    meta = din("meta", [128, 4])

    y_own = dout("y_own", [T, D])
    y_smp = dout("y_smp", [4, D])
    gla_p = dout("gla_p", [4, 128, 256])
    swa_p = [dout("swa128_p", [128, 1024]), dout("swa512_p", [512, 1024]), dout("swa2048_p", [2048, 1024])]
    memkv_p = dout("memkv_p", [256, 1024])
    gla_s = dout("gla_s", [4, 4, 128, 256])
    swa_s = [dout("s128_s", [4, 128, 1024]), dout("s512_s", [4, 512, 1024]), dout("s2048_s", [4, 2048, 1024])]

    Zg = dscr("Zg", [4 * T, 1536])
    Zs = dscr("Zs", [2 * T, 1536])
    ZT = dscr("ZT", [INW, T])
    ZTga = dscr("ZTga", [4, 16, T])
    ZTskh = dscr("ZTskh", [1536, T])
    ZStm = dscr("ZStm", [4, INW])
    ZSfm = dscr("ZSfm", [INW, 4])
    ZSR = dscr("ZSR", [4, 3072])
    ZSP = dscr("ZSP", [4, 12])
    XN = dscr("XN", [T, D])
    XNs = dscr("XNs", [4, D])

    pp = np.arange(128)
    tri_np = (pp[:, None] <= pp[None, :]).astype(np.float32)
    consts_np = np.zeros((128, 7, 128), np.float32)
    consts_np[:, 0] = np.eye(128, dtype=np.float32)
    consts_np[:, 1] = tri_np
    consts_np[:, 2] = 1.0 - tri_np
    consts_np[:, 3] = NEGM * (1.0 - tri_np.T)
    consts_np[:, 4] = NEGM * (1.0 - tri_np)
    consts_np[:, 5] = 1.0
    consts_np[:, 6] = np.roll(np.eye(128, dtype=np.float32), 64, axis=0)
    consts_d = nc.inline_tensor(consts_np, name="consts").ap()
    half = 64
    inv_np = (np.float32(10000.0) ** (-(np.arange(half, dtype=np.float32) / np.float32(half)))).astype(np.float32)
    col_np = np.zeros((128, 2), np.float32)
    col_np[:, 0] = np.concatenate([inv_np, inv_np])
    col_np[:64, 1] = -1.0
    col_np[64:, 1] = 1.0
    col_d = nc.inline_tensor(col_np, name="colc").ap()
    invrow_d = nc.inline_tensor(inv_np.reshape(1, 64).copy(), name="invrow").ap()

    with ExitStack() as es:
        b = B(nc, es)
        for i in range(8):
            t = es.enter_context(nc.psum_tensor("bank%d" % i, [128, 512], F32))
            b.banks.append((t, b.reg("bank")))
            b.banks[-1][1].excl = True

        cF, cF_r = b.sb(es, "cF", [128, 7, 128], F32)
        cB, cB_r = b.sb(es, "cB", [128, 7, 128], BF16)
        colc, colc_r = b.sb(es, "colc", [128, 2], F32)
        metat, meta_r = b.sb(es, "meta", [128, 4], F32)
        gn, gn_r = b.sb(es, "gn", [128, NCH], F32)
        gm, gm_r = b.sb(es, "gm", [128, NCH], F32)
        gg, gg_r = b.sb(es, "gg", [128, 8], F32)
        wa2, wa2_r = b.sb(es, "wa2", [32, 512], BF16)
        b.dma("sync", cF[:], consts_d[:, :, :], cF_r, writes=[cF_r])
        b.dma("gpsimd", cB[:], consts_d[:, :, :], cB_r, writes=[cB_r])
        b.dma("sync", colc[:], col_d[:, :], colc_r, writes=[colc_r])
        b.dma("sync", metat[:], meta[:, :], meta_r, writes=[meta_r])
        b.dma("sync", gn[:], gnt[:, :], gn_r, writes=[gn_r])
        b.dma("sync", gm[:], gmt[:, :], gm_r, writes=[gm_r])
        b.dma("sync", gg[:], ggt[:, :], gg_r, writes=[gg_r])
        b.dma("gpsimd", wa2[0:17, :], wa2a[:, :], wa2_r, writes=[wa2_r])
        identF = cF[:, 0, :]
        triF = cF[:, 1, :]
        onesF = cF[:, 5, :]
        identB = cB[:, 0, :]
        triB = cB[:, 1, :]
        LmB = cB[:, 2, :]
        mskP = cB[:, 3, :]
        mskC = cB[:, 4, :]
        onesB = cB[:, 5, :]
        PswB = cB[:, 6, :]

        KMT, KMT_r = b.sb(es, "KMT", [128, 4, 256], BF16)
        VM, VM_r = b.sb(es, "VM", [128, 2, 512], BF16)
        hsT, hsT_r = b.sb(es, "hsT", [128, NCH, 4], BF16)
        Sst, Sst_r = b.sb(es, "Sst", [128, 4, 256], F32)
        dd_r = b.reg("dram2dram")


        def norm_T(es_l, x_rows, nrows, gt, dst_fn, dst_reg, xq="sync"):
            xt, xt_r = b.sb(es_l, "xt", [128, D], F32)
            sq, sq_r = b.sb(es_l, "sq", [128, D], F32)
            ss, ss_r = b.sb(es_l, "ss", [128, 2], F32)
            return xt, xt_r, sq, sq_r, ss, ss_r

        def rms_rows(xt, xt_r, sq, sq_r, ss, ss_r, n, dim):
            b.act(sq[0:n, 0:dim], xt[0:n, 0:dim], AF.Square, [xt_r], [sq_r, ss_r], accum_out=ss[0:n, 0:1])
            b.act(ss[0:n, 1:2], ss[0:n, 0:1], AF.Ln, [ss_r, epsc_r], [ss_r], scale=1.0 / dim, bias=epsc[0:n, 0:1])
            b.act(ss[0:n, 1:2], ss[0:n, 1:2], AF.Exp, [ss_r], [ss_r], scale=-0.5)

        epsc, epsc_r = b.sb(es, "epsc", [128, 2], F32)
        b.memset(epsc[:, 0:1], EPS, [epsc_r])
        b.memset(epsc[:, 1:2], 1.0, [epsc_r])
        one_col = epsc[:, 1:2]

        def build_hT(es_unused, rows_ap, ntiles, nrows_last, gt, gt_r, hT, hT_r):
          with ExitStack() as es_l:
            xt, xt_r, sq, sq_r, ss, ss_r = norm_T(es_l, None, None, None, None, None)
            xt2, xt2_r = b.sb(es_l, "xt2", [128, D], F32)
            xbuf = [(xt, xt_r), (xt2, xt2_r)]
            sq2, sq2_r = b.sb(es_l, "sq2", [128, D], F32)
            jk, jk_r = b.sb(es_l, "jk", [128, D], F32)
            ss2, ss2_r = b.sb(es_l, "ss2", [128, 2], F32)
            sqb_ = [(sq, sq_r, ss, ss_r), (sq2, sq2_r, ss2, ss2_r)]
            for i in range(ntiles):
                n = 128 if i < ntiles - 1 or nrows_last == 128 else nrows_last
                xa, xa_r = xbuf[i % 2]
                sq, sq_r, ss, ss_r = sqb_[i % 2]
                b.dma("sync", xa[0:n, :], rows_ap[i * 128:i * 128 + n, :], xa_r, writes=[xa_r])
                rms_rows(xa, xa_r, jk, jk_r, ss, ss_r, n, D)
                b.act(sq[0:n, :], xa[0:n, :], AF.Copy, [xa_r, ss_r], [sq_r], scale=ss[0:n, 1:2])
                for q in range(4):
                    pb, pb_r = b.bank()
                    for cc in range(4):
                        c = q * 4 + cc
                        b.tr(pb[:, cc * 128:cc * 128 + n], sq[0:n, c * 128:(c + 1) * 128], identF[0:n, 0:n],
                             [sq_r, cF_r], [pb_r])
                    b.tt(hT[:, q * 4:(q + 1) * 4, i * 128:i * 128 + n],
                         pb[:, :].rearrange("p (c t) -> p c t", c=4)[:, :, 0:n],
                         gt[:, q * 4:(q + 1) * 4].unsqueeze(2).to_broadcast([128, 4, n]),
                         ALU.mult, [pb_r, gt_r], [hT_r])
            b.end_scope(es_l)

        stg = []
        copy_chunks = []
        for g, (win, dil) in enumerate(SWA):
            for bb in range(4):
                for r0 in range(0, win - 1, 128):
                    r1 = min(r0 + 128, win - 1)
                    copy_chunks.append((swa_s[g][bb, r0:r1, :], cch[g][bb, r0 + 1:r1 + 1, :]))

        def wload(wb, wb_r, src, rows, w):
            nchk = rows // 128
            b.dma("gpsimd", wb[:, 0:nchk, 0:w], src.rearrange("(c p) w -> p c w", p=128), wb_r, writes=[wb_r])

        with ExitStack() as st:
            hmT, hmT_r = b.sb(st, "hmT", [128, NCH, 256], BF16)
            build_hT(st, mem, 2, 128, gm, gm_r, hmT, hmT_r)
            mk, mk_r = b.sb(st, "mk", [128, 2, 1024], F32)
            for cb in range(2):
                wb, wb_r = b.sb(st, "wbm", [128, NCH, 512], BF16)
                wload(wb, wb_r, wmkv[:, cb * 512:(cb + 1) * 512], D, 512)
                for i in range(2):
                    pb, pb_r = b.bank()
                    for c in range(NCH):
                        b.mm(pb[:, :], hmT[:, c, i * 128:(i + 1) * 128], wb[:, c, :], c == 0, c == NCH - 1,
                             [hmT_r, wb_r], [pb_r])
                    b.cp(mk[:, i, cb * 512:(cb + 1) * 512], pb[:, :], [pb_r], [mk_r])
                if cb == 0:
                    for h in range(4):
                        pb, pb_r = b.bank()
                        for c in range(NCH):
                            b.mm(pb[:, 0:256], wb[:, c, h * 128:(h + 1) * 128], hmT[:, c, :], c == 0, c == NCH - 1,
                                 [hmT_r, wb_r], [pb_r])
                        b.cp(KMT[:, h, :], pb[:, 0:256], [pb_r], [KMT_r])
            b.cp(VM[:, :, :], mk[:, :, 512:1024], [mk_r], [VM_r])
            b.dma("sync", memkv_p.rearrange("(i p) e -> p i e", p=128), mk[:, :, :], mk_r, reads=[mk_r])
            build_hT(st, xsmp, 1, 4, gn, gn_r, hsT, hsT_r)
            b.end_scope(st)

        PI_LO = 3.1415925

        def silu_to(dst, dst_r, x, x_r, tmp=None, tmp_r=None):
            rd = [] if dst_r is x_r else [x_r]
            b.act(dst, x, AF.Silu, rd, [dst_r])

        def sincos(ang, ang_r, ki, ki_r, kf, kf_r, m1, m1_r, rc, rc_r, cos_out, cos_r, sin_out, sin_r):
            b.ts(ki, ang, 1.0 / TWO_PI, ALU.mult, [ang_r], [ki_r])
            b.cp(kf, ki, [ki_r], [kf_r])
            b.stt(ang, kf, -6.28125, ang, ALU.mult, ALU.add, [kf_r, ang_r], [ang_r])
            b.stt(ang, kf, -0.0019353071795864769, ang, ALU.mult, ALU.add, [kf_r, ang_r], [ang_r])
            b.ts(m1, ang, PI_LO, ALU.is_gt, [ang_r], [m1_r])
            b.stt(ang, m1, -TWO_PI, ang, ALU.mult, ALU.add, [m1_r, ang_r], [ang_r])
            b.ts(m1, ang, -PI_LO, ALU.is_lt, [ang_r], [m1_r])
            b.stt(ang, m1, TWO_PI, ang, ALU.mult, ALU.add, [m1_r, ang_r], [ang_r])
            b.ts(rc, ang, np.pi / 2, ALU.add, [ang_r], [rc_r])
            b.ts(m1, rc, PI_LO, ALU.is_gt, [rc_r], [m1_r])
            b.stt(rc, m1, -TWO_PI, rc, ALU.mult, ALU.add, [m1_r, rc_r], [rc_r])
            b.ts(ang, ang, PI_LO, ALU.min, [ang_r], [ang_r], s2=-PI_LO, op1=ALU.max)
            b.ts(rc, rc, PI_LO, ALU.min, [rc_r], [rc_r], s2=-PI_LO, op1=ALU.max)
            b.act(sin_out, ang, AF.Sin, [ang_r], [sin_r])
            b.act(cos_out, rc, AF.Sin, [rc_r], [cos_r])

        stT = ExitStack()
        cosT, cos_r = b.sb(stT, "cosT", [128, 2 * T], F32)
        sinT, sin_r = b.sb(stT, "sinT", [128, 2 * T], F32)
        def make_table_steps(sc):
            TW = 512
            ang, ang_r = b.sb(sc, "ang", [128, TW], F32)
            ki, ki_r = b.sb(sc, "ki", [128, TW], I32)
            kf, kf_r = b.sb(sc, "kf", [128, TW], F32)
            m1, m1_r = b.sb(sc, "m1", [128, TW], F32)
            rc, rc_r = b.sb(sc, "rc", [128, TW], F32)

            def step(ch):
                b.op("gpsimd", lambda e: e.iota(ki[:], pattern=[[1, TW]], base=ch * TW, channel_multiplier=0),
                     (), [ki_r])
                b.cp(ang[:], ki[:], [ki_r], [ang_r])
                b.ts(ang[:], ang[:], metat[:, 0:1], ALU.add, [ang_r, meta_r, colc_r], [ang_r], s2=colc[:, 0:1],
                     op1=ALU.mult)
                csl = slice(ch * TW, (ch + 1) * TW)
                sincos(ang[:], ang_r, ki[:], ki_r, kf[:], kf_r, m1[:], m1_r, rc[:], rc_r,
                       cosT[:, csl], cos_r, sinT[:, csl], sin_r)
                b.ts(sinT[:, csl], sinT[:, csl], colc[:, 1:2], ALU.mult, [sin_r, colc_r], [sin_r])

            return [lambda ch=ch: step(ch) for ch in range(2 * T // TW)]

        table_steps = []

        for s in range(4):
            own = s == 3
            halo = s == 2
            with ExitStack() as st:
                hT, hT_r = b.sb(st, "hT", [128, NCH, T], BF16)
                wbs = [b.sb(st, "wb", [128, NCH, 512], BF16) for _ in range(2)]
                stgs = [b.sb(st, "stg", [128, 512], F32) for _ in range(6)]
                cnt = {"w": 0, "s": 0}
                wload(wbs[0][0], wbs[0][1], w_in[:, C_GK:C_GK + 512], D, 512)
                pre_w = [C_GK]
                build_hT(st, xs[s * T:(s + 1) * T, :], NT, 128, gn, gn_r, hT, hT_r)
                if s == 0:
                    table_steps.extend(make_table_steps(st))

                def next_w():
                    r = wbs[cnt["w"] % 2]
                    cnt["w"] += 1
                    return r

                def trickle(flush=False):
                    if own and copy_chunks and (flush or cnt["s"] % 6 == 0):
                        for _ in range(len(copy_chunks) if flush else 1):
                            d_, s_ = copy_chunks.pop(0)
                            b.dma("sync", d_, s_, dd_r)

                def _in(c, lo, hi):
                    return lo <= c < hi

                def need_tm(c):
                    return _in(c, C_GV, C_GR) or _in(c, C_SQ, C_SR) or _in(c, C_MQ, C_MR)

                def need_fm(c):
                    return not (_in(c, C_GV, C_GR) or _in(c, C_SQ, C_SV) or _in(c, C_MQ, C_MR))

                def evac(pb, pb_r, np_, nf, dst):
                    sg, sg_r = stgs[cnt["s"] % 6]
                    cnt["s"] += 1
                    if cnt["s"] % 2 == 0:
                        b.cp(sg[0:np_, 0:nf], pb[0:np_, 0:nf], [pb_r], [sg_r])
                    else:
                        b.act(sg[0:np_, 0:nf], pb[0:np_, 0:nf], AF.Copy, [pb_r], [sg_r])
                    b.dma("sync", dst, sg[0:np_, 0:nf], sg_r, reads=[sg_r])
                    trickle()

                xbs = [b.sb(st, "xbr", [128, 512], BF16) for _ in range(3)]
                xbc = [0]
                t2s = [b.sb(st, "t2r", [128, 512], F32) for _ in range(2)]
                pend = []

                def rope_finish(pb, pb_r, xb, xb_r, tcol, dst):
                    pb2, pb2_r = b.bank()
                    b.mm(pb2[:, :], PswB, xb[:, :], True, True, [cB_r, xb_r], [pb2_r])
                    sg, sg_r = stgs[cnt["s"] % 6]
                    t2, t2_r = t2s[cnt["s"] % 2]
                    cnt["s"] += 1
                    b.tt(sg[:, :], pb[:, :], cosT[:, tcol:tcol + 512], ALU.mult, [pb_r, cos_r], [sg_r])
                    b.tt(t2[:, :], pb2[:, :], sinT[:, tcol:tcol + 512], ALU.mult, [pb2_r, sin_r], [t2_r])
                    b.tt(sg[:, :], sg[:, :], t2[:, :], ALU.add, [sg_r, t2_r], [sg_r])
                    b.dma("sync", dst, sg[:, :], sg_r, reads=[sg_r])
                    trickle()

                def fm_job(col0, ncols, dst_rows_fn, tb0_fn=lambda c0: 0, rope_t0=None):
                    for c0 in range(col0, col0 + ncols, 512):
                        w = min(512, col0 + ncols - c0)
                        wb, wb_r = next_w()
                        wload(wb, wb_r, w_in[:, c0:c0 + w], D, w)
                        for m0 in range(0, w, 128):
                            mw = min(128, w - m0)
                            for tb in range(tb0_fn(c0), 4):
                                pb, pb_r = b.bank()
                                for c in range(NCH):
                                    b.mm(pb[0:mw, :], wb[:, c, m0:m0 + mw], hT[:, c, tb * 512:(tb + 1) * 512],
                                         c == 0, c == NCH - 1, [wb_r, hT_r], [pb_r])
                                dst_ = dst_rows_fn(c0 + m0, mw)[:, tb * 512:(tb + 1) * 512]
                                if rope_t0 is None:
                                    evac(pb, pb_r, mw, 512, dst_)
                                else:
                                    xb, xb_r = xbs[xbc[0] % 3]
                                    xbc[0] += 1
                                    b.act(xb[:, :], pb[:, :], AF.Copy, [pb_r], [xb_r])
                                    prev = pend[:]
                                    del pend[:]
                                    pend.append((pb, pb_r, xb, xb_r, rope_t0 + tb * 512, dst_))
                                    for pa in prev:
                                        rope_finish(*pa)
                            if rope_t0 is not None and m0 + 128 >= w and c0 + 512 >= col0 + ncols:
                                for pa in pend:
                                    rope_finish(*pa)
                                del pend[:]
                            if own and need_fm(c0):
                                pb, pb_r = b.bank()
                                for c in range(NCH):
                                    b.mm(pb[0:mw, 0:4], wb[:, c, m0:m0 + mw], hsT[:, c, :], c == 0, c == NCH - 1,
                                         [wb_r, hsT_r], [pb_r])
                                evac(pb, pb_r, mw, 4, ZSfm[c0 + m0:c0 + m0 + mw, :])
                        if own and need_tm(c0):
                            pb, pb_r = b.bank()
                            for c in range(NCH):
                                b.mm(pb[0:4, 0:w], hsT[:, c, :], wb[:, c, 0:w], c == 0, c == NCH - 1,
                                     [wb_r, hsT_r], [pb_r])
                            evac(pb, pb_r, 4, w, ZStm[:, c0:c0 + w])

                def tm_job(col0, ncols, dst_fn, tile0_fn=lambda c0: 0):
                    for c0 in range(col0, col0 + ncols, 512):
                        w = 512
                        wb, wb_r = next_w()
                        if pre_w and pre_w[0] == c0:
                            pre_w.pop()
                        else:
                            wload(wb, wb_r, w_in[:, c0:c0 + w], D, w)
                        for i in range(tile0_fn(c0), NT):
                            pb, pb_r = b.bank()
                            for c in range(NCH):
                                b.mm(pb[:, :], hT[:, c, i * 128:(i + 1) * 128], wb[:, c, :], c == 0, c == NCH - 1,
                                     [wb_r, hT_r], [pb_r])
                            evac(pb, pb_r, 128, 512, dst_fn(i, c0))
                            if table_steps:
                                table_steps.pop(0)()
                        if own and need_fm(c0):
                            for m0 in range(0, w, 128):
                                pb, pb_r = b.bank()
                                for c in range(NCH):
                                    b.mm(pb[:, 0:4], wb[:, c, m0:m0 + 128], hsT[:, c, :], c == 0, c == NCH - 1,
                                         [wb_r, hsT_r], [pb_r])
                                evac(pb, pb_r, 128, 4, ZSfm[c0 + m0:c0 + m0 + 128, :])
                        if own and need_tm(c0):
                            pb, pb_r = b.bank()
                            for c in range(NCH):
                                b.mm(pb[0:4, 0:w], hsT[:, c, :], wb[:, c, 0:w], c == 0, c == NCH - 1,
                                     [wb_r, hsT_r], [pb_r])
                            evac(pb, pb_r, 4, w, ZStm[:, c0:c0 + w])

                tm_job(C_GK, 1536, lambda i, c0: Zg[s * T + i * 128:s * T + (i + 1) * 128, c0 - C_GK:c0 - C_GK + 512])
                fm_job(C_GA, 16, lambda r0, n: ZTga[s, r0 - C_GA:r0 - C_GA + n, :] if not own else ZTga[s, r0 - C_GA:r0 - C_GA + n, :])
                while table_steps:
                    table_steps.pop(0)()
                if halo or own:
                    tm_job(C_SV, 1536,
                           lambda i, c0: Zs[(s - 2) * T + i * 128:(s - 2) * T + (i + 1) * 128, c0 - C_SV:c0 - C_SV + 512],
                           tile0_fn=(lambda c0: NT - min(NT, SWA[(c0 - C_SV) // 512][0] // 128)) if halo else (lambda c0: 0))
                if halo:
                    fm_job(C_SK, 1536, lambda r0, n: ZTskh[r0 - C_SK:r0 - C_SK + n, :],
                           tb0_fn=lambda c0: 4 - max(1, SWA[(c0 - C_SK) // 512][0] // 512), rope_t0=0)
                if own:
                    fm_job(C_GQ, 1024, lambda r0, n: ZT[r0:r0 + n, :])
                    fm_job(C_GR, 1024, lambda r0, n: ZT[r0:r0 + n, :])
                    fm_job(C_SQ, 3072, lambda r0, n: ZT[r0:r0 + n, :], rope_t0=T)
                    fm_job(C_SR, INW - C_SR, lambda r0, n: ZT[r0:r0 + n, :])
                    trickle(flush=True)
                b.end_scope(st)

        b.end_scope(stT)
        stT.close()
        actA, actA_r = b.sb(es, "actA", [128, 8, T + 4], BF16)
        actB, actB_r = b.sb(es, "actB", [128, 4, T + 4], BF16)
        actC, actC_r = b.sb(es, "actC", [128, 4, T + 4], BF16)
        with ExitStack() as st:
            gaT, gaT_r = b.sb(st, "gaT", [32, T], BF16)
            kts = [b.sb(st, "kt", [128, NT, 128], F32) for _ in range(2)]
            vbs2 = [b.sb(st, "vb", [128, NT, 256], BF16) for _ in range(2)]
            la, la_r = b.sb(st, "la", [128, NT, 128], BF16)
            kh, kh_r = b.sb(st, "kh", [128, NT, 128], BF16)
            Sb, Sb_r = b.sb(st, "Sb", [128, NT, 256], BF16)
            tE, tE_r = b.sb(st, "tE", [128, 512], F32)
            tE2, tE2_r = b.sb(st, "tE2", [128, 512], F32)
            Td, Td_r = b.sb(st, "Td", [128, NT], F32)
            qT, qT_r = b.sb(st, "qT", [128, T], F32)
            kT, kT_r = b.sb(st, "kT", [128, T], F32)
            qh, qh_r = b.sb(st, "qh", [128, T], BF16)
            kth, kth_r = b.sb(st, "kth", [128, T], BF16)
            AT, AT_r = b.sb(st, "AT", [128, NT, 128], BF16)
            oF, oF_r = b.sb(st, "oF", [128, 2, T], F32)
            grT, grT_r = b.sb(st, "grT", [128, 2, T], F32)
            sqb, sqb_r = b.sb(st, "sqb", [128, 2, 512], BF16)
            rstd, rstd_r = b.sb(st, "rstd", [128, 512], F32)
            tmpn, tmpn_r = b.sb(st, "tmpn", [128, 512], F32)
            b.memset(Sst[:], 0.0, [Sst_r])
            b.memset(gaT[:], 1.0, [gaT_r])
            for s in range(4):
                own = s == 3
                b.dma("gpsimd", gaT[0:16, :], ZTga[s, :, :], gaT_r, writes=[gaT_r])
                for hh in range(4):
                    kt, kt_r = kts[(s * 4 + hh) % 2]
                    vb, vb_r = vbs2[(s * 4 + hh) % 2]
                    b.dma("sync" if hh % 2 == 0 else "scalar", kt[:],
                          Zg[s * T:(s + 1) * T, hh * 128:(hh + 1) * 128].rearrange("(i p) e -> p i e", p=128),
                          kt_r, writes=[kt_r])
                    b.dma("gpsimd", vb[:],
                          Zg[s * T:(s + 1) * T, 512 + hh * 256:512 + (hh + 1) * 256].rearrange("(i p) e -> p i e", p=128),
                          vb_r, writes=[vb_r])
                    for q in range(4):
                        pb, pb_r = b.bank()
                        for cc in range(4):
                            i = q * 4 + cc
                            b.mm(pb[:, cc * 128:(cc + 1) * 128], gaT[0:17, i * 128:(i + 1) * 128],
                                 wa2[0:17, hh * 128:(hh + 1) * 128], True, True, [gaT_r, wa2_r], [pb_r])
                        b.act(tE[:], pb[:], AF.Exp, [pb_r], [tE_r], scale=-1.0)
                        b.act(la[:, q * 4:(q + 1) * 4, :], tE[:].rearrange("p (c t) -> p c t", c=4), AF.Ln,
                              [tE_r, epsc_r], [la_r], bias=one_col)
                    if not own:
                        for q in range(4):
                            pb, pb_r = b.bank()
                            for cc in range(4):
                                i = q * 4 + cc
                                b.mm(pb[:, cc * 128:(cc + 1) * 128], LmB, la[:, i, :], True, i == NT - 1,
                                     [cB_r, la_r], [pb_r])
                                for j2 in range(i + 1, NT):
                                    b.mm(pb[:, cc * 128:(cc + 1) * 128], onesB, la[:, j2, :], False, j2 == NT - 1,
                                         [cB_r, la_r], [pb_r])
                            b.act(tE[:], pb[:], AF.Exp, [pb_r], [tE_r], scale=-1.0 / 16)
                            b.tt(kh[:, q * 4:(q + 1) * 4, :], kt[:, q * 4:(q + 1) * 4, :],
                                 tE[:].rearrange("p (c t) -> p c t", c=4), ALU.mult, [kt_r, tE_r], [kh_r])
                        pb, pb_r = b.bank()
                        for i in range(NT):
                            b.mm(pb[:, 0:1], la[:, i, :], onesB[:, 0:1], i == 0, i == NT - 1, [la_r, cB_r], [pb_r])
                        b.act(Td[:, 0:1], pb[:, 0:1], AF.Exp, [pb_r], [Td_r], scale=-1.0 / 16)
                        pbU, pbU_r = b.bank()
                        for i in range(NT):
                            b.mm(pbU[:, 0:256], kh[:, i, :], vb[:, i, :], i == 0, i == NT - 1, [kh_r, vb_r], [pbU_r])
                        b.stt(Sst[:, hh, :], Sst[:, hh, :], Td[:, 0:1], pbU[:, 0:256], ALU.mult, ALU.add,
                              [Sst_r, Td_r, pbU_r], [Sst_r])
                        continue
                    for q in range(4):
                        pb, pb_r = b.bank()
                        for cc in range(4):
                            i = q * 4 + cc
                            b.mm(pb[:, cc * 128:(cc + 1) * 128], LmB, la[:, i, :], True, True, [cB_r, la_r], [pb_r])
                        b.act(tE[:], pb[:], AF.Exp, [pb_r], [tE_r], scale=-1.0 / 16)
                        b.tt(kh[:, q * 4:(q + 1) * 4, :], kt[:, q * 4:(q + 1) * 4, :],
                             tE[:].rearrange("p (c t) -> p c t", c=4), ALU.mult, [kt_r, tE_r], [kh_r])
                    pb, pb_r = b.bank()
                    for i in range(NT):
                        b.mm(pb[:, i:i + 1], la[:, i, :], onesB[:, 0:1], True, True, [la_r, cB_r], [pb_r])
                    b.act(Td[:], pb[:, 0:NT], AF.Exp, [pb_r], [Td_r], scale=-1.0 / 16)
                    for i in range(NT):
                        if i % 2 == 0:
                            pbU, pbU_r = b.bank()
                        usl = slice((i % 2) * 256, (i % 2 + 1) * 256)
                        b.mm(pbU[:, usl], kh[:, i, :], vb[:, i, :], True, True, [kh_r, vb_r], [pbU_r])
                        if own:
                            b.act(Sb[:, i, :], Sst[:, hh, :], AF.Copy, [Sst_r], [Sb_r])
                        b.stt(Sst[:, hh, :], Sst[:, hh, :], Td[:, i:i + 1], pbU[:, usl], ALU.mult, ALU.add,
                              [Sst_r, Td_r, pbU_r], [Sst_r])
                    if not own:
                        continue
                    b.dma("sync", qT[:], ZT[C_GQ + hh * 128:C_GQ + (hh + 1) * 128, :], qT_r, writes=[qT_r])
                    b.dma("sync", kT[:], ZT[C_GK + hh * 128:C_GK + (hh + 1) * 128, :], kT_r, writes=[kT_r])
                    b.dma("sync", grT[:], ZT[C_GR + hh * 256:C_GR + (hh + 1) * 256, :].rearrange("(e p) t -> p e t", p=128),
                          grT_r, writes=[grT_r])
                    for q in range(4):
                        pb, pb_r = b.bank()
                        for cc in range(4):
                            i = q * 4 + cc
                            b.mm(pb[:, cc * 128:(cc + 1) * 128], la[:, i, :], triB, True, True, [la_r, cB_r], [pb_r])
                        sl = slice(q * 512, (q + 1) * 512)
                        b.act(tE[:], pb[:], AF.Exp, [pb_r], [tE_r], scale=-1.0 / 16)
                        b.stt(qh[:, sl], qT[:, sl], SC, tE[:], ALU.mult, ALU.mult, [qT_r, tE_r], [qh_r])
                        b.act(tE2[:], pb[:], AF.Exp, [pb_r], [tE2_r], scale=1.0 / 16)
                        b.tt(kth[:, sl], kT[:, sl], tE2[:], ALU.mult, [kT_r, tE2_r], [kth_r])
                    for q in range(4):
                        pb, pb_r = b.bank()
                        for cc in range(4):
                            i = q * 4 + cc
                            b.mm(pb[:, cc * 128:(cc + 1) * 128], kth[:, i * 128:(i + 1) * 128],
                                 qh[:, i * 128:(i + 1) * 128], True, True, [kth_r, qh_r], [pb_r])
                        b.tt(AT[:, q * 4:(q + 1) * 4, :], pb[:].rearrange("p (c t) -> p c t", c=4),
                             triF.unsqueeze(1).to_broadcast([128, 4, 128]), ALU.mult, [pb_r, cF_r], [AT_r])
                    for e2 in range(2):
                        for q in range(4):
                            pb, pb_r = b.bank()
                            for cc in range(4):
                                i = q * 4 + cc
                                b.mm(pb[:, cc * 128:(cc + 1) * 128], vb[:, i, e2 * 128:(e2 + 1) * 128], AT[:, i, :],
                                     True, False, [vb_r, AT_r], [pb_r])
                                b.mm(pb[:, cc * 128:(cc + 1) * 128], Sb[:, i, e2 * 128:(e2 + 1) * 128],
                                     qh[:, i * 128:(i + 1) * 128], False, True, [Sb_r, qh_r], [pb_r])
                            b.act(oF[:, e2, q * 512:(q + 1) * 512], pb[:], AF.Copy, [pb_r], [oF_r])
                    silu_to(grT[:], grT_r, grT[:], grT_r)
                    for tb in range(4):
                        sl = slice(tb * 512, (tb + 1) * 512)
                        b.tt(sqb[:], oF[:, :, sl], oF[:, :, sl], ALU.mult, [oF_r], [sqb_r])
                        pb, pb_r = b.bank()
                        b.mm(pb[:], onesB, sqb[:, 0, :], True, False, [cB_r, sqb_r], [pb_r])
                        b.mm(pb[:], onesB, sqb[:, 1, :], False, True, [cB_r, sqb_r], [pb_r])
                        b.act(rstd[:], pb[:], AF.Ln, [pb_r, epsc_r], [rstd_r], scale=1.0 / 256, bias=epsc[:, 0:1])
                        b.act(rstd[:], rstd[:], AF.Exp, [rstd_r], [rstd_r], scale=-0.5)
                        for e2 in range(2):
                            b.stt(tmpn[:], oF[:, e2, sl], gg[:, hh * 2 + e2:hh * 2 + e2 + 1], rstd[:], ALU.mult, ALU.mult,
                                  [oF_r, gg_r, rstd_r], [tmpn_r])
                            b.tt(actA[:, hh * 2 + e2, sl], tmpn[:], grT[:, e2, sl], ALU.mult, [tmpn_r, grT_r], [actA_r])
            b.dma("sync", gla_p.rearrange("h p v -> p h v"), Sst[:], Sst_r, reads=[Sst_r])
            b.end_scope(st)

        with ExitStack() as st:
            krFs = [b.sb(st, "krF", [128, 2 * T], F32) for _ in range(2)]
            krbs = [b.sb(st, "krb", [128, 2 * T], BF16) for _ in range(2)]
            qrbs = [b.sb(st, "qrb", [128, T], BF16) for _ in range(2)]
            vbss = [b.sb(st, "vbs", [128, 32, 128], BF16) for _ in range(2)]
            nd, nd_r = b.sb(st, "nd", [128, 2, T], F32)
            num, num_r, den, den_r = nd[:, 0, :], nd_r, nd[:, 1, :], nd_r
            PTs = [b.sb(st, "PT", [128, 256], BF16) for _ in range(3)]
            kst, kst_r = b.sb(st, "kst", [128, 4, 128], F32)
            srT, srT_r = b.sb(st, "srT", [128, 1024], F32)
            ssil, ssil_r = b.sb(st, "ssil", [128, 1024], F32)
            ptc = [0]
            unit = 0
            for j in range(4):
                b.memset(nd[:], 0.0, [nd_r])
                for g, (win, d) in enumerate(SWA):
                    krF, krF_r = krFs[unit % 2]
                    krb, krb_r = krbs[unit % 2]
                    qrb, qrb_r = qrbs[unit % 2]
                    vbs, vbs_r = vbss[unit % 2]
                    unit += 1
                    H = 128 * d
                    L = H + T
                    NB = 16 // d + 1
                    row = g * 512 + j * 128
                    W = min(win, T)
                    b.dma("sync", krF[:, 0:H], ZTskh[row:row + 128, T - H:T], krF_r, writes=[krF_r])
                    b.dma("scalar", krF[:, H:L], ZT[C_SK + row:C_SK + row + 128, :], krF_r, writes=[krF_r])
                    b.dma("gpsimd", qrb[:, :], ZT[C_SQ + row:C_SQ + row + 128, :], qrb_r, writes=[qrb_r])
                    if d <= NB:
                        vsrc = Zs[T - H:2 * T, row:row + 128].rearrange("(nb p dd) e -> dd p nb e", p=128, dd=d)
                        for r in range(d):
                            b.dma("gpsimd", vbs[:, r * NB:(r + 1) * NB, :], vsrc[r], vbs_r, writes=[vbs_r])
                        vblk = lambda r, nbi, NB=NB, d=d: r * NB + nbi
                    else:
                        for nbi in range(NB):
                            r0_ = T - H + nbi * 128 * d
                            b.dma("gpsimd", vbs[:, nbi * d:(nbi + 1) * d, :],
                                  Zs[r0_:r0_ + 128 * d, row:row + 128].rearrange("(p dd) e -> p dd e", dd=d),
                                  vbs_r, writes=[vbs_r])
                        vblk = lambda r, nbi, NB=NB, d=d: nbi * d + r
                    b.act(krb[:, 0:L], krF[:, 0:L], AF.Copy, [krF_r], [krb_r])
                    nt_out = W // 128
                    for q0 in range(0, nt_out, 4):
                        nq = min(4, nt_out - q0)
                        pb, pb_r = b.bank()
                        for cc in range(nq):
                            col0 = L - W + (q0 + cc) * 128
                            b.tr(pb[:, cc * 128:(cc + 1) * 128], krF[:, col0:col0 + 128], identF, [krF_r, cF_r], [pb_r])
                        b.cp(kst[:, 0:nq, :], pb[:, 0:nq * 128].rearrange("p (c t) -> p c t", c=nq), [pb_r], [kst_r])
                        b.dma("sync",
                              swa_p[g][q0 * 128:(q0 + nq) * 128, j * 128:(j + 1) * 128].rearrange("(c p) e -> p c e", p=128),
                              kst[:, 0:nq, :], kst_r, reads=[kst_r])
                    if j == 0:
                        b.dma("sync", swa_p[g][:, 512:1024], Zs[2 * T - W:2 * T, g * 512:(g + 1) * 512], dd_r)
                    qv = qrb[:, :].rearrange("p (u dd) -> p dd u", dd=d)
                    kv = krb[:, 0:L].rearrange("p (u dd) -> p dd u", dd=d)
                    ndv = nd[:, :, :].rearrange("p c (u dd) -> p c dd u", dd=d)

                    def emit_S(r, n):
                        qs = qv[:, r, n * 128:(n + 1) * 128]
                        PT, PT_r = PTs[ptc[0] % 3]
                        ptc[0] += 1
                        pb, pb_r = b.bank()
                        b.mm(pb[:, 0:128], kv[:, r, n * 128:(n + 1) * 128], qs, True, False, [krb_r, qrb_r], [pb_r])
                        b.mm(pb[:, 0:128], identB, mskP, False, True, [cB_r], [pb_r])
                        b.mm(pb[:, 128:256], kv[:, r, (n + 1) * 128:(n + 2) * 128], qs, True, False,
                             [krb_r, qrb_r], [pb_r])
                        b.mm(pb[:, 128:256], identB, mskC, False, True, [cB_r], [pb_r])
                        if n == 0:
                            b.act(PT[:, 0:128], pb[:, 0:128], AF.Exp, [pb_r, meta_r], [PT_r], scale=SC,
                                  bias=metat[:, 1:2])
                            b.act(PT[:, 128:256], pb[:, 128:256], AF.Exp, [pb_r], [PT_r], scale=SC)
                        else:
                            b.act(PT[:, 0:256], pb[:, 0:256], AF.Exp, [pb_r], [PT_r], scale=SC)
                        return PT, PT_r

                    def emit_PV(r, n, PT, PT_r):
                        pb2, pb2_r = b.bank()
                        b.mm(pb2[:, 0:128], vbs[:, vblk(r, n), :], PT[:, 0:128], True, False, [vbs_r, PT_r], [pb2_r])
                        b.mm(pb2[:, 0:128], vbs[:, vblk(r, n + 1), :], PT[:, 128:256], False, True,
                             [vbs_r, PT_r], [pb2_r])
                        b.mm(pb2[:, 128:256], onesB, PT[:, 0:128], True, False, [cB_r, PT_r], [pb2_r])
                        b.mm(pb2[:, 128:256], onesB, PT[:, 128:256], False, True, [cB_r, PT_r], [pb2_r])
                        ndsl = ndv[:, :, r, n * 128:(n + 1) * 128]
                        b.tt(ndsl, ndsl, pb2[:, 0:256].rearrange("p (c q) -> p c q", c=2), ALU.add,
                             [nd_r, pb2_r], [nd_r])

                    prevS = None
                    for r in range(d):
                        for n in range(16 // d):
                            cur = emit_S(r, n)
                            if prevS is not None:
                                emit_PV(*prevS)
                            prevS = (r, n) + cur
                    emit_PV(*prevS)
                b.act(den, den, AF.Ln, [], [den_r])
                b.act(den, den, AF.Exp, [], [den_r], scale=-1.0)
                b.tt(num, num, den, ALU.mult, [nd_r], [nd_r])
                for c0 in range(0, T, 1024):
                    b.dma("sync", srT[:], ZT[C_SR + j * 128:C_SR + (j + 1) * 128, c0:c0 + 1024], srT_r, writes=[srT_r])
                    silu_to(ssil[:], ssil_r, srT[:], srT_r)
                    b.tt(actB[:, j, c0:c0 + 1024], nd[:, 0, c0:c0 + 1024], ssil[:], ALU.mult, [nd_r, ssil_r], [actB_r])
            b.end_scope(st)

        with ExitStack() as st:
            mqb, mqb_r = b.sb(st, "mqb", [128, T], BF16)
            mrT, mrT_r = b.sb(st, "mrT", [128, T], F32)
            mo, mo_r = b.sb(st, "mo", [128, 512], F32)
            mden, mden_r = b.sb(st, "mden", [128, 512], F32)
            PM = [b.sb(st, "PM", [128, 512], BF16) for _ in range(2)]
            for h in range(4):
                b.dma("gpsimd", mqb[:], ZT[C_MQ + h * 128:C_MQ + (h + 1) * 128, :], mqb_r, writes=[mqb_r])
                b.dma("sync", mrT[:], ZT[C_MR + h * 128:C_MR + (h + 1) * 128, :], mrT_r, writes=[mrT_r])
                silu_to(mrT[:], mrT_r, mrT[:], mrT_r)
                for tb in range(4):
                    sl = slice(tb * 512, (tb + 1) * 512)
                    for t2 in range(2):
                        pb, pb_r = b.bank()
                        b.mm(pb[:], KMT[:, h, t2 * 128:(t2 + 1) * 128], mqb[:, sl], True, True, [KMT_r, mqb_r], [pb_r])
                        b.act(PM[t2][0][:], pb[:], AF.Exp, [pb_r], [PM[t2][1]], scale=SC)
                    pb, pb_r = b.bank()
                    b.mm(pb[:], VM[:, 0, h * 128:(h + 1) * 128], PM[0][0][:], True, False, [VM_r, PM[0][1]], [pb_r])
                    b.mm(pb[:], VM[:, 1, h * 128:(h + 1) * 128], PM[1][0][:], False, True, [VM_r, PM[1][1]], [pb_r])
                    pb2, pb2_r = b.bank()
                    b.mm(pb2[:], onesB, PM[0][0][:], True, False, [cB_r, PM[0][1]], [pb2_r])
                    b.mm(pb2[:], onesB, PM[1][0][:], False, True, [cB_r, PM[1][1]], [pb2_r])
                    b.act(mden[:], pb2[:], AF.Ln, [pb2_r], [mden_r])
                    b.act(mden[:], mden[:], AF.Exp, [], [mden_r], scale=-1.0)
                    b.tt(mo[:], pb[:], mden[:], ALU.mult, [pb_r, mden_r], [mo_r])
                    b.tt(actC[:, h, sl], mo[:], mrT[:, sl], ALU.mult, [mo_r, mrT_r], [actC_r])
            b.end_scope(st)

        with ExitStack() as st:
            stA = ExitStack()
            zqk, zqk_r = b.sb(stA, "zqk", [4, 3072], F32)
            zrot, zrot_r = b.sb(stA, "zrot", [4, 3072], F32)
            angs, angs_r = b.sb(stA, "angs", [4, 64], F32)
            kis, kis_r = b.sb(stA, "kis", [4, 64], I32)
            kfs, kfs_r = b.sb(stA, "kfs", [4, 64], F32)
            m1s, m1s_r = b.sb(stA, "m1s", [4, 64], F32)
            rcs, rcs_r = b.sb(stA, "rcs", [4, 64], F32)
            coss, coss_r = b.sb(stA, "coss", [4, 64], F32)
            sins, sins_r = b.sb(stA, "sins", [4, 64], F32)
            cos12, cos12_r = b.sb(stA, "cos12", [4, 24, 64], F32)
            sin12, sin12_r = b.sb(stA, "sin12", [4, 24, 64], F32)
            t1, t1_r = b.sb(stA, "t1", [4, 24, 64], F32)
            t2, t2_r = b.sb(stA, "t2", [4, 24, 64], F32)
            sself, sself_r = b.sb(stA, "sself", [4, 12], F32)
            b.dma("sync", zqk[:], ZStm[:, C_SQ:C_SQ + 3072], zqk_r, writes=[zqk_r])
            b.dma("sync", angs[:], invrow_d[0:1, :].to_broadcast([4, 64]), angs_r, writes=[angs_r])
            b.ts(angs[:], angs[:], 16384.0, ALU.mult, [angs_r], [angs_r])
            sincos(angs[:], angs_r, kis[:], kis_r, kfs[:], kfs_r, m1s[:], m1s_r, rcs[:], rcs_r,
                   coss[:], coss_r, sins[:], sins_r)
            b.cp(cos12[:], coss[:].unsqueeze(1).to_broadcast([4, 24, 64]), [coss_r], [cos12_r])
            b.cp(sin12[:], sins[:].unsqueeze(1).to_broadcast([4, 24, 64]), [sins_r], [sin12_r])
            z3 = zqk[:, :].rearrange("p (h e) -> p h e", e=128)
            r3 = zrot[:, :].rearrange("p (h e) -> p h e", e=128)
            x1, x2 = z3[:, :, 0:64], z3[:, :, 64:128]
            b.tt(t1[:], x1, cos12[:], ALU.mult, [zqk_r, cos12_r], [t1_r])
            b.tt(t2[:], x2, sin12[:], ALU.mult, [zqk_r, sin12_r], [t2_r])
            b.tt(r3[:, :, 0:64], t1[:], t2[:], ALU.subtract, [t1_r, t2_r], [zrot_r])
            b.tt(t1[:], x2, cos12[:], ALU.mult, [zqk_r, cos12_r], [t1_r])
            b.tt(t2[:], x1, sin12[:], ALU.mult, [zqk_r, sin12_r], [t2_r])
            b.tt(r3[:, :, 64:128], t1[:], t2[:], ALU.add, [t1_r, t2_r], [zrot_r])
            b.dma("sync", ZSR[:, :], zrot[:], zrot_r, reads=[zrot_r])
            prod, prod_r = b.sb(stA, "prod", [4, 1536], F32)
            b.tt(prod[:], zrot[:, 0:1536], zrot[:, 1536:3072], ALU.mult, [zrot_r], [prod_r])
            b.op("vector", lambda e: e.tensor_reduce(out=sself[:], in_=prod[:, :].rearrange("p (h e) -> p h e", e=128),
                                                     axis=AX.X, op=ALU.add), [prod_r], [sself_r])
            b.act(sself[:], sself[:], AF.Exp, [sself_r], [sself_r], scale=SC)
            b.dma("sync", ZSP[:, :], sself[:], sself_r, reads=[sself_r])
            b.end_scope(stA)
            stA.close()

            def fmload(name, c0, nch, dt=F32, q="sync"):
                t, r = b.sb(st, name, [128, nch, 4], dt)
                b.dma(q, t[:], ZSfm[c0:c0 + nch * 128, :].rearrange("(c p) bb -> p c bb", p=128), r, writes=[r])
                return t, r

            gqTs, gqTs_r = fmload("gqTs", C_GQ, 4)
            gkTs, gkTs_r = fmload("gkTs", C_GK, 4)
            grTs, grTs_r = fmload("grTs", C_GR, 8)
            svTs, svTs_r = fmload("svTs", C_SV, 12)
            srTs, srTs_r = fmload("srTs", C_SR, 4)
            mrTs, mrTs_r = fmload("mrTs", C_MR, 4)
            gaTs, gaTs_r = b.sb(st, "gaTs", [32, 4], BF16)
            b.memset(gaTs[:], 1.0, [gaTs_r])
            b.dma("gpsimd", gaTs[0:16, :], ZSfm[C_GA:C_GA + 16, :], gaTs_r, writes=[gaTs_r])
            dec, dec_r = b.sb(st, "dec", [128, 16], F32)
            pb, pb_r = b.bank()
            for h in range(4):
                b.mm(pb[:, h * 4:(h + 1) * 4], wa2[0:17, h * 128:(h + 1) * 128], gaTs[0:17, :], True, True,
                     [wa2_r, gaTs_r], [pb_r])
            b.act(dec[:], pb[:, 0:16], AF.Exp, [pb_r], [dec_r], scale=-1.0)
            b.act(dec[:], dec[:], AF.Ln, [dec_r, epsc_r], [dec_r], bias=one_col)
            b.act(dec[:], dec[:], AF.Exp, [dec_r], [dec_r], scale=-1.0 / 16)

            S0_l = [b.sb(st, "S0", [128, 4, 256], F32) for _ in range(2)]
            Sn_l = [b.sb(st, "Sn", [128, 4, 256], F32) for _ in range(2)]
            vbc_l = [b.sb(st, "vbc", [128, 1024], F32) for _ in range(2)]
            tmpv_l = [b.sb(st, "tmpv", [128, 256], F32) for _ in range(2)]
            os8_l = [b.sb(st, "os8", [128, 8], F32) for _ in range(2)]
            sq8_l = [b.sb(st, "sq8", [128, 8], F32) for _ in range(2)]
            ss4_l = [b.sb(st, "ss4", [128, 4], F32) for _ in range(2)]
            gs8_l = [b.sb(st, "gs8", [128, 8], F32) for _ in range(2)]
            gt8_l = [b.sb(st, "gt8", [128, 8], F32) for _ in range(2)]
            gx8_l = [b.sb(st, "gx8", [128, 8], F32) for _ in range(2)]
            qbc_l = [b.sb(st, "qbc", [128, 1536], F32) for _ in range(2)]
            mqbc_l = [b.sb(st, "mqbc", [128, 512], F32) for _ in range(2)]
            psbc_l = [b.sb(st, "psbc", [128, 12], F32) for _ in range(2)]
            Kc_l = [b.sb(st, "Kc", [128, 3, 512], F32) for _ in range(2)]
            Vc_l = [b.sb(st, "Vc", [128, 3, 512], F32) for _ in range(2)]
            Kcm_l = [b.sb(st, "Kcm", [128, 2, 512], F32) for _ in range(2)]
            Vcm_l = [b.sb(st, "Vcm", [128, 2, 512], F32) for _ in range(2)]
            prd_l = [b.sb(st, "prd", [128, 512], F32) for _ in range(2)]
            sc12_l = [b.sb(st, "sc12", [128, 12], F32) for _ in range(2)]
            sm8_l = [b.sb(st, "sm8", [128, 8], F32) for _ in range(2)]
            n4_l = [b.sb(st, "n4", [128, 4], F32) for _ in range(2)]
            d4_l = [b.sb(st, "d4", [128, 4], F32) for _ in range(2)]
            u4_l = [b.sb(st, "u4", [128, 4], F32) for _ in range(2)]
            for bb in range(4):
                S0, S0_r = S0_l[bb % 2]
                Sn, Sn_r = Sn_l[bb % 2]
                vbc, vbc_r = vbc_l[bb % 2]
                tmpv, tmpv_r = tmpv_l[bb % 2]
                os8, os8_r = os8_l[bb % 2]
                sq8, sq8_r = sq8_l[bb % 2]
                ss4, ss4_r = ss4_l[bb % 2]
                gs8, gs8_r = gs8_l[bb % 2]
                gt8, gt8_r = gt8_l[bb % 2]
                gx8, gx8_r = gx8_l[bb % 2]
                qbc, qbc_r = qbc_l[bb % 2]
                mqbc, mqbc_r = mqbc_l[bb % 2]
                psbc, psbc_r = psbc_l[bb % 2]
                Kc, Kc_r = Kc_l[bb % 2]
                Vc, Vc_r = Vc_l[bb % 2]
                Kcm, Kcm_r = Kcm_l[bb % 2]
                Vcm, Vcm_r = Vcm_l[bb % 2]
                prd, prd_r = prd_l[bb % 2]
                sc12, sc12_r = sc12_l[bb % 2]
                sm8, sm8_r = sm8_l[bb % 2]
                n4, n4_r = n4_l[bb % 2]
                d4, d4_r = d4_l[bb % 2]
                u4, u4_r = u4_l[bb % 2]
                b.dma("sync", S0[:], sgla[bb].rearrange("h p v -> p h v"), S0_r, writes=[S0_r])
                b.dma("sync", vbc[:], ZStm[bb:bb + 1, C_GV:C_GV + 1024].to_broadcast([128, 1024]), vbc_r, writes=[vbc_r])
                for h in range(4):
                    b.ts(tmpv[:], vbc[:, h * 256:(h + 1) * 256], gkTs[:, h, bb:bb + 1], ALU.mult, [vbc_r, gkTs_r], [tmpv_r])
                    b.stt(Sn[:, h, :], S0[:, h, :], dec[:, h * 4 + bb:h * 4 + bb + 1], tmpv[:], ALU.mult, ALU.add,
                          [S0_r, dec_r, tmpv_r], [Sn_r])
                b.dma("sync", gla_s[bb].rearrange("h p v -> p h v"), Sn[:], Sn_r, reads=[Sn_r])
                pb, pb_r = b.bank()
                for h in range(4):
                    for e2 in range(2):
                        b.mm(pb[:, h * 2 + e2:h * 2 + e2 + 1], Sn[:, h, e2 * 128:(e2 + 1) * 128], gqTs[:, h, bb:bb + 1],
                             True, True, [Sn_r, gqTs_r], [pb_r])
                b.ts(os8[:], pb[:, 0:8], SC, ALU.mult, [pb_r], [os8_r])
                b.tt(sq8[:], os8[:], os8[:], ALU.mult, [os8_r], [sq8_r])
                pb2, pb2_r = b.bank()
                b.mm(pb2[:, 0:8], onesF, sq8[:], True, True, [cF_r, sq8_r], [pb2_r])
                b.op("vector", lambda e: e.tensor_reduce(out=ss4[:], in_=pb2[:, 0:8].rearrange("p (h e) -> p h e", e=2),
                                                         axis=AX.X, op=ALU.add), [pb2_r], [ss4_r])
                b.act(ss4[:], ss4[:], AF.Ln, [ss4_r, epsc_r], [ss4_r], scale=1.0 / 256, bias=epsc[:, 0:1])
                b.act(ss4[:], ss4[:], AF.Exp, [ss4_r], [ss4_r], scale=-0.5)
                b.cp(gx8[:], grTs[:, :, bb], [grTs_r], [gx8_r])
                silu_to(gs8[:], gs8_r, gx8[:], gx8_r, gt8[:], gt8_r)
                b.tt(os8[:], os8[:], gg[:, 0:8], ALU.mult, [os8_r, gg_r], [os8_r])
                b.tt(os8[:, :].rearrange("p (h e) -> p h e", e=2), os8[:, :].rearrange("p (h e) -> p h e", e=2),
                     ss4[:].unsqueeze(2).to_broadcast([128, 4, 2]), ALU.mult, [os8_r, ss4_r], [os8_r])
                b.tt(actA[:, :, T + bb], os8[:], gs8[:], ALU.mult, [os8_r, gs8_r], [actA_r])
                b.dma("sync", qbc[:], ZSR[bb:bb + 1, 0:1536].to_broadcast([128, 1536]), qbc_r, writes=[qbc_r])
                b.dma("sync", mqbc[:], ZStm[bb:bb + 1, C_MQ:C_MQ + 512].to_broadcast([128, 512]), mqbc_r, writes=[mqbc_r])
                b.dma("sync", psbc[:], ZSP[bb:bb + 1, :].to_broadcast([128, 12]), psbc_r, writes=[psbc_r])
                for g, (win, d) in enumerate(SWA):
                    csrc = cch[g][bb].rearrange("(u dd) e -> dd u e", dd=d)[0]
                    b.dma("sync" if g != 1 else "scalar", Kc[:, g, :], csrc[:, 0:512], Kc_r, writes=[Kc_r])
                    b.dma("sync", Vc[:, g, :], csrc[:, 512:1024], Vc_r, writes=[Vc_r])
                    b.tt(prd[:], Kc[:, g, :], qbc[:, g * 512:(g + 1) * 512], ALU.mult, [Kc_r, qbc_r], [prd_r])
                    b.op("vector", lambda e: e.tensor_reduce(out=sc12[:, g * 4:(g + 1) * 4],
                                                             in_=prd[:, :].rearrange("p (h e) -> p h e", e=128),
                                                             axis=AX.X, op=ALU.add), [prd_r], [sc12_r])
                b.act(sc12[:], sc12[:], AF.Exp, [sc12_r], [sc12_r], scale=SC)
                pb, pb_r = b.bank()
                for jj in range(4):
                    for g in range(3):
                        b.mm(pb[:, jj:jj + 1], Vc[:, g, jj * 128:(jj + 1) * 128], sc12[:, g * 4 + jj:g * 4 + jj + 1],
                             g == 0, g == 2, [Vc_r, sc12_r], [pb_r])
                pb2, pb2_r = b.bank()
                b.mm(pb2[:, 0:12], onesF, sc12[:], True, True, [cF_r, sc12_r], [pb2_r])
                b.cp(n4[:], pb[:, 0:4], [pb_r], [n4_r])
                b.cp(d4[:], pb2[:, 0:4], [pb2_r], [d4_r])
                for g in range(3):
                    b.tt(u4[:], psbc[:, g * 4:(g + 1) * 4], svTs[:, g * 4:(g + 1) * 4, bb], ALU.mult,
                         [psbc_r, svTs_r], [u4_r])
                    b.tt(n4[:], n4[:], u4[:], ALU.add, [n4_r, u4_r], [n4_r])
                    b.tt(d4[:], d4[:], psbc[:, g * 4:(g + 1) * 4], ALU.add, [d4_r, psbc_r], [d4_r])
                    if g > 0:
                        b.tt(d4[:], d4[:], pb2[:, g * 4:(g + 1) * 4], ALU.add, [d4_r, pb2_r], [d4_r])
                b.recip(d4[:], d4[:], [d4_r], [d4_r])
                b.tt(n4[:], n4[:], d4[:], ALU.mult, [n4_r, d4_r], [n4_r])
                b.cp(gx8[:, 0:4], srTs[:, :, bb], [srTs_r], [gx8_r])
                silu_to(gs8[:, 0:4], gs8_r, gx8[:, 0:4], gx8_r, gt8[:, 0:4], gt8_r)
                b.tt(actB[:, :, T + bb], n4[:], gs8[:, 0:4], ALU.mult, [n4_r, gs8_r], [actB_r])
                msrc = cmem[bb].rearrange("(t p) e -> p t e", p=128)
                b.dma("sync", Kcm[:], msrc[:, :, 0:512], Kcm_r, writes=[Kcm_r])
                b.dma("sync", Vcm[:], msrc[:, :, 512:1024], Vcm_r, writes=[Vcm_r])
                for t2i in range(2):
                    b.tt(prd[:], Kcm[:, t2i, :], mqbc[:], ALU.mult, [Kcm_r, mqbc_r], [prd_r])
                    b.op("vector", lambda e: e.tensor_reduce(out=sm8[:, t2i * 4:(t2i + 1) * 4],
                                                             in_=prd[:, :].rearrange("p (h e) -> p h e", e=128),
                                                             axis=AX.X, op=ALU.add), [prd_r], [sm8_r])
                b.act(sm8[:], sm8[:], AF.Exp, [sm8_r], [sm8_r], scale=SC)
                pb, pb_r = b.bank()
                for jj in range(4):
                    for t2i in range(2):
                        b.mm(pb[:, jj:jj + 1], Vcm[:, t2i, jj * 128:(jj + 1) * 128], sm8[:, t2i * 4 + jj:t2i * 4 + jj + 1],
                             t2i == 0, t2i == 1, [Vcm_r, sm8_r], [pb_r])
                pb2, pb2_r = b.bank()
                b.mm(pb2[:, 0:8], onesF, sm8[:], True, True, [cF_r, sm8_r], [pb2_r])
                b.cp(d4[:], pb2[:, 0:4], [pb2_r], [d4_r])
                b.tt(d4[:], d4[:], pb2[:, 4:8], ALU.add, [d4_r, pb2_r], [d4_r])
                b.recip(d4[:], d4[:], [d4_r], [d4_r])
                b.tt(n4[:], pb[:, 0:4], d4[:], ALU.mult, [pb_r, d4_r], [n4_r])
                b.cp(gx8[:, 0:4], mrTs[:, :, bb], [mrTs_r], [gx8_r])
                silu_to(gs8[:, 0:4], gs8_r, gx8[:, 0:4], gx8_r, gt8[:, 0:4], gt8_r)
                b.tt(actC[:, :, T + bb], n4[:], gs8[:, 0:4], ALU.mult, [n4_r, gs8_r], [actC_r])
                for g, (win, d) in enumerate(SWA):
                    b.dma("sync", swa_s[g][bb, win - 1:win, 0:512], ZSR[bb:bb + 1, 1536 + g * 512:1536 + (g + 1) * 512], dd_r)
                    b.dma("sync", swa_s[g][bb, win - 1:win, 512:1024],
                          ZStm[bb:bb + 1, C_SV + g * 512:C_SV + (g + 1) * 512], dd_r)
            b.end_scope(st)

        with ExitStack() as st:
            mTf, mTf_r = b.sb(st, "mTf", [128, NCH, T + 4], BF16)
            acts = [(actA, actA_r, 8, wpa_d, 0), (actB, actB_r, 4, wpb_d, 8), (actC, actC_r, 4, wpc_d, 12)]
            with ExitStack() as st1:
                wpfs = [b.sb(st1, "wpf", [128, 16, 128], BF16) for _ in range(2)]
                gtts = [b.sb(st1, "gtt", [128, 3, 1024], F32) for _ in range(3)]
                mgs = [b.sb(st1, "mg", [128, 512], F32) for _ in range(2)]
                mg2s = [b.sb(st1, "mg2", [128, 512], F32) for _ in range(2)]
                fc = 0
                gsrc = ZT[C_GT:C_GT + 6144, :].rearrange("(br f p) t -> f p br t", br=3, f=16, p=128)
                gsrc_s = ZSfm[C_GT:C_GT + 6144, :].rearrange("(br f p) t -> f p br t", br=3, f=16, p=128)
                for f in range(16):
                    wpf, wpf_r = wpfs[f % 2]
                    for (a_t, a_r, nk, wd, ko) in acts:
                        b.dma("gpsimd", wpf[:, ko:ko + nk, :],
                              wd[:, f * 128:(f + 1) * 128].rearrange("(c p) w -> p c w", p=128), wpf_r, writes=[wpf_r])
                    for tb2 in range(3):
                        NN = 1024 if tb2 < 2 else 4
                        t0 = tb2 * 1024 if tb2 < 2 else T
                        gtt, gtt_r = gtts[fc % 3]
                        fc += 1
                        src_ = gsrc[f][:, :, t0:t0 + NN] if tb2 < 2 else gsrc_s[f]
                        b.dma("sync" if fc % 2 == 0 else "scalar", gtt[:, :, 0:NN], src_, gtt_r, writes=[gtt_r])
                        b.act(gtt[:, :, 0:NN], gtt[:, :, 0:NN], AF.Sigmoid, [], [gtt_r])
                        for hb in range(2 if tb2 < 2 else 1):
                            N = 512 if tb2 < 2 else 4
                            tok0 = t0 + hb * 512
                            gs = slice(hb * 512, hb * 512 + N)
                            mg, mg_r = mgs[hb]
                            mg2, mg2_r = mg2s[hb]
                            for br, (a_t, a_r, nk, wd, ko) in enumerate(acts):
                                pb, pb_r = b.bank()
                                for c in range(nk):
                                    b.mm(pb[:, 0:N], wpf[:, ko + c, :], a_t[:, c, tok0:tok0 + N], c == 0, c == nk - 1,
                                         [wpf_r, a_r], [pb_r])
                                if br == 0:
                                    b.tt(mg[:, 0:N], pb[:, 0:N], gtt[:, 0, gs], ALU.mult, [pb_r, gtt_r], [mg_r])
                                else:
                                    b.tt(mg2[:, 0:N], pb[:, 0:N], gtt[:, br, gs], ALU.mult, [pb_r, gtt_r], [mg2_r])
                                    if br == 1:
                                        b.tt(mg[:, 0:N], mg[:, 0:N], mg2[:, 0:N], ALU.add, [mg_r, mg2_r], [mg_r])
                                    else:
                                        b.tt(mTf[:, f, tok0:tok0 + N], mg[:, 0:N], mg2[:, 0:N], ALU.add,
                                             [mg_r, mg2_r], [mTf_r])
                b.end_scope(st1)
            ssqa, ssqa_r = b.sb(st, "ssqa", [128, 17, 4], F32)
            with ExitStack() as st2:
                wobs = [b.sb(st2, "wob", [128, NCH, 512], BF16) for _ in range(2)]
                xcs = [b.sb(st2, "xc", [128, 4, 512], F32) for _ in range(3)]
                junk, junk_r = b.sb(st2, "junk", [128, 512], F32)
                b.memset(ssqa[:], 0.0, [ssqa_r])
                xcnt = 0
                for cb in range(4):
                    wob, wob_r = wobs[cb % 2]
                    wload(wob, wob_r, wo_d[:, cb * 512:(cb + 1) * 512], D, 512)
                    csl = slice(cb * 512, (cb + 1) * 512)
                    for tg in range(5):
                        ntl = 4 if tg < 4 else 1
                        M = 128 if tg < 4 else 4
                        xc, xc_r = xcs[xcnt % 3]
                        xcnt += 1
                        if tg < 4:
                            src_ = xs[3 * T + tg * 512:3 * T + (tg + 1) * 512, csl].rearrange("(i p) c -> p i c", p=128)
                            b.dma("sync", xc[:, :, :], src_, xc_r, writes=[xc_r])
                        else:
                            b.dma("sync", xc[0:4, 0, :], xsmp[:, csl], xc_r, writes=[xc_r])
                        for il in range(ntl):
                            ti = tg * 4 + il
                            tok0 = ti * 128 if tg < 4 else T
                            pb, pb_r = b.bank()
                            for c in range(NCH):
                                b.mm(pb[0:M, :], mTf[:, c, tok0:tok0 + M], wob[:, c, :], c == 0, c == NCH - 1,
                                     [mTf_r, wob_r], [pb_r])
                            b.tt(xc[0:M, il, :], xc[0:M, il, :], pb[0:M, :], ALU.add, [xc_r, pb_r], [xc_r])
                            b.act(junk[0:M, :], xc[0:M, il, :], AF.Square, [xc_r], [junk_r, ssqa_r],
                                  accum_out=ssqa[0:M, ti, cb:cb + 1])
                        if tg < 4:
                            dst = XN[tg * 512:(tg + 1) * 512, csl].rearrange("(i p) c -> p i c", p=128)
                            b.dma("scalar", dst, xc[:, :, :], xc_r, reads=[xc_r])
                        else:
                            b.dma("scalar", XNs[:, csl], xc[0:4, 0, :], xc_r, reads=[xc_r])
                b.end_scope(st2)
            with ExitStack() as st3:
                gfb, gfb_r = b.sb(st3, "gfb", [128, D], F32)
                b.dma("sync", gfb[:], gfin[0:1, :].to_broadcast([128, D]), gfb_r, writes=[gfb_r])
                xts = [b.sb(st3, "xts", [128, D], F32) for _ in range(4)]
                rst, rst_r = b.sb(st3, "rst", [128, 17], F32)
                b.op("vector", lambda e: e.tensor_reduce(out=rst[:], in_=ssqa[:], axis=AX.X, op=ALU.add), [ssqa_r], [rst_r])
                b.act(rst[:], rst[:], AF.Ln, [rst_r, epsc_r], [rst_r], scale=1.0 / D, bias=epsc[:, 0:1])
                b.act(rst[:], rst[:], AF.Exp, [rst_r], [rst_r], scale=-0.5)
                for ti in range(17):
                    M = 128 if ti < 16 else 4
                    tok0 = ti * 128 if ti < 16 else T
                    xt_, xt_r_ = xts[ti % 4]
                    src_ = XN[tok0:tok0 + M, :] if ti < 16 else XNs[:, :]
                    b.dma("sync" if ti % 2 == 0 else "gpsimd", xt_[0:M, :], src_, xt_r_, writes=[xt_r_])
                    b.stt(xt_[0:M, :], xt_[0:M, :], rst[0:M, ti:ti + 1], gfb[0:M, :], ALU.mult, ALU.mult,
                          [xt_r_, rst_r, gfb_r], [xt_r_])
                    dst = y_own[tok0:tok0 + M, :] if ti < 16 else y_smp[:, :]
                    b.dma("scalar" if ti % 2 == 0 else "sync", dst, xt_[0:M, :], xt_r_, reads=[xt_r_])
                b.end_scope(st3)
            b.end_scope(st)

        b.barrier()
        b.barrier()
    return nc


_NC = None
DEBUG = {}


def _prep_inputs(x_prompt, x_sample, mem_prompt, state_gla, cache_swa_w128, cache_swa_w512, cache_swa_w2048,
                 cache_mem_kv, g_norm, w_in, w_alpha2, b_alpha, g_gla_out, g_mem, w_mem_kv, w_proj_a, w_proj_b,
                 w_proj_c, w_out, g_final):
    f = lambda a: np.ascontiguousarray(np.asarray(a, dtype=np.float32))
    x_prompt = f(x_prompt)
    shared = {
        "w_in": f(w_in[0]),
        "wa2a": f(np.concatenate([np.asarray(w_alpha2[0]), np.asarray(b_alpha[0])[None, :]], axis=0)),
        "gnt": f(np.asarray(g_norm[0]).reshape(NCH, 128).T),
        "gmt": f(np.asarray(g_mem[0]).reshape(NCH, 128).T),
        "ggt": f(np.asarray(g_gla_out[0]).reshape(8, 128).T),
        "gfin": f(np.asarray(g_final).reshape(1, D)),
        "wmkv": f(w_mem_kv[0]),
        "wpa": f(w_proj_a[0]),
        "wpb": f(w_proj_b[0]),
        "wpc": f(w_proj_c[0]),
        "wo": f(w_out[0]),
    }
    maps = []
    for c in range(8):
        bq, j = c // 4, c % 4
        xs = np.zeros((4, T, D), np.float32)
        for s in range(4):
            sh = j - 3 + s
            if sh >= 0:
                xs[s] = x_prompt[bq, sh * T:(sh + 1) * T]
        meta = np.zeros((128, 4), np.float32)
        meta[:, 0] = float(j * T - T)
        meta[:, 1] = 0.0 if j > 0 else NEGM
        m = dict(shared)
        m.update({
            "xs": xs.reshape(4 * T, D),
            "xsmp": f(np.asarray(x_sample)[4 * c:4 * c + 4, 0]),
            "mem": f(np.asarray(mem_prompt)[bq]),
            "sgla": f(np.asarray(state_gla)[0, 4 * c:4 * c + 4]),
            "c128": f(np.asarray(cache_swa_w128)[0, 4 * c:4 * c + 4]).reshape(4, 128, 1024),
            "c512": f(np.asarray(cache_swa_w512)[0, 4 * c:4 * c + 4]).reshape(4, 512, 1024),
            "c2048": f(np.asarray(cache_swa_w2048)[0, 4 * c:4 * c + 4]).reshape(4, 2048, 1024),
            "cmem": f(np.asarray(cache_mem_kv)[0, 4 * c:4 * c + 4]).reshape(4, 256, 1024),
            "meta": meta,
        })
        maps.append(m)
    return maps


def _assemble(res):
    R = res.results
    y_prompt = np.zeros((2, 4 * T, D), np.float32)
    for c in range(8):
        y_prompt[c // 4, (c % 4) * T:(c % 4 + 1) * T] = R[c]["y_own"]
    y_sample = np.concatenate([R[c]["y_smp"] for c in range(8)], axis=0).reshape(32, 1, D)
    last = [3, 7]
    gla_prompt = np.stack([R[c]["gla_p"] for c in last])[None]
    swa_p = []
    for nm, w in (("swa128_p", 128), ("swa512_p", 512), ("swa2048_p", 2048)):
        swa_p.append(np.stack([R[c][nm].reshape(w, 2, 4, 128) for c in last])[None])
    mem_kv = np.stack([R[c]["memkv_p"].reshape(256, 2, 4, 128) for c in (0, 4)])[None]
    gla_sample = np.concatenate([R[c]["gla_s"] for c in range(8)], axis=0)[None]
    swa_s = []
    for nm, w in (("s128_s", 128), ("s512_s", 512), ("s2048_s", 2048)):
        swa_s.append(np.concatenate([R[c][nm].reshape(4, w, 2, 4, 128) for c in range(8)], axis=0)[None])
    return (y_prompt, y_sample, gla_prompt, swa_p[0], swa_p[1], swa_p[2], mem_kv, gla_sample,
            swa_s[0], swa_s[1], swa_s[2])


def kernel(**inputs):
    global _NC
    maps = _prep_inputs(**inputs)
    nc = build_program()
    res = run_bass_kernel_spmd(nc, maps, core_ids=list(range(8)))
    DEBUG["res"] = res
    outs = _assemble(res)
    return tuple(np.ascontiguousarray(o.astype(np.float32)) for o in outs)
```

```python
import numpy as np
from contextlib import ExitStack
import concourse.bass as bass
import concourse.mybir as mybir
from concourse.bass_utils import run_bass_kernel_spmd

F32 = mybir.dt.float32
BF16 = mybir.dt.bfloat16
I32 = mybir.dt.int32
AF = mybir.ActivationFunctionType
ALU = mybir.AluOpType
AX = mybir.AxisListType

D = 2048
NCH = 16
T = 2048
NT = 16
INW = 15376
C_GQ, C_GK, C_GV, C_GR, C_GA = 0, 512, 1024, 2048, 3072
C_SQ, C_SK, C_SV, C_SR, C_MQ, C_MR, C_GT = 3088, 4624, 6160, 7696, 8208, 8720, 9232
EPS = 1e-6
NEGM = -30000.0
SC = 128.0 ** -0.5
TWO_PI = 6.283185307179586
SWA = ((128, 1), (512, 4), (2048, 16))


class Reg:
    __slots__ = ("name", "last_w", "readers", "dsem", "dcount", "ssem", "scount", "excl")

    def __init__(self, name):
        self.name = name
        self.last_w = None
        self.readers = {}
        self.dsem = None
        self.dcount = 0
        self.ssem = None
        self.scount = 0
        self.excl = False


class EngS:
    def __init__(self, name, eng, sem):
        self.name = name
        self.eng = eng
        self.sem = sem
        self.count = 0
        self.waited = {}


class B:
    def __init__(self, nc, es):
        self.nc = nc
        self.es = es
        self.E = {}
        for n in ["tensor", "vector", "scalar", "gpsimd", "sync"]:
            sem = es.enter_context(nc.semaphore("s_" + n))
            self.E[n] = EngS(n, getattr(nc, n), sem)
        self.regs = []
        self.nreg = 0
        self.banks = []
        self.bank_i = 0
        self.sem_pool = []
        self.sw_pool = []
        self.scope_regs = {}

    def reg(self, name):
        self.nreg += 1
        r = Reg("%s_%d" % (name, self.nreg))
        self.regs.append(r)
        return r

    def sb(self, es, name, shape, dt):
        self.nreg += 1
        t = es.enter_context(self.nc.sbuf_tensor("%s_%d" % (name, self.nreg), shape, dt))
        r = self.reg(name)
        self.scope_regs.setdefault(id(es), []).append(r)
        return t, r

    def end_scope(self, es):
        self.barrier()
        for r in self.scope_regs.pop(id(es), []):
            if r.dsem is not None:
                self.sem_pool.append((r.dsem, r.dcount))
                r.dsem = None
            if r.ssem is not None:
                self.sw_pool.append((r.ssem, r.scount))
                r.ssem = None
            self.regs.remove(r)

    def _need(self, e, toks):
        best = {}
        for t in toks:
            if t is None:
                continue
            sem, val = t
            k = id(sem)
            if k not in best or val > best[k][1]:
                best[k] = (sem, val)
        for k, (sem, val) in best.items():
            if val > e.waited.get(k, 0):
                e.eng.wait_ge(sem, val)
                e.waited[k] = val

    def _collect(self, reads, writes):
        toks = []
        for r in reads:
            toks.append(r.last_w)
            if r.excl:
                toks.extend(r.readers.values())
        for w in writes:
            toks.append(w.last_w)
            toks.extend(w.readers.values())
        return toks

    def _update(self, tok, reads, writes):
        k = id(tok[0])
        for r in reads:
            o = r.readers.get(k)
            if o is None or o[1] < tok[1]:
                r.readers[k] = tok
        for w in writes:
            w.last_w = tok
            w.readers = {}

    def op(self, en, fn, reads=(), writes=(), signal=True):
        e = self.E[en]
        toks = self._collect(reads, writes)
        if en == "tensor":
            toks = [t for t in toks if t is not None and t[0] is not e.sem]
        self._need(e, toks)
        inst = fn(e.eng)
        if signal:
            inst.then_inc(e.sem, 1)
            e.count += 1
            tok = (e.sem, e.count)
        else:
            tok = (e.sem, e.count + 1)
        self._update(tok, reads, writes)

    def dma(self, qn, out, in_, sbr, reads=(), writes=()):
        q = self.E[qn]
        sw = qn == "gpsimd"
        if sw and sbr.ssem is None:
            if self.sw_pool:
                sbr.ssem, sbr.scount = self.sw_pool.pop()
            else:
                sbr.ssem = self.es.enter_context(self.nc.semaphore("s_" + sbr.name))
        if not sw and sbr.dsem is None:
            if self.sem_pool:
                sbr.dsem, sbr.dcount = self.sem_pool.pop()
            else:
                sbr.dsem = self.es.enter_context(self.nc.semaphore("d_" + sbr.name))
        toks = self._collect(reads, writes)
        self._need(q, toks)
        if sw:
            q.eng.dma_start(out=out, in_=in_).then_inc(sbr.ssem, 16)
            sbr.scount += 16
            tok = (sbr.ssem, sbr.scount)
        else:
            q.eng.dma_start(out=out, in_=in_).then_inc(sbr.dsem, 16)
            sbr.dcount += 16
            tok = (sbr.dsem, sbr.dcount)
        self._update(tok, reads, writes)

    def barrier(self):
        for e in self.E.values():
            toks = [(f.sem, f.count) for f in self.E.values() if f is not e and f.count > 0]
            toks += [(r.dsem, r.dcount) for r in self.regs if r.dsem is not None and r.dcount > 0]
            toks += [(r.ssem, r.scount) for r in self.regs if r.ssem is not None and r.scount > 0]
            self._need(e, toks)

    def bank(self):
        t, r = self.banks[self.bank_i % len(self.banks)]
        self.bank_i += 1
        return t, r

    def mm(self, out, lhsT, rhs, start, stop, reads, writes):
        self.op("tensor", lambda e: e.matmul(out, lhsT=lhsT, rhs=rhs, start=start, stop=stop),
                reads, writes, signal=stop)

    def tr(self, out, in_, ident, reads, writes):
        self.op("tensor", lambda e: e.transpose(out, in_, ident), reads, writes)

    def act(self, out, in_, func, reads, writes, scale=1.0, bias=0.0, accum_out=None, en="scalar"):
        kw = {}
        if accum_out is not None:
            kw["accum_out"] = accum_out
        self.op("scalar", lambda e: e.activation(out=out, in_=in_, func=func, bias=bias, scale=scale, **kw),
                reads, writes)

    def tt(self, out, in0, in1, op, reads, writes, en="vector"):
        self.op(en, lambda e: e.tensor_tensor(out=out, in0=in0, in1=in1, op=op), reads, writes)

    def ts(self, out, in0, s1, op0, reads, writes, s2=None, op1=None, en="vector"):
        if op1 is None:
            self.op(en, lambda e: e.tensor_scalar(out=out, in0=in0, scalar1=s1, scalar2=None, op0=op0), reads, writes)
        else:
            self.op(en, lambda e: e.tensor_scalar(out=out, in0=in0, scalar1=s1, scalar2=s2, op0=op0, op1=op1),
                    reads, writes)

    def stt(self, out, in0, scalar, in1, op0, op1, reads, writes, en="vector"):
        self.op(en, lambda e: e.scalar_tensor_tensor(out=out, in0=in0, scalar=scalar, in1=in1, op0=op0, op1=op1),
                reads, writes)

    def cp(self, out, in_, reads, writes, en="vector"):
        self.op(en, lambda e: e.tensor_copy(out=out, in_=in_), reads, writes)

    def recip(self, out, in_, reads, writes):
        self.op("vector", lambda e: e.reciprocal(out=out, in_=in_), reads, writes)

    def memset(self, ap, val, writes, en="vector"):
        self.op(en, lambda e: e.memset(ap, val), (), writes)


def build_program():
    nc = bass.Bass("TRN2", target_bir_lowering=False)

    def din(name, shape):
        return nc.dram_tensor(name, list(shape), F32, kind="ExternalInput").ap()

    def dout(name, shape):
        return nc.dram_tensor(name, list(shape), F32, kind="ExternalOutput").ap()

    def dscr(name, shape, dt=F32):
        return nc.dram_tensor(name, list(shape), dt).ap()

    xs = din("xs", [4 * T, D])
    xsmp = din("xsmp", [4, D])
    mem = din("mem", [256, D])
    w_in = din("w_in", [D, INW])
    wa2a = din("wa2a", [17, 512])
    gnt = din("gnt", [128, NCH])
    gmt = din("gmt", [128, NCH])
    ggt = din("ggt", [128, 8])
    gfin = din("gfin", [1, D])
    wmkv = din("wmkv", [D, 1024])
    wpa_d = din("wpa", [1024, D])
    wpb_d = din("wpb", [512, D])
    wpc_d = din("wpc", [512, D])
    wo_d = din("wo", [D, D])
    sgla = din("sgla", [4, 4, 128, 256])
    cch = [din("c128", [4, 128, 1024]), din("c512", [4, 512, 1024]), din("c2048", [4, 2048, 1024])]
    cmem = din("cmem", [4, 256, 1024])
    meta = din("meta", [128, 4])

    y_own = dout("y_own", [T, D])
    y_smp = dout("y_smp", [4, D])
    gla_p = dout("gla_p", [4, 128, 256])
    swa_p = [dout("swa128_p", [128, 1024]), dout("swa512_p", [512, 1024]), dout("swa2048_p", [2048, 1024])]
    memkv_p = dout("memkv_p", [256, 1024])
    gla_s = dout("gla_s", [4, 4, 128, 256])
    swa_s = [dout("s128_s", [4, 128, 1024]), dout("s512_s", [4, 512, 1024]), dout("s2048_s", [4, 2048, 1024])]

    Zg = dscr("Zg", [4 * T, 1536])
    Zs = dscr("Zs", [2 * T, 1536])
    ZT = dscr("ZT", [INW, T])
    ZTga = dscr("ZTga", [4, 16, T])
    ZTskh = dscr("ZTskh", [1536, T])
    ZStm = dscr("ZStm", [4, INW])
    ZSfm = dscr("ZSfm", [INW, 4])
    ZSR = dscr("ZSR", [4, 3072])
    ZSP = dscr("ZSP", [4, 12])
    XN = dscr("XN", [T, D])
    XNs = dscr("XNs", [4, D])

    pp = np.arange(128)
    tri_np = (pp[:, None] <= pp[None, :]).astype(np.float32)
    consts_np = np.zeros((128, 7, 128), np.float32)
    consts_np[:, 0] = np.eye(128, dtype=np.float32)
    consts_np[:, 1] = tri_np
    consts_np[:, 2] = 1.0 - tri_np
    consts_np[:, 3] = NEGM * (1.0 - tri_np.T)
    consts_np[:, 4] = NEGM * (1.0 - tri_np)
    consts_np[:, 5] = 1.0
    consts_np[:, 6] = np.roll(np.eye(128, dtype=np.float32), 64, axis=0)
    consts_d = nc.inline_tensor(consts_np, name="consts").ap()
    half = 64
    inv_np = (np.float32(10000.0) ** (-(np.arange(half, dtype=np.float32) / np.float32(half)))).astype(np.float32)
    col_np = np.zeros((128, 2), np.float32)
    col_np[:, 0] = np.concatenate([inv_np, inv_np])
    col_np[:64, 1] = -1.0
    col_np[64:, 1] = 1.0
    col_d = nc.inline_tensor(col_np, name="colc").ap()
    invrow_d = nc.inline_tensor(inv_np.reshape(1, 64).copy(), name="invrow").ap()

    with ExitStack() as es:
        b = B(nc, es)
        for i in range(8):
            t = es.enter_context(nc.psum_tensor("bank%d" % i, [128, 512], F32))
            b.banks.append((t, b.reg("bank")))
            b.banks[-1][1].excl = True

        cF, cF_r = b.sb(es, "cF", [128, 7, 128], F32)
        cB, cB_r = b.sb(es, "cB", [128, 7, 128], BF16)
        colc, colc_r = b.sb(es, "colc", [128, 2], F32)
        metat, meta_r = b.sb(es, "meta", [128, 4], F32)
        gn, gn_r = b.sb(es, "gn", [128, NCH], F32)
        gm, gm_r = b.sb(es, "gm", [128, NCH], F32)
        gg, gg_r = b.sb(es, "gg", [128, 8], F32)
        wa2, wa2_r = b.sb(es, "wa2", [32, 512], BF16)
        b.dma("sync", cF[:], consts_d[:, :, :], cF_r, writes=[cF_r])
        b.dma("gpsimd", cB[:], consts_d[:, :, :], cB_r, writes=[cB_r])
        b.dma("sync", colc[:], col_d[:, :], colc_r, writes=[colc_r])
        b.dma("sync", metat[:], meta[:, :], meta_r, writes=[meta_r])
        b.dma("sync", gn[:], gnt[:, :], gn_r, writes=[gn_r])
        b.dma("sync", gm[:], gmt[:, :], gm_r, writes=[gm_r])
        b.dma("sync", gg[:], ggt[:, :], gg_r, writes=[gg_r])
        b.dma("gpsimd", wa2[0:17, :], wa2a[:, :], wa2_r, writes=[wa2_r])
        identF = cF[:, 0, :]
        triF = cF[:, 1, :]
        onesF = cF[:, 5, :]
        identB = cB[:, 0, :]
        triB = cB[:, 1, :]
        LmB = cB[:, 2, :]
        mskP = cB[:, 3, :]
        mskC = cB[:, 4, :]
        onesB = cB[:, 5, :]
        PswB = cB[:, 6, :]

        KMT, KMT_r = b.sb(es, "KMT", [128, 4, 256], BF16)
        VM, VM_r = b.sb(es, "VM", [128, 2, 512], BF16)
        hsT, hsT_r = b.sb(es, "hsT", [128, NCH, 4], BF16)
        Sst, Sst_r = b.sb(es, "Sst", [128, 4, 256], F32)
        dd_r = b.reg("dram2dram")


        def norm_T(es_l, x_rows, nrows, gt, dst_fn, dst_reg, xq="sync"):
            xt, xt_r = b.sb(es_l, "xt", [128, D], F32)
            sq, sq_r = b.sb(es_l, "sq", [128, D], F32)
            ss, ss_r = b.sb(es_l, "ss", [128, 2], F32)
            return xt, xt_r, sq, sq_r, ss, ss_r

        def rms_rows(xt, xt_r, sq, sq_r, ss, ss_r, n, dim):
            b.act(sq[0:n, 0:dim], xt[0:n, 0:dim], AF.Square, [xt_r], [sq_r, ss_r], accum_out=ss[0:n, 0:1])
            b.act(ss[0:n, 1:2], ss[0:n, 0:1], AF.Ln, [ss_r, epsc_r], [ss_r], scale=1.0 / dim, bias=epsc[0:n, 0:1])
            b.act(ss[0:n, 1:2], ss[0:n, 1:2], AF.Exp, [ss_r], [ss_r], scale=-0.5)

        epsc, epsc_r = b.sb(es, "epsc", [128, 2], F32)
        b.memset(epsc[:, 0:1], EPS, [epsc_r])
        b.memset(epsc[:, 1:2], 1.0, [epsc_r])
        one_col = epsc[:, 1:2]

        def build_hT(es_unused, rows_ap, ntiles, nrows_last, gt, gt_r, hT, hT_r):
          with ExitStack() as es_l:
            xt, xt_r, sq, sq_r, ss, ss_r = norm_T(es_l, None, None, None, None, None)
            xt2, xt2_r = b.sb(es_l, "xt2", [128, D], F32)
            xbuf = [(xt, xt_r), (xt2, xt2_r)]
            sq2, sq2_r = b.sb(es_l, "sq2", [128, D], F32)
            jk, jk_r = b.sb(es_l, "jk", [128, D], F32)
            ss2, ss2_r = b.sb(es_l, "ss2", [128, 2], F32)
            sqb_ = [(sq, sq_r, ss, ss_r), (sq2, sq2_r, ss2, ss2_r)]
            for i in range(ntiles):
                n = 128 if i < ntiles - 1 or nrows_last == 128 else nrows_last
                xa, xa_r = xbuf[i % 2]
                sq, sq_r, ss, ss_r = sqb_[i % 2]
                b.dma("sync", xa[0:n, :], rows_ap[i * 128:i * 128 + n, :], xa_r, writes=[xa_r])
                rms_rows(xa, xa_r, jk, jk_r, ss, ss_r, n, D)
                b.act(sq[0:n, :], xa[0:n, :], AF.Copy, [xa_r, ss_r], [sq_r], scale=ss[0:n, 1:2])
                for q in range(4):
                    pb, pb_r = b.bank()
                    for cc in range(4):
                        c = q * 4 + cc
                        b.tr(pb[:, cc * 128:cc * 128 + n], sq[0:n, c * 128:(c + 1) * 128], identF[0:n, 0:n],
                             [sq_r, cF_r], [pb_r])
                    b.tt(hT[:, q * 4:(q + 1) * 4, i * 128:i * 128 + n],
                         pb[:, :].rearrange("p (c t) -> p c t", c=4)[:, :, 0:n],
                         gt[:, q * 4:(q + 1) * 4].unsqueeze(2).to_broadcast([128, 4, n]),
                         ALU.mult, [pb_r, gt_r], [hT_r])
            b.end_scope(es_l)

        stg = []
        copy_chunks = []
        for g, (win, dil) in enumerate(SWA):
            for bb in range(4):
                for r0 in range(0, win - 1, 128):
                    r1 = min(r0 + 128, win - 1)
                    copy_chunks.append((swa_s[g][bb, r0:r1, :], cch[g][bb, r0 + 1:r1 + 1, :]))

        def wload(wb, wb_r, src, rows, w):
            nchk = rows // 128
            b.dma("gpsimd", wb[:, 0:nchk, 0:w], src.rearrange("(c p) w -> p c w", p=128), wb_r, writes=[wb_r])

        with ExitStack() as st:
            hmT, hmT_r = b.sb(st, "hmT", [128, NCH, 256], BF16)
            build_hT(st, mem, 2, 128, gm, gm_r, hmT, hmT_r)
            mk, mk_r = b.sb(st, "mk", [128, 2, 1024], F32)
            for cb in range(2):
                wb, wb_r = b.sb(st, "wbm", [128, NCH, 512], BF16)
                wload(wb, wb_r, wmkv[:, cb * 512:(cb + 1) * 512], D, 512)
                for i in range(2):
                    pb, pb_r = b.bank()
                    for c in range(NCH):
                        b.mm(pb[:, :], hmT[:, c, i * 128:(i + 1) * 128], wb[:, c, :], c == 0, c == NCH - 1,
                             [hmT_r, wb_r], [pb_r])
                    b.cp(mk[:, i, cb * 512:(cb + 1) * 512], pb[:, :], [pb_r], [mk_r])
                if cb == 0:
                    for h in range(4):
                        pb, pb_r = b.bank()
                        for c in range(NCH):
                            b.mm(pb[:, 0:256], wb[:, c, h * 128:(h + 1) * 128], hmT[:, c, :], c == 0, c == NCH - 1,
                                 [hmT_r, wb_r], [pb_r])
                        b.cp(KMT[:, h, :], pb[:, 0:256], [pb_r], [KMT_r])
            b.cp(VM[:, :, :], mk[:, :, 512:1024], [mk_r], [VM_r])
            b.dma("sync", memkv_p.rearrange("(i p) e -> p i e", p=128), mk[:, :, :], mk_r, reads=[mk_r])
            build_hT(st, xsmp, 1, 4, gn, gn_r, hsT, hsT_r)
            b.end_scope(st)

        PI_LO = 3.1415925

        def silu_to(dst, dst_r, x, x_r, tmp=None, tmp_r=None):
            rd = [] if dst_r is x_r else [x_r]
            b.act(dst, x, AF.Silu, rd, [dst_r])

        def sincos(ang, ang_r, ki, ki_r, kf, kf_r, m1, m1_r, rc, rc_r, cos_out, cos_r, sin_out, sin_r):
            b.ts(ki, ang, 1.0 / TWO_PI, ALU.mult, [ang_r], [ki_r])
            b.cp(kf, ki, [ki_r], [kf_r])
            b.stt(ang, kf, -6.28125, ang, ALU.mult, ALU.add, [kf_r, ang_r], [ang_r])
            b.stt(ang, kf, -0.0019353071795864769, ang, ALU.mult, ALU.add, [kf_r, ang_r], [ang_r])
            b.ts(m1, ang, PI_LO, ALU.is_gt, [ang_r], [m1_r])
            b.stt(ang, m1, -TWO_PI, ang, ALU.mult, ALU.add, [m1_r, ang_r], [ang_r])
            b.ts(m1, ang, -PI_LO, ALU.is_lt, [ang_r], [m1_r])
            b.stt(ang, m1, TWO_PI, ang, ALU.mult, ALU.add, [m1_r, ang_r], [ang_r])
            b.ts(rc, ang, np.pi / 2, ALU.add, [ang_r], [rc_r])
            b.ts(m1, rc, PI_LO, ALU.is_gt, [rc_r], [m1_r])
            b.stt(rc, m1, -TWO_PI, rc, ALU.mult, ALU.add, [m1_r, rc_r], [rc_r])
            b.ts(ang, ang, PI_LO, ALU.min, [ang_r], [ang_r], s2=-PI_LO, op1=ALU.max)
            b.ts(rc, rc, PI_LO, ALU.min, [rc_r], [rc_r], s2=-PI_LO, op1=ALU.max)
            b.act(sin_out, ang, AF.Sin, [ang_r], [sin_r])
            b.act(cos_out, rc, AF.Sin, [rc_r], [cos_r])

        stT = ExitStack()
        cosT, cos_r = b.sb(stT, "cosT", [128, 2 * T], F32)
        sinT, sin_r = b.sb(stT, "sinT", [128, 2 * T], F32)
        def make_table_steps(sc):
            TW = 512
            ang, ang_r = b.sb(sc, "ang", [128, TW], F32)
            ki, ki_r = b.sb(sc, "ki", [128, TW], I32)
            kf, kf_r = b.sb(sc, "kf", [128, TW], F32)
            m1, m1_r = b.sb(sc, "m1", [128, TW], F32)
            rc, rc_r = b.sb(sc, "rc", [128, TW], F32)

            def step(ch):
                b.op("gpsimd", lambda e: e.iota(ki[:], pattern=[[1, TW]], base=ch * TW, channel_multiplier=0),
                     (), [ki_r])
                b.cp(ang[:], ki[:], [ki_r], [ang_r])
                b.ts(ang[:], ang[:], metat[:, 0:1], ALU.add, [ang_r, meta_r, colc_r], [ang_r], s2=colc[:, 0:1],
                     op1=ALU.mult)
                csl = slice(ch * TW, (ch + 1) * TW)
                sincos(ang[:], ang_r, ki[:], ki_r, kf[:], kf_r, m1[:], m1_r, rc[:], rc_r,
                       cosT[:, csl], cos_r, sinT[:, csl], sin_r)
                b.ts(sinT[:, csl], sinT[:, csl], colc[:, 1:2], ALU.mult, [sin_r, colc_r], [sin_r])

            return [lambda ch=ch: step(ch) for ch in range(2 * T // TW)]

        table_steps = []

        for s in range(4):
            own = s == 3
            halo = s == 2
            with ExitStack() as st:
                hT, hT_r = b.sb(st, "hT", [128, NCH, T], BF16)
                wbs = [b.sb(st, "wb", [128, NCH, 512], BF16) for _ in range(2)]
                stgs = [b.sb(st, "stg", [128, 512], F32) for _ in range(6)]
                cnt = {"w": 0, "s": 0}
                wload(wbs[0][0], wbs[0][1], w_in[:, C_GK:C_GK + 512], D, 512)
                pre_w = [C_GK]
                build_hT(st, xs[s * T:(s + 1) * T, :], NT, 128, gn, gn_r, hT, hT_r)
                if s == 0:
                    table_steps.extend(make_table_steps(st))

                def next_w():
                    r = wbs[cnt["w"] % 2]
                    cnt["w"] += 1
                    return r

                def trickle(flush=False):
                    if own and copy_chunks and (flush or cnt["s"] % 6 == 0):
                        for _ in range(len(copy_chunks) if flush else 1):
                            d_, s_ = copy_chunks.pop(0)
                            b.dma("sync", d_, s_, dd_r)

                def _in(c, lo, hi):
                    return lo <= c < hi

                def need_tm(c):
                    return _in(c, C_GV, C_GR) or _in(c, C_SQ, C_SR) or _in(c, C_MQ, C_MR)

                def need_fm(c):
                    return not (_in(c, C_GV, C_GR) or _in(c, C_SQ, C_SV) or _in(c, C_MQ, C_MR))

                def evac(pb, pb_r, np_, nf, dst):
                    sg, sg_r = stgs[cnt["s"] % 6]
                    cnt["s"] += 1
                    if cnt["s"] % 2 == 0:
                        b.cp(sg[0:np_, 0:nf], pb[0:np_, 0:nf], [pb_r], [sg_r])
                    else:
                        b.act(sg[0:np_, 0:nf], pb[0:np_, 0:nf], AF.Copy, [pb_r], [sg_r])
                    b.dma("sync", dst, sg[0:np_, 0:nf], sg_r, reads=[sg_r])
                    trickle()

                xbs = [b.sb(st, "xbr", [128, 512], BF16) for _ in range(3)]
                xbc = [0]
                t2s = [b.sb(st, "t2r", [128, 512], F32) for _ in range(2)]
                pend = []

                def rope_finish(pb, pb_r, xb, xb_r, tcol, dst):
                    pb2, pb2_r = b.bank()
                    b.mm(pb2[:, :], PswB, xb[:, :], True, True, [cB_r, xb_r], [pb2_r])
                    sg, sg_r = stgs[cnt["s"] % 6]
                    t2, t2_r = t2s[cnt["s"] % 2]
                    cnt["s"] += 1
                    b.tt(sg[:, :], pb[:, :], cosT[:, tcol:tcol + 512], ALU.mult, [pb_r, cos_r], [sg_r])
                    b.tt(t2[:, :], pb2[:, :], sinT[:, tcol:tcol + 512], ALU.mult, [pb2_r, sin_r], [t2_r])
                    b.tt(sg[:, :], sg[:, :], t2[:, :], ALU.add, [sg_r, t2_r], [sg_r])
                    b.dma("sync", dst, sg[:, :], sg_r, reads=[sg_r])
                    trickle()

                def fm_job(col0, ncols, dst_rows_fn, tb0_fn=lambda c0: 0, rope_t0=None):
                    for c0 in range(col0, col0 + ncols, 512):
                        w = min(512, col0 + ncols - c0)
                        wb, wb_r = next_w()
                        wload(wb, wb_r, w_in[:, c0:c0 + w], D, w)
                        for m0 in range(0, w, 128):
                            mw = min(128, w - m0)
                            for tb in range(tb0_fn(c0), 4):
                                pb, pb_r = b.bank()
                                for c in range(NCH):
                                    b.mm(pb[0:mw, :], wb[:, c, m0:m0 + mw], hT[:, c, tb * 512:(tb + 1) * 512],
                                         c == 0, c == NCH - 1, [wb_r, hT_r], [pb_r])
                                dst_ = dst_rows_fn(c0 + m0, mw)[:, tb * 512:(tb + 1) * 512]
                                if rope_t0 is None:
                                    evac(pb, pb_r, mw, 512, dst_)
                                else:
                                    xb, xb_r = xbs[xbc[0] % 3]
                                    xbc[0] += 1
                                    b.act(xb[:, :], pb[:, :], AF.Copy, [pb_r], [xb_r])
                                    prev = pend[:]
                                    del pend[:]
                                    pend.append((pb, pb_r, xb, xb_r, rope_t0 + tb * 512, dst_))
                                    for pa in prev:
                                        rope_finish(*pa)
                            if rope_t0 is not None and m0 + 128 >= w and c0 + 512 >= col0 + ncols:
                                for pa in pend:
                                    rope_finish(*pa)
                                del pend[:]
                            if own and need_fm(c0):
                                pb, pb_r = b.bank()
                                for c in range(NCH):
                                    b.mm(pb[0:mw, 0:4], wb[:, c, m0:m0 + mw], hsT[:, c, :], c == 0, c == NCH - 1,
                                         [wb_r, hsT_r], [pb_r])
                                evac(pb, pb_r, mw, 4, ZSfm[c0 + m0:c0 + m0 + mw, :])
                        if own and need_tm(c0):
                            pb, pb_r = b.bank()
                            for c in range(NCH):
                                b.mm(pb[0:4, 0:w], hsT[:, c, :], wb[:, c, 0:w], c == 0, c == NCH - 1,
                                     [wb_r, hsT_r], [pb_r])
                            evac(pb, pb_r, 4, w, ZStm[:, c0:c0 + w])

                def tm_job(col0, ncols, dst_fn, tile0_fn=lambda c0: 0):
                    for c0 in range(col0, col0 + ncols, 512):
                        w = 512
                        wb, wb_r = next_w()
                        if pre_w and pre_w[0] == c0:
                            pre_w.pop()
                        else:
                            wload(wb, wb_r, w_in[:, c0:c0 + w], D, w)
                        for i in range(tile0_fn(c0), NT):
                            pb, pb_r = b.bank()
                            for c in range(NCH):
                                b.mm(pb[:, :], hT[:, c, i * 128:(i + 1) * 128], wb[:, c, :], c == 0, c == NCH - 1,
                                     [wb_r, hT_r], [pb_r])
                            evac(pb, pb_r, 128, 512, dst_fn(i, c0))
                            if table_steps:
                                table_steps.pop(0)()
                        if own and need_fm(c0):
                            for m0 in range(0, w, 128):
                                pb, pb_r = b.bank()
                                for c in range(NCH):
                                    b.mm(pb[:, 0:4], wb[:, c, m0:m0 + 128], hsT[:, c, :], c == 0, c == NCH - 1,
                                         [wb_r, hsT_r], [pb_r])
                                evac(pb, pb_r, 128, 4, ZSfm[c0 + m0:c0 + m0 + 128, :])
                        if own and need_tm(c0):
                            pb, pb_r = b.bank()
                            for c in range(NCH):
                                b.mm(pb[0:4, 0:w], hsT[:, c, :], wb[:, c, 0:w], c == 0, c == NCH - 1,
                                     [wb_r, hsT_r], [pb_r])
                            evac(pb, pb_r, 4, w, ZStm[:, c0:c0 + w])

                tm_job(C_GK, 1536, lambda i, c0: Zg[s * T + i * 128:s * T + (i + 1) * 128, c0 - C_GK:c0 - C_GK + 512])
                fm_job(C_GA, 16, lambda r0, n: ZTga[s, r0 - C_GA:r0 - C_GA + n, :] if not own else ZTga[s, r0 - C_GA:r0 - C_GA + n, :])
                while table_steps:
                    table_steps.pop(0)()
                if halo or own:
                    tm_job(C_SV, 1536,
                           lambda i, c0: Zs[(s - 2) * T + i * 128:(s - 2) * T + (i + 1) * 128, c0 - C_SV:c0 - C_SV + 512],
                           tile0_fn=(lambda c0: NT - min(NT, SWA[(c0 - C_SV) // 512][0] // 128)) if halo else (lambda c0: 0))
                if halo:
                    fm_job(C_SK, 1536, lambda r0, n: ZTskh[r0 - C_SK:r0 - C_SK + n, :],
                           tb0_fn=lambda c0: 4 - max(1, SWA[(c0 - C_SK) // 512][0] // 512), rope_t0=0)
                if own:
                    fm_job(C_GQ, 1024, lambda r0, n: ZT[r0:r0 + n, :])
                    fm_job(C_GR, 1024, lambda r0, n: ZT[r0:r0 + n, :])
                    fm_job(C_SQ, 3072, lambda r0, n: ZT[r0:r0 + n, :], rope_t0=T)
                    fm_job(C_SR, INW - C_SR, lambda r0, n: ZT[r0:r0 + n, :])
                    trickle(flush=True)
                b.end_scope(st)

        b.end_scope(stT)
        stT.close()
        actA, actA_r = b.sb(es, "actA", [128, 8, T + 4], BF16)
        actB, actB_r = b.sb(es, "actB", [128, 4, T + 4], BF16)
        actC, actC_r = b.sb(es, "actC", [128, 4, T + 4], BF16)
        with ExitStack() as st:
            gaT, gaT_r = b.sb(st, "gaT", [32, T], BF16)
            kts = [b.sb(st, "kt", [128, NT, 128], F32) for _ in range(2)]
            vbs2 = [b.sb(st, "vb", [128, NT, 256], BF16) for _ in range(2)]
            la, la_r = b.sb(st, "la", [128, NT, 128], BF16)
            kh, kh_r = b.sb(st, "kh", [128, NT, 128], BF16)
            Sb, Sb_r = b.sb(st, "Sb", [128, NT, 256], BF16)
            tE, tE_r = b.sb(st, "tE", [128, 512], F32)
            tE2, tE2_r = b.sb(st, "tE2", [128, 512], F32)
            Td, Td_r = b.sb(st, "Td", [128, NT], F32)
            qT, qT_r = b.sb(st, "qT", [128, T], F32)
            kT, kT_r = b.sb(st, "kT", [128, T], F32)
            qh, qh_r = b.sb(st, "qh", [128, T], BF16)
            kth, kth_r = b.sb(st, "kth", [128, T], BF16)
            AT, AT_r = b.sb(st, "AT", [128, NT, 128], BF16)
            oF, oF_r = b.sb(st, "oF", [128, 2, T], F32)
            grT, grT_r = b.sb(st, "grT", [128, 2, T], F32)
            sqb, sqb_r = b.sb(st, "sqb", [128, 2, 512], BF16)
            rstd, rstd_r = b.sb(st, "rstd", [128, 512], F32)
            tmpn, tmpn_r = b.sb(st, "tmpn", [128, 512], F32)
            b.memset(Sst[:], 0.0, [Sst_r])
            b.memset(gaT[:], 1.0, [gaT_r])
            for s in range(4):
                own = s == 3
                b.dma("gpsimd", gaT[0:16, :], ZTga[s, :, :], gaT_r, writes=[gaT_r])
                for hh in range(4):
                    kt, kt_r = kts[(s * 4 + hh) % 2]
                    vb, vb_r = vbs2[(s * 4 + hh) % 2]
                    b.dma("sync" if hh % 2 == 0 else "scalar", kt[:],
                          Zg[s * T:(s + 1) * T, hh * 128:(hh + 1) * 128].rearrange("(i p) e -> p i e", p=128),
                          kt_r, writes=[kt_r])
                    b.dma("gpsimd", vb[:],
                          Zg[s * T:(s + 1) * T, 512 + hh * 256:512 + (hh + 1) * 256].rearrange("(i p) e -> p i e", p=128),
                          vb_r, writes=[vb_r])
                    for q in range(4):
                        pb, pb_r = b.bank()
                        for cc in range(4):
                            i = q * 4 + cc
                            b.mm(pb[:, cc * 128:(cc + 1) * 128], gaT[0:17, i * 128:(i + 1) * 128],
                                 wa2[0:17, hh * 128:(hh + 1) * 128], True, True, [gaT_r, wa2_r], [pb_r])
                        b.act(tE[:], pb[:], AF.Exp, [pb_r], [tE_r], scale=-1.0)
                        b.act(la[:, q * 4:(q + 1) * 4, :], tE[:].rearrange("p (c t) -> p c t", c=4), AF.Ln,
                              [tE_r, epsc_r], [la_r], bias=one_col)
                    if not own:
                        for q in range(4):
                            pb, pb_r = b.bank()
                            for cc in range(4):
                                i = q * 4 + cc
                                b.mm(pb[:, cc * 128:(cc + 1) * 128], LmB, la[:, i, :], True, i == NT - 1,
                                     [cB_r, la_r], [pb_r])
                                for j2 in range(i + 1, NT):
                                    b.mm(pb[:, cc * 128:(cc + 1) * 128], onesB, la[:, j2, :], False, j2 == NT - 1,
                                         [cB_r, la_r], [pb_r])
                            b.act(tE[:], pb[:], AF.Exp, [pb_r], [tE_r], scale=-1.0 / 16)
                            b.tt(kh[:, q * 4:(q + 1) * 4, :], kt[:, q * 4:(q + 1) * 4, :],
                                 tE[:].rearrange("p (c t) -> p c t", c=4), ALU.mult, [kt_r, tE_r], [kh_r])
                        pb, pb_r = b.bank()
                        for i in range(NT):
                            b.mm(pb[:, 0:1], la[:, i, :], onesB[:, 0:1], i == 0, i == NT - 1, [la_r, cB_r], [pb_r])
                        b.act(Td[:, 0:1], pb[:, 0:1], AF.Exp, [pb_r], [Td_r], scale=-1.0 / 16)
                        pbU, pbU_r = b.bank()
                        for i in range(NT):
                            b.mm(pbU[:, 0:256], kh[:, i, :], vb[:, i, :], i == 0, i == NT - 1, [kh_r, vb_r], [pbU_r])
                        b.stt(Sst[:, hh, :], Sst[:, hh, :], Td[:, 0:1], pbU[:, 0:256], ALU.mult, ALU.add,
                              [Sst_r, Td_r, pbU_r], [Sst_r])
                        continue
                    for q in range(4):
                        pb, pb_r = b.bank()
                        for cc in range(4):
                            i = q * 4 + cc
                            b.mm(pb[:, cc * 128:(cc + 1) * 128], LmB, la[:, i, :], True, True, [cB_r, la_r], [pb_r])
                        b.act(tE[:], pb[:], AF.Exp, [pb_r], [tE_r], scale=-1.0 / 16)
                        b.tt(kh[:, q * 4:(q + 1) * 4, :], kt[:, q * 4:(q + 1) * 4, :],
                             tE[:].rearrange("p (c t) -> p c t", c=4), ALU.mult, [kt_r, tE_r], [kh_r])
                    pb, pb_r = b.bank()
                    for i in range(NT):
                        b.mm(pb[:, i:i + 1], la[:, i, :], onesB[:, 0:1], True, True, [la_r, cB_r], [pb_r])
                    b.act(Td[:], pb[:, 0:NT], AF.Exp, [pb_r], [Td_r], scale=-1.0 / 16)
                    for i in range(NT):
                        if i % 2 == 0:
                            pbU, pbU_r = b.bank()
                        usl = slice((i % 2) * 256, (i % 2 + 1) * 256)
                        b.mm(pbU[:, usl], kh[:, i, :], vb[:, i, :], True, True, [kh_r, vb_r], [pbU_r])
                        if own:
                            b.act(Sb[:, i, :], Sst[:, hh, :], AF.Copy, [Sst_r], [Sb_r])
                        b.stt(Sst[:, hh, :], Sst[:, hh, :], Td[:, i:i + 1], pbU[:, usl], ALU.mult, ALU.add,
                              [Sst_r, Td_r, pbU_r], [Sst_r])
                    if not own:
                        continue
                    b.dma("sync", qT[:], ZT[C_GQ + hh * 128:C_GQ + (hh + 1) * 128, :], qT_r, writes=[qT_r])
                    b.dma("sync", kT[:], ZT[C_GK + hh * 128:C_GK + (hh + 1) * 128, :], kT_r, writes=[kT_r])
                    b.dma("sync", grT[:], ZT[C_GR + hh * 256:C_GR + (hh + 1) * 256, :].rearrange("(e p) t -> p e t", p=128),
                          grT_r, writes=[grT_r])
                    for q in range(4):
                        pb, pb_r = b.bank()
                        for cc in range(4):
                            i = q * 4 + cc
                            b.mm(pb[:, cc * 128:(cc + 1) * 128], la[:, i, :], triB, True, True, [la_r, cB_r], [pb_r])
                        sl = slice(q * 512, (q + 1) * 512)
                        b.act(tE[:], pb[:], AF.Exp, [pb_r], [tE_r], scale=-1.0 / 16)
                        b.stt(qh[:, sl], qT[:, sl], SC, tE[:], ALU.mult, ALU.mult, [qT_r, tE_r], [qh_r])
                        b.act(tE2[:], pb[:], AF.Exp, [pb_r], [tE2_r], scale=1.0 / 16)
                        b.tt(kth[:, sl], kT[:, sl], tE2[:], ALU.mult, [kT_r, tE2_r], [kth_r])
                    for q in range(4):
                        pb, pb_r = b.bank()
                        for cc in range(4):
                            i = q * 4 + cc
                            b.mm(pb[:, cc * 128:(cc + 1) * 128], kth[:, i * 128:(i + 1) * 128],
                                 qh[:, i * 128:(i + 1) * 128], True, True, [kth_r, qh_r], [pb_r])
                        b.tt(AT[:, q * 4:(q + 1) * 4, :], pb[:].rearrange("p (c t) -> p c t", c=4),
                             triF.unsqueeze(1).to_broadcast([128, 4, 128]), ALU.mult, [pb_r, cF_r], [AT_r])
                    for e2 in range(2):
                        for q in range(4):
                            pb, pb_r = b.bank()
                            for cc in range(4):
                                i = q * 4 + cc
                                b.mm(pb[:, cc * 128:(cc + 1) * 128], vb[:, i, e2 * 128:(e2 + 1) * 128], AT[:, i, :],
                                     True, False, [vb_r, AT_r], [pb_r])
                                b.mm(pb[:, cc * 128:(cc + 1) * 128], Sb[:, i, e2 * 128:(e2 + 1) * 128],
                                     qh[:, i * 128:(i + 1) * 128], False, True, [Sb_r, qh_r], [pb_r])
                            b.act(oF[:, e2, q * 512:(q + 1) * 512], pb[:], AF.Copy, [pb_r], [oF_r])
                    silu_to(grT[:], grT_r, grT[:], grT_r)
                    for tb in range(4):
                        sl = slice(tb * 512, (tb + 1) * 512)
                        b.tt(sqb[:], oF[:, :, sl], oF[:, :, sl], ALU.mult, [oF_r], [sqb_r])
                        pb, pb_r = b.bank()
                        b.mm(pb[:], onesB, sqb[:, 0, :], True, False, [cB_r, sqb_r], [pb_r])
                        b.mm(pb[:], onesB, sqb[:, 1, :], False, True, [cB_r, sqb_r], [pb_r])
                        b.act(rstd[:], pb[:], AF.Ln, [pb_r, epsc_r], [rstd_r], scale=1.0 / 256, bias=epsc[:, 0:1])
                        b.act(rstd[:], rstd[:], AF.Exp, [rstd_r], [rstd_r], scale=-0.5)
                        for e2 in range(2):
                            b.stt(tmpn[:], oF[:, e2, sl], gg[:, hh * 2 + e2:hh * 2 + e2 + 1], rstd[:], ALU.mult, ALU.mult,
                                  [oF_r, gg_r, rstd_r], [tmpn_r])
                            b.tt(actA[:, hh * 2 + e2, sl], tmpn[:], grT[:, e2, sl], ALU.mult, [tmpn_r, grT_r], [actA_r])
            b.dma("sync", gla_p.rearrange("h p v -> p h v"), Sst[:], Sst_r, reads=[Sst_r])
            b.end_scope(st)

        with ExitStack() as st:
            krFs = [b.sb(st, "krF", [128, 2 * T], F32) for _ in range(2)]
            krbs = [b.sb(st, "krb", [128, 2 * T], BF16) for _ in range(2)]
            qrbs = [b.sb(st, "qrb", [128, T], BF16) for _ in range(2)]
            vbss = [b.sb(st, "vbs", [128, 32, 128], BF16) for _ in range(2)]
            nd, nd_r = b.sb(st, "nd", [128, 2, T], F32)
            num, num_r, den, den_r = nd[:, 0, :], nd_r, nd[:, 1, :], nd_r
            PTs = [b.sb(st, "PT", [128, 256], BF16) for _ in range(3)]
            kst, kst_r = b.sb(st, "kst", [128, 4, 128], F32)
            srT, srT_r = b.sb(st, "srT", [128, 1024], F32)
            ssil, ssil_r = b.sb(st, "ssil", [128, 1024], F32)
            ptc = [0]
            unit = 0
            for j in range(4):
                b.memset(nd[:], 0.0, [nd_r])
                for g, (win, d) in enumerate(SWA):
                    krF, krF_r = krFs[unit % 2]
                    krb, krb_r = krbs[unit % 2]
                    qrb, qrb_r = qrbs[unit % 2]
                    vbs, vbs_r = vbss[unit % 2]
                    unit += 1
                    H = 128 * d
                    L = H + T
                    NB = 16 // d + 1
                    row = g * 512 + j * 128
                    W = min(win, T)
                    b.dma("sync", krF[:, 0:H], ZTskh[row:row + 128, T - H:T], krF_r, writes=[krF_r])
                    b.dma("scalar", krF[:, H:L], ZT[C_SK + row:C_SK + row + 128, :], krF_r, writes=[krF_r])
                    b.dma("gpsimd", qrb[:, :], ZT[C_SQ + row:C_SQ + row + 128, :], qrb_r, writes=[qrb_r])
                    if d <= NB:
                        vsrc = Zs[T - H:2 * T, row:row + 128].rearrange("(nb p dd) e -> dd p nb e", p=128, dd=d)
                        for r in range(d):
                            b.dma("gpsimd", vbs[:, r * NB:(r + 1) * NB, :], vsrc[r], vbs_r, writes=[vbs_r])
                        vblk = lambda r, nbi, NB=NB, d=d: r * NB + nbi
                    else:
                        for nbi in range(NB):
                            r0_ = T - H + nbi * 128 * d
                            b.dma("gpsimd", vbs[:, nbi * d:(nbi + 1) * d, :],
                                  Zs[r0_:r0_ + 128 * d, row:row + 128].rearrange("(p dd) e -> p dd e", dd=d),
                                  vbs_r, writes=[vbs_r])
                        vblk = lambda r, nbi, NB=NB, d=d: nbi * d + r
                    b.act(krb[:, 0:L], krF[:, 0:L], AF.Copy, [krF_r], [krb_r])
                    nt_out = W // 128
                    for q0 in range(0, nt_out, 4):
                        nq = min(4, nt_out - q0)
                        pb, pb_r = b.bank()
                        for cc in range(nq):
                            col0 = L - W + (q0 + cc) * 128
                            b.tr(pb[:, cc * 128:(cc + 1) * 128], krF[:, col0:col0 + 128], identF, [krF_r, cF_r], [pb_r])
                        b.cp(kst[:, 0:nq, :], pb[:, 0:nq * 128].rearrange("p (c t) -> p c t", c=nq), [pb_r], [kst_r])
                        b.dma("sync",
                              swa_p[g][q0 * 128:(q0 + nq) * 128, j * 128:(j + 1) * 128].rearrange("(c p) e -> p c e", p=128),
                              kst[:, 0:nq, :], kst_r, reads=[kst_r])
                    if j == 0:
                        b.dma("sync", swa_p[g][:, 512:1024], Zs[2 * T - W:2 * T, g * 512:(g + 1) * 512], dd_r)
                    qv = qrb[:, :].rearrange("p (u dd) -> p dd u", dd=d)
                    kv = krb[:, 0:L].rearrange("p (u dd) -> p dd u", dd=d)
                    ndv = nd[:, :, :].rearrange("p c (u dd) -> p c dd u", dd=d)

                    def emit_S(r, n):
                        qs = qv[:, r, n * 128:(n + 1) * 128]
                        PT, PT_r = PTs[ptc[0] % 3]
                        ptc[0] += 1
                        pb, pb_r = b.bank()
                        b.mm(pb[:, 0:128], kv[:, r, n * 128:(n + 1) * 128], qs, True, False, [krb_r, qrb_r], [pb_r])
                        b.mm(pb[:, 0:128], identB, mskP, False, True, [cB_r], [pb_r])
                        b.mm(pb[:, 128:256], kv[:, r, (n + 1) * 128:(n + 2) * 128], qs, True, False,
                             [krb_r, qrb_r], [pb_r])
                        b.mm(pb[:, 128:256], identB, mskC, False, True, [cB_r], [pb_r])
                        if n == 0:
                            b.act(PT[:, 0:128], pb[:, 0:128], AF.Exp, [pb_r, meta_r], [PT_r], scale=SC,
                                  bias=metat[:, 1:2])
                            b.act(PT[:, 128:256], pb[:, 128:256], AF.Exp, [pb_r], [PT_r], scale=SC)
                        else:
                            b.act(PT[:, 0:256], pb[:, 0:256], AF.Exp, [pb_r], [PT_r], scale=SC)
                        return PT, PT_r

                    def emit_PV(r, n, PT, PT_r):
                        pb2, pb2_r = b.bank()
                        b.mm(pb2[:, 0:128], vbs[:, vblk(r, n), :], PT[:, 0:128], True, False, [vbs_r, PT_r], [pb2_r])
                        b.mm(pb2[:, 0:128], vbs[:, vblk(r, n + 1), :], PT[:, 128:256], False, True,
                             [vbs_r, PT_r], [pb2_r])
                        b.mm(pb2[:, 128:256], onesB, PT[:, 0:128], True, False, [cB_r, PT_r], [pb2_r])
                        b.mm(pb2[:, 128:256], onesB, PT[:, 128:256], False, True, [cB_r, PT_r], [pb2_r])
                        ndsl = ndv[:, :, r, n * 128:(n + 1) * 128]
                        b.tt(ndsl, ndsl, pb2[:, 0:256].rearrange("p (c q) -> p c q", c=2), ALU.add,
                             [nd_r, pb2_r], [nd_r])

                    prevS = None
                    for r in range(d):
                        for n in range(16 // d):
                            cur = emit_S(r, n)
                            if prevS is not None:
                                emit_PV(*prevS)
                            prevS = (r, n) + cur
                    emit_PV(*prevS)
                b.act(den, den, AF.Ln, [], [den_r])
                b.act(den, den, AF.Exp, [], [den_r], scale=-1.0)
                b.tt(num, num, den, ALU.mult, [nd_r], [nd_r])
                for c0 in range(0, T, 1024):
                    b.dma("sync", srT[:], ZT[C_SR + j * 128:C_SR + (j + 1) * 128, c0:c0 + 1024], srT_r, writes=[srT_r])
                    silu_to(ssil[:], ssil_r, srT[:], srT_r)
                    b.tt(actB[:, j, c0:c0 + 1024], nd[:, 0, c0:c0 + 1024], ssil[:], ALU.mult, [nd_r, ssil_r], [actB_r])
            b.end_scope(st)

        with ExitStack() as st:
            mqbs = [b.sb(st, "mqb", [128, T], BF16) for _ in range(2)]
            mrTs = [b.sb(st, "mrT", [128, T], F32) for _ in range(2)]
            mos = [b.sb(st, "mo", [128, 512], F32) for _ in range(2)]
            mdens = [b.sb(st, "mden", [128, 512], F32) for _ in range(2)]
            PMs = [[b.sb(st, "PM", [128, 512], BF16) for _ in range(2)] for _ in range(3)]
            its = [(h, tb) for h in range(4) for tb in range(4)]

            def mem_S(k):
                h, tb = its[k]
                mqb, mqb_r = mqbs[h % 2]
                mrT, mrT_r = mrTs[h % 2]
                if tb == 0:
                    b.dma("gpsimd", mqb[:], ZT[C_MQ + h * 128:C_MQ + (h + 1) * 128, :], mqb_r, writes=[mqb_r])
                    b.dma("sync", mrT[:], ZT[C_MR + h * 128:C_MR + (h + 1) * 128, :], mrT_r, writes=[mrT_r])
                    silu_to(mrT[:], mrT_r, mrT[:], mrT_r)
                sl = slice(tb * 512, (tb + 1) * 512)
                PM = PMs[k % 3]
                for t2 in range(2):
                    pb, pb_r = b.bank()
                    b.mm(pb[:], KMT[:, h, t2 * 128:(t2 + 1) * 128], mqb[:, sl], True, True, [KMT_r, mqb_r], [pb_r])
                    b.act(PM[t2][0][:], pb[:], AF.Exp, [pb_r], [PM[t2][1]], scale=SC)

            def mem_PV(k):
                h, tb = its[k]
                mrT, mrT_r = mrTs[h % 2]
                sl = slice(tb * 512, (tb + 1) * 512)
                PM = PMs[k % 3]
                mo, mo_r = mos[k % 2]
                mden, mden_r = mdens[k % 2]
                pb, pb_r = b.bank()
                b.mm(pb[:], VM[:, 0, h * 128:(h + 1) * 128], PM[0][0][:], True, False, [VM_r, PM[0][1]], [pb_r])
                b.mm(pb[:], VM[:, 1, h * 128:(h + 1) * 128], PM[1][0][:], False, True, [VM_r, PM[1][1]], [pb_r])
                pb2, pb2_r = b.bank()
                b.mm(pb2[:], onesB, PM[0][0][:], True, False, [cB_r, PM[0][1]], [pb2_r])
                b.mm(pb2[:], onesB, PM[1][0][:], False, True, [cB_r, PM[1][1]], [pb2_r])
                b.act(mden[:], pb2[:], AF.Ln, [pb2_r], [mden_r])
                b.act(mden[:], mden[:], AF.Exp, [], [mden_r], scale=-1.0)
                b.tt(mo[:], pb[:], mden[:], ALU.mult, [pb_r, mden_r], [mo_r])
                b.tt(actC[:, h, sl], mo[:], mrT[:, sl], ALU.mult, [mo_r, mrT_r], [actC_r])

            mem_S(0)
            for k in range(len(its)):
                if k + 1 < len(its):
                    mem_S(k + 1)
                mem_PV(k)
            b.end_scope(st)

        with ExitStack() as st:
            stA = ExitStack()
            zqk, zqk_r = b.sb(stA, "zqk", [4, 3072], F32)
            zrot, zrot_r = b.sb(stA, "zrot", [4, 3072], F32)
            angs, angs_r = b.sb(stA, "angs", [4, 64], F32)
            kis, kis_r = b.sb(stA, "kis", [4, 64], I32)
            kfs, kfs_r = b.sb(stA, "kfs", [4, 64], F32)
            m1s, m1s_r = b.sb(stA, "m1s", [4, 64], F32)
            rcs, rcs_r = b.sb(stA, "rcs", [4, 64], F32)
            coss, coss_r = b.sb(stA, "coss", [4, 64], F32)
            sins, sins_r = b.sb(stA, "sins", [4, 64], F32)
            cos12, cos12_r = b.sb(stA, "cos12", [4, 24, 64], F32)
            sin12, sin12_r = b.sb(stA, "sin12", [4, 24, 64], F32)
            t1, t1_r = b.sb(stA, "t1", [4, 24, 64], F32)
            t2, t2_r = b.sb(stA, "t2", [4, 24, 64], F32)
            sself, sself_r = b.sb(stA, "sself", [4, 12], F32)
            b.dma("sync", zqk[:], ZStm[:, C_SQ:C_SQ + 3072], zqk_r, writes=[zqk_r])
            b.dma("sync", angs[:], invrow_d[0:1, :].to_broadcast([4, 64]), angs_r, writes=[angs_r])
            b.ts(angs[:], angs[:], 16384.0, ALU.mult, [angs_r], [angs_r])
            sincos(angs[:], angs_r, kis[:], kis_r, kfs[:], kfs_r, m1s[:], m1s_r, rcs[:], rcs_r,
                   coss[:], coss_r, sins[:], sins_r)
            b.cp(cos12[:], coss[:].unsqueeze(1).to_broadcast([4, 24, 64]), [coss_r], [cos12_r])
            b.cp(sin12[:], sins[:].unsqueeze(1).to_broadcast([4, 24, 64]), [sins_r], [sin12_r])
            z3 = zqk[:, :].rearrange("p (h e) -> p h e", e=128)
            r3 = zrot[:, :].rearrange("p (h e) -> p h e", e=128)
            x1, x2 = z3[:, :, 0:64], z3[:, :, 64:128]
            b.tt(t1[:], x1, cos12[:], ALU.mult, [zqk_r, cos12_r], [t1_r])
            b.tt(t2[:], x2, sin12[:], ALU.mult, [zqk_r, sin12_r], [t2_r])
            b.tt(r3[:, :, 0:64], t1[:], t2[:], ALU.subtract, [t1_r, t2_r], [zrot_r])
            b.tt(t1[:], x2, cos12[:], ALU.mult, [zqk_r, cos12_r], [t1_r])
            b.tt(t2[:], x1, sin12[:], ALU.mult, [zqk_r, sin12_r], [t2_r])
            b.tt(r3[:, :, 64:128], t1[:], t2[:], ALU.add, [t1_r, t2_r], [zrot_r])
            b.dma("sync", ZSR[:, :], zrot[:], zrot_r, reads=[zrot_r])
            prod, prod_r = b.sb(stA, "prod", [4, 1536], F32)
            b.tt(prod[:], zrot[:, 0:1536], zrot[:, 1536:3072], ALU.mult, [zrot_r], [prod_r])
            b.op("vector", lambda e: e.tensor_reduce(out=sself[:], in_=prod[:, :].rearrange("p (h e) -> p h e", e=128),
                                                     axis=AX.X, op=ALU.add), [prod_r], [sself_r])
            b.act(sself[:], sself[:], AF.Exp, [sself_r], [sself_r], scale=SC)
            b.dma("sync", ZSP[:, :], sself[:], sself_r, reads=[sself_r])
            b.end_scope(stA)
            stA.close()

            def fmload(name, c0, nch, dt=F32, q="sync"):
                t, r = b.sb(st, name, [128, nch, 4], dt)
                b.dma(q, t[:], ZSfm[c0:c0 + nch * 128, :].rearrange("(c p) bb -> p c bb", p=128), r, writes=[r])
                return t, r

            gqTs, gqTs_r = fmload("gqTs", C_GQ, 4)
            gkTs, gkTs_r = fmload("gkTs", C_GK, 4)
            grTs, grTs_r = fmload("grTs", C_GR, 8)
            svTs, svTs_r = fmload("svTs", C_SV, 12)
            srTs, srTs_r = fmload("srTs", C_SR, 4)
            mrTs, mrTs_r = fmload("mrTs", C_MR, 4)
            gaTs, gaTs_r = b.sb(st, "gaTs", [32, 4], BF16)
            b.memset(gaTs[:], 1.0, [gaTs_r])
            b.dma("gpsimd", gaTs[0:16, :], ZSfm[C_GA:C_GA + 16, :], gaTs_r, writes=[gaTs_r])
            dec, dec_r = b.sb(st, "dec", [128, 16], F32)
            pb, pb_r = b.bank()
            for h in range(4):
                b.mm(pb[:, h * 4:(h + 1) * 4], wa2[0:17, h * 128:(h + 1) * 128], gaTs[0:17, :], True, True,
                     [wa2_r, gaTs_r], [pb_r])
            b.act(dec[:], pb[:, 0:16], AF.Exp, [pb_r], [dec_r], scale=-1.0)
            b.act(dec[:], dec[:], AF.Ln, [dec_r, epsc_r], [dec_r], bias=one_col)
            b.act(dec[:], dec[:], AF.Exp, [dec_r], [dec_r], scale=-1.0 / 16)

            S0_l = [b.sb(st, "S0", [128, 4, 256], F32) for _ in range(2)]
            Sn_l = [b.sb(st, "Sn", [128, 4, 256], F32) for _ in range(2)]
            vbc_l = [b.sb(st, "vbc", [128, 1024], F32) for _ in range(2)]
            tmpv_l = [b.sb(st, "tmpv", [128, 256], F32) for _ in range(2)]
            os8_l = [b.sb(st, "os8", [128, 8], F32) for _ in range(2)]
            sq8_l = [b.sb(st, "sq8", [128, 8], F32) for _ in range(2)]
            ss4_l = [b.sb(st, "ss4", [128, 4], F32) for _ in range(2)]
            gs8_l = [b.sb(st, "gs8", [128, 8], F32) for _ in range(2)]
            gt8_l = [b.sb(st, "gt8", [128, 8], F32) for _ in range(2)]
            gx8_l = [b.sb(st, "gx8", [128, 8], F32) for _ in range(2)]
            qbc_l = [b.sb(st, "qbc", [128, 1536], F32) for _ in range(2)]
            mqbc_l = [b.sb(st, "mqbc", [128, 512], F32) for _ in range(2)]
            psbc_l = [b.sb(st, "psbc", [128, 12], F32) for _ in range(2)]
            Kc_l = [b.sb(st, "Kc", [128, 3, 512], F32) for _ in range(2)]
            Vc_l = [b.sb(st, "Vc", [128, 3, 512], F32) for _ in range(2)]
            Kcm_l = [b.sb(st, "Kcm", [128, 2, 512], F32) for _ in range(2)]
            Vcm_l = [b.sb(st, "Vcm", [128, 2, 512], F32) for _ in range(2)]
            prd_l = [b.sb(st, "prd", [128, 512], F32) for _ in range(2)]
            sc12_l = [b.sb(st, "sc12", [128, 12], F32) for _ in range(2)]
            sm8_l = [b.sb(st, "sm8", [128, 8], F32) for _ in range(2)]
            n4_l = [b.sb(st, "n4", [128, 4], F32) for _ in range(2)]
            d4_l = [b.sb(st, "d4", [128, 4], F32) for _ in range(2)]
            u4_l = [b.sb(st, "u4", [128, 4], F32) for _ in range(2)]
            for bb in range(4):
                S0, S0_r = S0_l[bb % 2]
                Sn, Sn_r = Sn_l[bb % 2]
                vbc, vbc_r = vbc_l[bb % 2]
                tmpv, tmpv_r = tmpv_l[bb % 2]
                os8, os8_r = os8_l[bb % 2]
                sq8, sq8_r = sq8_l[bb % 2]
                ss4, ss4_r = ss4_l[bb % 2]
                gs8, gs8_r = gs8_l[bb % 2]
                gt8, gt8_r = gt8_l[bb % 2]
                gx8, gx8_r = gx8_l[bb % 2]
                qbc, qbc_r = qbc_l[bb % 2]
                mqbc, mqbc_r = mqbc_l[bb % 2]
                psbc, psbc_r = psbc_l[bb % 2]
                Kc, Kc_r = Kc_l[bb % 2]
                Vc, Vc_r = Vc_l[bb % 2]
                Kcm, Kcm_r = Kcm_l[bb % 2]
                Vcm, Vcm_r = Vcm_l[bb % 2]
                prd, prd_r = prd_l[bb % 2]
                sc12, sc12_r = sc12_l[bb % 2]
                sm8, sm8_r = sm8_l[bb % 2]
                n4, n4_r = n4_l[bb % 2]
                d4, d4_r = d4_l[bb % 2]
                u4, u4_r = u4_l[bb % 2]
                b.dma("sync", S0[:], sgla[bb].rearrange("h p v -> p h v"), S0_r, writes=[S0_r])
                b.dma("sync", vbc[:], ZStm[bb:bb + 1, C_GV:C_GV + 1024].to_broadcast([128, 1024]), vbc_r, writes=[vbc_r])
                for h in range(4):
                    b.ts(tmpv[:], vbc[:, h * 256:(h + 1) * 256], gkTs[:, h, bb:bb + 1], ALU.mult, [vbc_r, gkTs_r], [tmpv_r])
                    b.stt(Sn[:, h, :], S0[:, h, :], dec[:, h * 4 + bb:h * 4 + bb + 1], tmpv[:], ALU.mult, ALU.add,
                          [S0_r, dec_r, tmpv_r], [Sn_r])
                b.dma("sync", gla_s[bb].rearrange("h p v -> p h v"), Sn[:], Sn_r, reads=[Sn_r])
                pb, pb_r = b.bank()
                for h in range(4):
                    for e2 in range(2):
                        b.mm(pb[:, h * 2 + e2:h * 2 + e2 + 1], Sn[:, h, e2 * 128:(e2 + 1) * 128], gqTs[:, h, bb:bb + 1],
                             True, True, [Sn_r, gqTs_r], [pb_r])
                b.ts(os8[:], pb[:, 0:8], SC, ALU.mult, [pb_r], [os8_r])
                b.tt(sq8[:], os8[:], os8[:], ALU.mult, [os8_r], [sq8_r])
                pb2, pb2_r = b.bank()
                b.mm(pb2[:, 0:8], onesF, sq8[:], True, True, [cF_r, sq8_r], [pb2_r])
                b.op("vector", lambda e: e.tensor_reduce(out=ss4[:], in_=pb2[:, 0:8].rearrange("p (h e) -> p h e", e=2),
                                                         axis=AX.X, op=ALU.add), [pb2_r], [ss4_r])
                b.act(ss4[:], ss4[:], AF.Ln, [ss4_r, epsc_r], [ss4_r], scale=1.0 / 256, bias=epsc[:, 0:1])
                b.act(ss4[:], ss4[:], AF.Exp, [ss4_r], [ss4_r], scale=-0.5)
                b.cp(gx8[:], grTs[:, :, bb], [grTs_r], [gx8_r])
                silu_to(gs8[:], gs8_r, gx8[:], gx8_r, gt8[:], gt8_r)
                b.tt(os8[:], os8[:], gg[:, 0:8], ALU.mult, [os8_r, gg_r], [os8_r])
                b.tt(os8[:, :].rearrange("p (h e) -> p h e", e=2), os8[:, :].rearrange("p (h e) -> p h e", e=2),
                     ss4[:].unsqueeze(2).to_broadcast([128, 4, 2]), ALU.mult, [os8_r, ss4_r], [os8_r])
                b.tt(actA[:, :, T + bb], os8[:], gs8[:], ALU.mult, [os8_r, gs8_r], [actA_r])
                b.dma("sync", qbc[:], ZSR[bb:bb + 1, 0:1536].to_broadcast([128, 1536]), qbc_r, writes=[qbc_r])
                b.dma("sync", mqbc[:], ZStm[bb:bb + 1, C_MQ:C_MQ + 512].to_broadcast([128, 512]), mqbc_r, writes=[mqbc_r])
                b.dma("sync", psbc[:], ZSP[bb:bb + 1, :].to_broadcast([128, 12]), psbc_r, writes=[psbc_r])
                for g, (win, d) in enumerate(SWA):
                    csrc = cch[g][bb].rearrange("(u dd) e -> dd u e", dd=d)[0]
                    b.dma("sync" if g != 1 else "scalar", Kc[:, g, :], csrc[:, 0:512], Kc_r, writes=[Kc_r])
                    b.dma("sync", Vc[:, g, :], csrc[:, 512:1024], Vc_r, writes=[Vc_r])
                    b.tt(prd[:], Kc[:, g, :], qbc[:, g * 512:(g + 1) * 512], ALU.mult, [Kc_r, qbc_r], [prd_r])
                    b.op("vector", lambda e: e.tensor_reduce(out=sc12[:, g * 4:(g + 1) * 4],
                                                             in_=prd[:, :].rearrange("p (h e) -> p h e", e=128),
                                                             axis=AX.X, op=ALU.add), [prd_r], [sc12_r])
                b.act(sc12[:], sc12[:], AF.Exp, [sc12_r], [sc12_r], scale=SC)
                pb, pb_r = b.bank()
                for jj in range(4):
                    for g in range(3):
                        b.mm(pb[:, jj:jj + 1], Vc[:, g, jj * 128:(jj + 1) * 128], sc12[:, g * 4 + jj:g * 4 + jj + 1],
                             g == 0, g == 2, [Vc_r, sc12_r], [pb_r])
                pb2, pb2_r = b.bank()
                b.mm(pb2[:, 0:12], onesF, sc12[:], True, True, [cF_r, sc12_r], [pb2_r])
                b.cp(n4[:], pb[:, 0:4], [pb_r], [n4_r])
                b.cp(d4[:], pb2[:, 0:4], [pb2_r], [d4_r])
                for g in range(3):
                    b.tt(u4[:], psbc[:, g * 4:(g + 1) * 4], svTs[:, g * 4:(g + 1) * 4, bb], ALU.mult,
                         [psbc_r, svTs_r], [u4_r])
                    b.tt(n4[:], n4[:], u4[:], ALU.add, [n4_r, u4_r], [n4_r])
                    b.tt(d4[:], d4[:], psbc[:, g * 4:(g + 1) * 4], ALU.add, [d4_r, psbc_r], [d4_r])
                    if g > 0:
                        b.tt(d4[:], d4[:], pb2[:, g * 4:(g + 1) * 4], ALU.add, [d4_r, pb2_r], [d4_r])
                b.recip(d4[:], d4[:], [d4_r], [d4_r])
                b.tt(n4[:], n4[:], d4[:], ALU.mult, [n4_r, d4_r], [n4_r])
                b.cp(gx8[:, 0:4], srTs[:, :, bb], [srTs_r], [gx8_r])
                silu_to(gs8[:, 0:4], gs8_r, gx8[:, 0:4], gx8_r, gt8[:, 0:4], gt8_r)
                b.tt(actB[:, :, T + bb], n4[:], gs8[:, 0:4], ALU.mult, [n4_r, gs8_r], [actB_r])
                msrc = cmem[bb].rearrange("(t p) e -> p t e", p=128)
                b.dma("sync", Kcm[:], msrc[:, :, 0:512], Kcm_r, writes=[Kcm_r])
                b.dma("sync", Vcm[:], msrc[:, :, 512:1024], Vcm_r, writes=[Vcm_r])
                for t2i in range(2):
                    b.tt(prd[:], Kcm[:, t2i, :], mqbc[:], ALU.mult, [Kcm_r, mqbc_r], [prd_r])
                    b.op("vector", lambda e: e.tensor_reduce(out=sm8[:, t2i * 4:(t2i + 1) * 4],
                                                             in_=prd[:, :].rearrange("p (h e) -> p h e", e=128),
                                                             axis=AX.X, op=ALU.add), [prd_r], [sm8_r])
                b.act(sm8[:], sm8[:], AF.Exp, [sm8_r], [sm8_r], scale=SC)
                pb, pb_r = b.bank()
                for jj in range(4):
                    for t2i in range(2):
                        b.mm(pb[:, jj:jj + 1], Vcm[:, t2i, jj * 128:(jj + 1) * 128], sm8[:, t2i * 4 + jj:t2i * 4 + jj + 1],
                             t2i == 0, t2i == 1, [Vcm_r, sm8_r], [pb_r])
                pb2, pb2_r = b.bank()
                b.mm(pb2[:, 0:8], onesF, sm8[:], True, True, [cF_r, sm8_r], [pb2_r])
                b.cp(d4[:], pb2[:, 0:4], [pb2_r], [d4_r])
                b.tt(d4[:], d4[:], pb2[:, 4:8], ALU.add, [d4_r, pb2_r], [d4_r])
                b.recip(d4[:], d4[:], [d4_r], [d4_r])
                b.tt(n4[:], pb[:, 0:4], d4[:], ALU.mult, [pb_r, d4_r], [n4_r])
                b.cp(gx8[:, 0:4], mrTs[:, :, bb], [mrTs_r], [gx8_r])
                silu_to(gs8[:, 0:4], gs8_r, gx8[:, 0:4], gx8_r, gt8[:, 0:4], gt8_r)
                b.tt(actC[:, :, T + bb], n4[:], gs8[:, 0:4], ALU.mult, [n4_r, gs8_r], [actC_r])
                for g, (win, d) in enumerate(SWA):
                    b.dma("sync", swa_s[g][bb, win - 1:win, 0:512], ZSR[bb:bb + 1, 1536 + g * 512:1536 + (g + 1) * 512], dd_r)
                    b.dma("sync", swa_s[g][bb, win - 1:win, 512:1024],
                          ZStm[bb:bb + 1, C_SV + g * 512:C_SV + (g + 1) * 512], dd_r)
            b.end_scope(st)

        with ExitStack() as st:
            mTf, mTf_r = b.sb(st, "mTf", [128, NCH, T + 4], BF16)
            acts = [(actA, actA_r, 8, wpa_d, 0), (actB, actB_r, 4, wpb_d, 8), (actC, actC_r, 4, wpc_d, 12)]
            with ExitStack() as st1:
                wpfs = [b.sb(st1, "wpf", [128, 16, 128], BF16) for _ in range(2)]
                gtts = [b.sb(st1, "gtt", [128, 3, 1024], F32) for _ in range(3)]
                mgs = [b.sb(st1, "mg", [128, 512], F32) for _ in range(2)]
                mg2s = [b.sb(st1, "mg2", [128, 512], F32) for _ in range(2)]
                fc = 0
                gsrc = ZT[C_GT:C_GT + 6144, :].rearrange("(br f p) t -> f p br t", br=3, f=16, p=128)
                gsrc_s = ZSfm[C_GT:C_GT + 6144, :].rearrange("(br f p) t -> f p br t", br=3, f=16, p=128)
                for f in range(16):
                    wpf, wpf_r = wpfs[f % 2]
                    for (a_t, a_r, nk, wd, ko) in acts:
                        b.dma("gpsimd", wpf[:, ko:ko + nk, :],
                              wd[:, f * 128:(f + 1) * 128].rearrange("(c p) w -> p c w", p=128), wpf_r, writes=[wpf_r])
                    for tb2 in range(3):
                        NN = 1024 if tb2 < 2 else 4
                        t0 = tb2 * 1024 if tb2 < 2 else T
                        gtt, gtt_r = gtts[fc % 3]
                        fc += 1
                        src_ = gsrc[f][:, :, t0:t0 + NN] if tb2 < 2 else gsrc_s[f]
                        b.dma("sync" if fc % 2 == 0 else "scalar", gtt[:, :, 0:NN], src_, gtt_r, writes=[gtt_r])
                        b.act(gtt[:, :, 0:NN], gtt[:, :, 0:NN], AF.Sigmoid, [], [gtt_r])
                        for hb in range(2 if tb2 < 2 else 1):
                            N = 512 if tb2 < 2 else 4
                            tok0 = t0 + hb * 512
                            gs = slice(hb * 512, hb * 512 + N)
                            mg, mg_r = mgs[hb]
                            mg2, mg2_r = mg2s[hb]
                            for br, (a_t, a_r, nk, wd, ko) in enumerate(acts):
                                pb, pb_r = b.bank()
                                for c in range(nk):
                                    b.mm(pb[:, 0:N], wpf[:, ko + c, :], a_t[:, c, tok0:tok0 + N], c == 0, c == nk - 1,
                                         [wpf_r, a_r], [pb_r])
                                if br == 0:
                                    b.tt(mg[:, 0:N], pb[:, 0:N], gtt[:, 0, gs], ALU.mult, [pb_r, gtt_r], [mg_r])
                                else:
                                    b.tt(mg2[:, 0:N], pb[:, 0:N], gtt[:, br, gs], ALU.mult, [pb_r, gtt_r], [mg2_r])
                                    if br == 1:
                                        b.tt(mg[:, 0:N], mg[:, 0:N], mg2[:, 0:N], ALU.add, [mg_r, mg2_r], [mg_r])
                                    else:
                                        b.tt(mTf[:, f, tok0:tok0 + N], mg[:, 0:N], mg2[:, 0:N], ALU.add,
                                             [mg_r, mg2_r], [mTf_r])
                b.end_scope(st1)
            ssqa, ssqa_r = b.sb(st, "ssqa", [128, 17, 4], F32)
            with ExitStack() as st2:
                wobs = [b.sb(st2, "wob", [128, NCH, 512], BF16) for _ in range(2)]
                xcs = [b.sb(st2, "xc", [128, 4, 512], F32) for _ in range(3)]
                junk, junk_r = b.sb(st2, "junk", [128, 512], F32)
                b.memset(ssqa[:], 0.0, [ssqa_r])
                xcnt = 0
                for cb in range(4):
                    wob, wob_r = wobs[cb % 2]
                    wload(wob, wob_r, wo_d[:, cb * 512:(cb + 1) * 512], D, 512)
                    csl = slice(cb * 512, (cb + 1) * 512)
                    for tg in range(5):
                        ntl = 4 if tg < 4 else 1
                        M = 128 if tg < 4 else 4
                        xc, xc_r = xcs[xcnt % 3]
                        xcnt += 1
                        if tg < 4:
                            src_ = xs[3 * T + tg * 512:3 * T + (tg + 1) * 512, csl].rearrange("(i p) c -> p i c", p=128)
                            b.dma("sync", xc[:, :, :], src_, xc_r, writes=[xc_r])
                        else:
                            b.dma("sync", xc[0:4, 0, :], xsmp[:, csl], xc_r, writes=[xc_r])
                        for il in range(ntl):
                            ti = tg * 4 + il
                            tok0 = ti * 128 if tg < 4 else T
                            pb, pb_r = b.bank()
                            for c in range(NCH):
                                b.mm(pb[0:M, :], mTf[:, c, tok0:tok0 + M], wob[:, c, :], c == 0, c == NCH - 1,
                                     [mTf_r, wob_r], [pb_r])
                            b.tt(xc[0:M, il, :], xc[0:M, il, :], pb[0:M, :], ALU.add, [xc_r, pb_r], [xc_r])
                            b.act(junk[0:M, :], xc[0:M, il, :], AF.Square, [xc_r], [junk_r, ssqa_r],
                                  accum_out=ssqa[0:M, ti, cb:cb + 1])
                        if tg < 4:
                            dst = XN[tg * 512:(tg + 1) * 512, csl].rearrange("(i p) c -> p i c", p=128)
                            b.dma("scalar", dst, xc[:, :, :], xc_r, reads=[xc_r])
                        else:
                            b.dma("scalar", XNs[:, csl], xc[0:4, 0, :], xc_r, reads=[xc_r])
                b.end_scope(st2)
            with ExitStack() as st3:
                gfb, gfb_r = b.sb(st3, "gfb", [128, D], F32)
                b.dma("sync", gfb[:], gfin[0:1, :].to_broadcast([128, D]), gfb_r, writes=[gfb_r])
                xts = [b.sb(st3, "xts", [128, D], F32) for _ in range(4)]
                rst, rst_r = b.sb(st3, "rst", [128, 17], F32)
                b.op("vector", lambda e: e.tensor_reduce(out=rst[:], in_=ssqa[:], axis=AX.X, op=ALU.add), [ssqa_r], [rst_r])
                b.act(rst[:], rst[:], AF.Ln, [rst_r, epsc_r], [rst_r], scale=1.0 / D, bias=epsc[:, 0:1])
                b.act(rst[:], rst[:], AF.Exp, [rst_r], [rst_r], scale=-0.5)
                for ti in range(17):
                    M = 128 if ti < 16 else 4
                    tok0 = ti * 128 if ti < 16 else T
                    xt_, xt_r_ = xts[ti % 4]
                    src_ = XN[tok0:tok0 + M, :] if ti < 16 else XNs[:, :]
                    b.dma("sync" if ti % 2 == 0 else "gpsimd", xt_[0:M, :], src_, xt_r_, writes=[xt_r_])
                    b.stt(xt_[0:M, :], xt_[0:M, :], rst[0:M, ti:ti + 1], gfb[0:M, :], ALU.mult, ALU.mult,
                          [xt_r_, rst_r, gfb_r], [xt_r_])
                    dst = y_own[tok0:tok0 + M, :] if ti < 16 else y_smp[:, :]
                    b.dma("scalar" if ti % 2 == 0 else "sync", dst, xt_[0:M, :], xt_r_, reads=[xt_r_])
                b.end_scope(st3)
            b.end_scope(st)

        b.barrier()
        b.barrier()
    return nc


_NC = None
DEBUG = {}


def _prep_inputs(x_prompt, x_sample, mem_prompt, state_gla, cache_swa_w128, cache_swa_w512, cache_swa_w2048,
                 cache_mem_kv, g_norm, w_in, w_alpha2, b_alpha, g_gla_out, g_mem, w_mem_kv, w_proj_a, w_proj_b,
                 w_proj_c, w_out, g_final):
    f = lambda a: np.ascontiguousarray(np.asarray(a, dtype=np.float32))
    x_prompt = f(x_prompt)
    shared = {
        "w_in": f(w_in[0]),
        "wa2a": f(np.concatenate([np.asarray(w_alpha2[0]), np.asarray(b_alpha[0])[None, :]], axis=0)),
        "gnt": f(np.asarray(g_norm[0]).reshape(NCH, 128).T),
        "gmt": f(np.asarray(g_mem[0]).reshape(NCH, 128).T),
        "ggt": f(np.asarray(g_gla_out[0]).reshape(8, 128).T),
        "gfin": f(np.asarray(g_final).reshape(1, D)),
        "wmkv": f(w_mem_kv[0]),
        "wpa": f(w_proj_a[0]),
        "wpb": f(w_proj_b[0]),
        "wpc": f(w_proj_c[0]),
        "wo": f(w_out[0]),
    }
    maps = []
    for c in range(8):
        bq, j = c // 4, c % 4
        xs = np.zeros((4, T, D), np.float32)
        for s in range(4):
            sh = j - 3 + s
            if sh >= 0:
                xs[s] = x_prompt[bq, sh * T:(sh + 1) * T]
        meta = np.zeros((128, 4), np.float32)
        meta[:, 0] = float(j * T - T)
        meta[:, 1] = 0.0 if j > 0 else NEGM
        m = dict(shared)
        m.update({
            "xs": xs.reshape(4 * T, D),
            "xsmp": f(np.asarray(x_sample)[4 * c:4 * c + 4, 0]),
            "mem": f(np.asarray(mem_prompt)[bq]),
            "sgla": f(np.asarray(state_gla)[0, 4 * c:4 * c + 4]),
            "c128": f(np.asarray(cache_swa_w128)[0, 4 * c:4 * c + 4]).reshape(4, 128, 1024),
            "c512": f(np.asarray(cache_swa_w512)[0, 4 * c:4 * c + 4]).reshape(4, 512, 1024),
            "c2048": f(np.asarray(cache_swa_w2048)[0, 4 * c:4 * c + 4]).reshape(4, 2048, 1024),
            "cmem": f(np.asarray(cache_mem_kv)[0, 4 * c:4 * c + 4]).reshape(4, 256, 1024),
            "meta": meta,
        })
        maps.append(m)
    return maps


def _assemble(res):
    R = res.results
    y_prompt = np.zeros((2, 4 * T, D), np.float32)
    for c in range(8):
        y_prompt[c // 4, (c % 4) * T:(c % 4 + 1) * T] = R[c]["y_own"]
    y_sample = np.concatenate([R[c]["y_smp"] for c in range(8)], axis=0).reshape(32, 1, D)
    last = [3, 7]
    gla_prompt = np.stack([R[c]["gla_p"] for c in last])[None]
    swa_p = []
    for nm, w in (("swa128_p", 128), ("swa512_p", 512), ("swa2048_p", 2048)):
        swa_p.append(np.stack([R[c][nm].reshape(w, 2, 4, 128) for c in last])[None])
    mem_kv = np.stack([R[c]["memkv_p"].reshape(256, 2, 4, 128) for c in (0, 4)])[None]
    gla_sample = np.concatenate([R[c]["gla_s"] for c in range(8)], axis=0)[None]
    swa_s = []
    for nm, w in (("s128_s", 128), ("s512_s", 512), ("s2048_s", 2048)):
        swa_s.append(np.concatenate([R[c][nm].reshape(4, w, 2, 4, 128) for c in range(8)], axis=0)[None])
    return (y_prompt, y_sample, gla_prompt, swa_p[0], swa_p[1], swa_p[2], mem_kv, gla_sample,
            swa_s[0], swa_s[1], swa_s[2])


def kernel(**inputs):
    global _NC
    maps = _prep_inputs(**inputs)
    nc = build_program()
    res = run_bass_kernel_spmd(nc, maps, core_ids=list(range(8)))
    DEBUG["res"] = res
    outs = _assemble(res)
    return tuple(np.ascontiguousarray(o.astype(np.float32)) for o in outs)
```
